# Optimizing a Trainium2 kernel written in Bass

```python
import math
import jax, jax.numpy as jnp
from jax import lax
import numpy as np

D_MODEL = 1024
BATCH = 8
SEQ = 2048
DEPTH = 4
DEC_BATCH = 128
DEC_SEQ = 8
PAST_LEN = 16384
PAGE_SIZE = 128

D_MIX = 2 * D_MODEL
POOL_WIDTH = D_MIX // 4
POOL_WINDOWS = (2, 4, 8, 16)
N_POOL_GROUPS = 4
POOL_GROUP_DIM = POOL_WIDTH // N_POOL_GROUPS
POOL_STATE_LEN = 15
SSD_WIDTH = D_MIX // 2
SSD_HEAD_DIM = 64
SSD_HEADS = SSD_WIDTH // SSD_HEAD_DIM
SSD_GROUPS = 2
SSD_HEADS_PER_GROUP = SSD_HEADS // SSD_GROUPS
D_STATE = 128
CONV_WIDTH = 4
CONV_DIM = SSD_WIDTH + 2 * SSD_GROUPS * D_STATE
CHUNK = 128
XA_WIDTH = D_MIX // 4
XA_HEADS = 4
XA_HEAD_DIM = XA_WIDTH // XA_HEADS
N_MEM = 256
EPS = 1e-6
SPLIT_SIZES = (POOL_WIDTH, POOL_WIDTH, SSD_WIDTH, CONV_DIM, SSD_HEADS, XA_WIDTH, XA_WIDTH)
D_IN_PROJ = 2 * POOL_WIDTH + SSD_WIDTH + CONV_DIM + SSD_HEADS + 2 * XA_WIDTH

kernel_name = "hybrid_pool_ssd_memxattn_decoder_step"


def rmsnorm(x, g):
    xf = x.astype(jnp.float32)
    y = xf * lax.rsqrt(jnp.mean(xf * xf, axis=-1, keepdims=True) + EPS)
    return (y * g.astype(jnp.float32)).astype(x.dtype)


def split_cols(p, sizes):
    idx = [int(i) for i in np.cumsum(sizes)[:-1]]
    return jnp.split(p, idx, axis=-1)


def pool_mixer(u, state, start_pos, w_pool, s_pool):
    b, L, _ = u.shape
    s0 = POOL_STATE_LEN
    buf_raw = jnp.concatenate([state.astype(u.dtype), u], axis=1)
    buf = buf_raw.astype(jnp.float32)
    cs = jnp.concatenate([jnp.zeros((b, 1, POOL_WIDTH), jnp.float32),
                          jnp.cumsum(buf, axis=1)], axis=1)
    end = cs[:, s0 + 1:]
    pos = start_pos + jnp.arange(L, dtype=jnp.int32)
    means = []
    for gi, w in enumerate(POOL_WINDOWS):
        sl = slice(gi * POOL_GROUP_DIM, (gi + 1) * POOL_GROUP_DIM)
        start = cs[:, s0 + 1 - w: s0 + 1 - w + L, sl]
        cnt = jnp.minimum(pos + 1, w).astype(jnp.float32)
        means.append((end[..., sl] - start) / cnt[None, :, None])
    d = (jnp.concatenate(means, axis=-1) - buf[:, s0:]).reshape(b, L, N_POOL_GROUPS, POOL_GROUP_DIM)
    out = jnp.einsum('blgc,gcd->blgd', d, w_pool.astype(jnp.float32)).reshape(b, L, POOL_WIDTH)
    out = out * s_pool.astype(jnp.float32)
    return out.astype(u.dtype), buf_raw[:, -s0:]


def causal_dwconv(xbc, conv_state, conv_w, conv_b):
    buf = jnp.concatenate([conv_state.astype(xbc.dtype), xbc], axis=1)
    y = lax.conv_general_dilated(buf, conv_w.astype(xbc.dtype)[:, None, :], window_strides=(1,),
                                 padding='VALID', dimension_numbers=('NWC', 'WIO', 'NWC'),
                                 feature_group_count=CONV_DIM)
    return jax.nn.silu(y + conv_b.astype(xbc.dtype)), buf[:, -(CONV_WIDTH - 1):]


def ssd_scan(x, dt, A, Bm, Cm, s0):
    b, L = x.shape[0], x.shape[1]
    G, R, P, N = SSD_GROUPS, SSD_HEADS_PER_GROUP, SSD_HEAD_DIM, D_STATE
    q = CHUNK if L % CHUNK == 0 else L
    nc = L // q
    to_chunks = lambda a: jnp.moveaxis(a.reshape((b, nc, q) + a.shape[2:]), 1, 0)
    xc = to_chunks(x.astype(jnp.float32).reshape(b, L, G, R, P))
    dtc = to_chunks(dt.reshape(b, L, G, R))
    Bc = to_chunks(Bm.astype(jnp.float32))
    Cc = to_chunks(Cm.astype(jnp.float32))
    Ag = A.reshape(G, R)
    causal = jnp.tril(jnp.ones((q, q), dtype=bool))[None, :, :, None, None]

    def step(S, inp):
        xk, dtk, Bk, Ck = inp
        cum = jnp.cumsum(dtk * Ag, axis=1)
        seg = cum[:, :, None] - cum[:, None]
        Lm = jnp.exp(jnp.where(causal, seg, -jnp.inf))
        CB = jnp.einsum('bign,bjgn->bijg', Ck, Bk)
        M = CB[..., None] * Lm * dtk[:, None]
        y = jnp.einsum('bijgr,bjgrp->bigrp', M, xk)
        y = y + jnp.einsum('bign,bgrpn->bigrp', Ck, S) * jnp.exp(cum)[..., None]
        w_end = jnp.exp(cum[:, -1:] - cum) * dtk
        S_new = jnp.exp(cum[:, -1])[..., None, None] * S + jnp.einsum('bjgr,bjgrp,bjgn->bgrpn', w_end, xk, Bk)
        return S_new, y

    S0 = s0.astype(jnp.float32).reshape(b, G, R, P, N)
    S_fin, ys = lax.scan(step, S0, (xc, dtc, Bc, Cc))
    y = jnp.moveaxis(ys, 0, 1).reshape(b, L, SSD_HEADS, P)
    return y, S_fin.reshape(b, SSD_HEADS, P, N)


def mem_kv(mem, mem_norm_g, w_mem_k, w_mem_v):
    b = mem.shape[0]
    mn = rmsnorm(mem, mem_norm_g)
    k = (mn @ w_mem_k).reshape(b, N_MEM, XA_HEADS, XA_HEAD_DIM)
    v = (mn @ w_mem_v).reshape(b, N_MEM, XA_HEADS, XA_HEAD_DIM)
    return k, v


def layer(h, pool_st, conv_st, ssm_st, mk, mv, start_pos,
          norm_g, w_in, pool_w, pool_scale, conv_w, conv_b, dt_bias, a_log, d_skip, ssd_norm_g, w_out):
    b, L, _ = h.shape
    xn = rmsnorm(h, norm_g)
    u, g_pool, z, xbc, dt_raw, q, g_xa = split_cols(xn @ w_in, SPLIT_SIZES)
    pool_y, new_pool = pool_mixer(u, pool_st, start_pos, pool_w, pool_scale)
    pool_o = pool_y * jax.nn.silu(g_pool)
    xbc_c, new_conv = causal_dwconv(xbc, conv_st, conv_w, conv_b)
    xs, Bm, Cm = split_cols(xbc_c, (SSD_WIDTH, SSD_GROUPS * D_STATE, SSD_GROUPS * D_STATE))
    xs = xs.reshape(b, L, SSD_HEADS, SSD_HEAD_DIM)
    dt = jax.nn.softplus(dt_raw.astype(jnp.float32) + dt_bias.astype(jnp.float32))
    A = -jnp.exp(a_log.astype(jnp.float32))
    y, new_ssm = ssd_scan(xs, dt, A, Bm.reshape(b, L, SSD_GROUPS, D_STATE),
                          Cm.reshape(b, L, SSD_GROUPS, D_STATE), ssm_st)
    y = y + d_skip.astype(jnp.float32)[:, None] * xs.astype(jnp.float32)
    y = (y.reshape(b, L, SSD_WIDTH) * jax.nn.silu(z.astype(jnp.float32))).reshape(b, L, SSD_GROUPS, -1)
    y = y * lax.rsqrt(jnp.mean(y * y, axis=-1, keepdims=True) + EPS)
    ssd_o = (y.reshape(b, L, SSD_WIDTH) * ssd_norm_g.astype(jnp.float32)).astype(h.dtype)
    qh = q.reshape(b, L, XA_HEADS, XA_HEAD_DIM)
    scores = jnp.einsum('blhd,bmhd->bhlm', qh, mk.astype(qh.dtype)).astype(jnp.float32) / math.sqrt(XA_HEAD_DIM)
    probs = jax.nn.softmax(scores, axis=-1)
    xa = jnp.einsum('bhlm,bmhd->blhd', probs.astype(h.dtype), mv.astype(h.dtype)).reshape(b, L, XA_WIDTH)
    xa_o = xa * jax.nn.silu(g_xa)
    out = jnp.concatenate([pool_o, ssd_o, xa_o], axis=-1) @ w_out
    return h + out, new_pool, new_conv, new_ssm.astype(h.dtype)


def setup_inputs(seed: int = 0) -> dict:
    key = jax.random.key(seed)
    ks = jax.random.split(key, 24)
    nrm = lambda k, shape, s: jax.random.normal(k, shape, jnp.float32) * s
    dt0 = jnp.exp(jax.random.uniform(ks[0], (DEPTH, SSD_HEADS), jnp.float32, math.log(1e-3), math.log(1e-1)))
    return {
        "x_prompt": nrm(ks[1], (BATCH, SEQ, D_MODEL), 1.0),
        "x_sample": nrm(ks[2], (DEC_BATCH, DEC_SEQ, D_MODEL), 1.0),
        "mem_prompt": nrm(ks[3], (BATCH, N_MEM, D_MODEL), 1.0),
        "state_pool": nrm(ks[4], (DEPTH, DEC_BATCH, POOL_STATE_LEN, POOL_WIDTH), 1.0),
        "state_conv": nrm(ks[5], (DEPTH, DEC_BATCH, CONV_WIDTH - 1, CONV_DIM), 1.0),
        "state_ssm": nrm(ks[6], (DEPTH, DEC_BATCH, SSD_HEADS, SSD_HEAD_DIM, D_STATE), 0.5),
        "cache_mem_k": nrm(ks[7], (DEPTH, DEC_BATCH, N_MEM, XA_HEADS, XA_HEAD_DIM), 1.0),
        "cache_mem_v": nrm(ks[8], (DEPTH, DEC_BATCH, N_MEM, XA_HEADS, XA_HEAD_DIM), 1.0),
        "norm_g": 1.0 + nrm(ks[9], (DEPTH, D_MODEL), 0.02),
        "w_in": nrm(ks[10], (DEPTH, D_MODEL, D_IN_PROJ), D_MODEL ** -0.5),
        "pool_w": nrm(ks[11], (DEPTH, N_POOL_GROUPS, POOL_GROUP_DIM, POOL_GROUP_DIM), POOL_GROUP_DIM ** -0.5),
        "pool_scale": 1.0 + nrm(ks[12], (DEPTH, POOL_WIDTH), 0.02),
        "conv_w": nrm(ks[13], (DEPTH, CONV_WIDTH, CONV_DIM), CONV_WIDTH ** -0.5),
        "conv_b": nrm(ks[14], (DEPTH, CONV_DIM), 0.01),
        "dt_bias": dt0 + jnp.log(-jnp.expm1(-dt0)),
        "a_log": jnp.log(jax.random.uniform(ks[15], (DEPTH, SSD_HEADS), jnp.float32, 1.0, 16.0)),
        "d_skip": 1.0 + nrm(ks[16], (DEPTH, SSD_HEADS), 0.02),
        "ssd_norm_g": 1.0 + nrm(ks[17], (DEPTH, SSD_WIDTH), 0.02),
        "mem_norm_g": 1.0 + nrm(ks[18], (DEPTH, D_MODEL), 0.02),
        "w_mem_k": nrm(ks[19], (DEPTH, D_MODEL, XA_WIDTH), D_MODEL ** -0.5),
        "w_mem_v": nrm(ks[20], (DEPTH, D_MODEL, XA_WIDTH), D_MODEL ** -0.5),
        "w_out": nrm(ks[21], (DEPTH, D_MIX, D_MODEL), D_MIX ** -0.5),
        "final_norm_g": 1.0 + nrm(ks[22], (D_MODEL,), 0.02),
    }


def reference(x_prompt, x_sample, mem_prompt, state_pool, state_conv, state_ssm, cache_mem_k, cache_mem_v,
              norm_g, w_in, pool_w, pool_scale, conv_w, conv_b, dt_bias, a_log, d_skip, ssd_norm_g,
              mem_norm_g, w_mem_k, w_mem_v, w_out, final_norm_g):
    hp, hs = x_prompt, x_sample
    pool_p, conv_p, ssm_p, mk_p, mv_p = [], [], [], [], []
    pool_s, conv_s, ssm_s = [], [], []
    zp_pool = jnp.zeros((BATCH, POOL_STATE_LEN, POOL_WIDTH), x_prompt.dtype)
    zp_conv = jnp.zeros((BATCH, CONV_WIDTH - 1, CONV_DIM), x_prompt.dtype)
    zp_ssm = jnp.zeros((BATCH, SSD_HEADS, SSD_HEAD_DIM, D_STATE), x_prompt.dtype)
    for l in range(DEPTH):
        lw = (norm_g[l], w_in[l], pool_w[l], pool_scale[l], conv_w[l], conv_b[l],
              dt_bias[l], a_log[l], d_skip[l], ssd_norm_g[l], w_out[l])
        mk, mv = mem_kv(mem_prompt, mem_norm_g[l], w_mem_k[l], w_mem_v[l])
        hp, npool, nconv, nssm = layer(hp, zp_pool, zp_conv, zp_ssm, mk, mv, 0, *lw)
        pool_p.append(npool); conv_p.append(nconv); ssm_p.append(nssm); mk_p.append(mk); mv_p.append(mv)
        hs, npool, nconv, nssm = layer(hs, state_pool[l], state_conv[l], state_ssm[l],
                                       cache_mem_k[l], cache_mem_v[l], PAST_LEN, *lw)
        pool_s.append(npool); conv_s.append(nconv); ssm_s.append(nssm)
    y_prompt = rmsnorm(hp, final_norm_g)
    y_sample = rmsnorm(hs, final_norm_g)
    return (y_prompt, y_sample,
            jnp.stack(pool_p), jnp.stack(conv_p), jnp.stack(ssm_p), jnp.stack(mk_p), jnp.stack(mv_p),
            jnp.stack(pool_s), jnp.stack(conv_s), jnp.stack(ssm_s))
```

```python
import contextlib
import math
import numpy as np
import concourse.bass as bass
import concourse.mybir as mybir
from concourse.bass_utils import run_bass_kernel_spmd

F32 = mybir.dt.float32
BF16 = mybir.dt.bfloat16
AF = mybir.ActivationFunctionType
ALU = mybir.AluOpType

NCORES = 8
DEPTH = 4
D = 1024
NT = 16
NS = 16
DIN = 4624
C_U, C_GP, C_Z, C_XBC, C_DT, C_Q, C_GX = 0, 512, 1024, 2048, 3584, 3600, 4112
EPS = 1e-6
POOL_W = (2, 4, 8, 16)
SEM_MAX = 3600

CP_NG, CP_MG, CP_SG, CP_PS, CP_CW, CP_CB, NCP = 0, 8, 16, 24, 28, 76, 88
CF_ID, CF_TRI, CF_TRIS, CF_SU, CF_ONE, CF_BONE, CF_BM, CF_IC0, NCF = 0, 128, 256, 384, 512, 640, 768, 784, 1296
CB_ID, CB_ONE, CB_BAND, NCB = 0, 128, 256, 256 + 24 * 128


class Buf:
    __slots__ = ("name", "t", "writes", "reads", "old", "dsem", "dval", "partial")

    def __init__(self, name, t, partial=False):
        self.name = name
        self.t = t
        self.writes = {}
        self.reads = {}
        self.old = {}
        self.dsem = None
        self.dval = 0
        self.partial = partial

    def __getitem__(self, k):
        return self.t[k]


def _merge(d, ev):
    for k, v in ev.items():
        if d.get(k, 0) < v:
            d[k] = v


class KB:
    def __init__(self, nc, stack):
        self.nc = nc
        self.stack = stack
        self.E = {"pe": nc.tensor, "act": nc.scalar, "dve": nc.vector, "pool": nc.gpsimd, "sp": nc.sync}
        self.sem, self.cnt, self.seen = {}, {}, {}
        for e in self.E:
            self.sem[e] = stack.enter_context(nc.semaphore("s_" + e))
            self.cnt[e] = 0
            self.seen[e] = {}
        self.nbuf = 0
        self.rr = {}
        self.psfree = []
        self.maxwaited = {}
        self.dmafinal = {}
        self.nops = 0
        self.limit = 1 << 60

    def sb(self, name, shape, dtype):
        self.nbuf += 1
        t = self.stack.enter_context(self.nc.sbuf_tensor(f"{name}_{self.nbuf}", list(shape), dtype))
        return Buf(name, t)

    def ps(self, name, shape, dtype=F32):
        self.nbuf += 1
        t = self.stack.enter_context(self.nc.psum_tensor(f"{name}_{self.nbuf}", list(shape), dtype))
        return Buf(name, t)

    def dram(self, name, shape, dtype, kind):
        t = self.nc.dram_tensor(name, list(shape), dtype, kind=kind)
        return Buf(name, t, partial=True)

    def pool(self, name, shape, dtype, n):
        bufs = [self.sb(f"{name}{i}", shape, dtype) for i in range(n)]
        self.rr[name] = [bufs, 0]
        return bufs

    def nxt(self, name):
        r = self.rr[name]
        b = r[0][r[1] % len(r[0])]
        r[1] += 1
        return b

    def psalloc(self):
        assert self.psfree, "out of PSUM banks"
        return self.psfree.pop(0)

    def psrel(self, *bs):
        for b in bs:
            self.psfree.append(b)

    def _isfirst(self, b, first):
        if b.partial:
            return False
        return first is None or b in first

    def _deps(self, reads, writes, first):
        dep = {}
        for b in reads:
            _merge(dep, b.writes)
        for b in writes:
            if self._isfirst(b, first):
                _merge(dep, b.writes)
                _merge(dep, b.reads)
            else:
                _merge(dep, b.old)
        return dep

    def _wait(self, e, dep):
        eng = self.E[e]
        seen = self.seen[e]
        for sem, val in dep.items():
            if seen.get(sem, 0) < val:
                eng.wait_ge(sem, val)
                seen[sem] = val
            if self.maxwaited.get(sem, 0) < val:
                self.maxwaited[sem] = val

    def _commit(self, ev, reads, writes, first):
        for b in writes:
            if self._isfirst(b, first):
                old = {}
                _merge(old, b.writes)
                _merge(old, b.reads)
                b.old = old
                b.writes = dict(ev)
                b.reads = {}
            else:
                _merge(b.writes, ev)
        for b in reads:
            if b in writes:
                continue
            _merge(b.reads, ev)

    def op(self, e, fn, reads=(), writes=(), first=None):
        self.nops += 1
        if self.nops > self.limit:
            return
        dep = self._deps(reads, writes, first)
        self._wait(e, dep)
        ins = fn(self.E[e])
        self.cnt[e] += 1
        ins.then_inc(self.sem[e], 1)
        self._commit({self.sem[e]: self.cnt[e]}, reads, writes, first)
        if self.cnt[e] >= SEM_MAX:
            self.nbuf += 1
            self.sem[e] = self.stack.enter_context(self.nc.semaphore(f"s_{e}_{self.nbuf}"))
            self.cnt[e] = 0

    def dma(self, q, out_ap, in_ap, reads=(), writes=(), first=None, sembuf=None, **kw):
        self.nops += 1
        if self.nops > self.limit:
            return
        dep = self._deps(reads, writes, first)
        b = sembuf
        if b.dsem is None or b.dval + 16 > SEM_MAX:
            self.nbuf += 1
            b.dsem = self.stack.enter_context(self.nc.semaphore(f"d_{b.name}_{self.nbuf}"))
            b.dval = 0
        if b.dsem is not None and self.maxwaited.get(b.dsem, 0) > 0:
            _merge(dep, {b.dsem: self.maxwaited[b.dsem]})
        self._wait(q, dep)
        ins = self.E[q].dma_start(out=out_ap, in_=in_ap, **kw)
        b.dval += 16
        ins.then_inc(b.dsem, 16)
        self.dmafinal[b.dsem] = b.dval
        self._commit({b.dsem: b.dval}, reads, writes, first)

    def finish(self, outs, e="sp"):
        dep = {}
        for b in outs:
            _merge(dep, b.writes)
        _merge(dep, self.dmafinal)
        self._wait(e, dep)


def A(buf, off, dims, p0=0, np_=128):
    row = int(np.prod(buf.t.shape[1:]))
    return bass.AP(buf.t, p0 * row + off, [[row, np_]] + [list(d) for d in dims])


def DA(buf, off, dims):
    return bass.AP(buf.t, off, [list(d) for d in dims])


def build_nc(cfg=None):
    cfg = dict(mem=1, layers=DEPTH, tiles=NT, sample=1) if cfg is None else cfg
    nc = bass.Bass("TRN2", target_bir_lowering=False)
    with contextlib.ExitStack() as st:
        kb = KB(nc, st)
        kb.limit = cfg.get('stop', 1 << 60)
        I, O = "ExternalInput", "ExternalOutput"
        xp = kb.dram("xp", [2048, D], F32, I)
        xs = kb.dram("xs", [128, D], F32, I)
        mem = kb.dram("mem", [256, D], F32, I)
        st_pool = kb.dram("st_pool", [DEPTH, NS * 15, 512], F32, I)
        st_conv = kb.dram("st_conv", [DEPTH, NS * 3, 1536], F32, I)
        st_ssm = kb.dram("st_ssm", [DEPTH, NS, 1024, 128], F32, I)
        ck = kb.dram("ck", [DEPTH, NS, 256, 512], F32, I)
        cv = kb.dram("cv", [DEPTH, NS, 256, 512], F32, I)
        w_in = kb.dram("w_in", [DEPTH, D, DIN], F32, I)
        w_out = kb.dram("w_out", [DEPTH, 2048, D], F32, I)
        pool_w = kb.dram("pool_w", [DEPTH, 4, 128, 128], F32, I)
        w_mk = kb.dram("w_mk", [DEPTH, D, 512], F32, I)
        w_mv = kb.dram("w_mv", [DEPTH, D, 512], F32, I)
        colpar_d = kb.dram("colpar", [128, DEPTH * NCP], F32, I)
        rowpar_d = kb.dram("rowpar", [DEPTH * 48], F32, I)
        fng_d = kb.dram("fng", [D], F32, I)
        cstf_d = kb.dram("cstf", [128, NCF], F32, I)
        cstb_d = kb.dram("cstb", [128, NCB], F32, I)

        yp = kb.dram("yp", [2048, D], F32, O)
        ys = kb.dram("ys", [128, D], F32, O)
        o_pool_p = kb.dram("o_pool_p", [DEPTH, 15, 512], F32, O)
        o_conv_p = kb.dram("o_conv_p", [DEPTH, 3, 1536], F32, O)
        o_ssm_p = kb.dram("o_ssm_p", [DEPTH, 1024, 128], F32, O)
        o_mk = kb.dram("o_mk", [DEPTH, 256, 512], F32, O)
        o_mv = kb.dram("o_mv", [DEPTH, 256, 512], F32, O)
        o_pool_s = kb.dram("o_pool_s", [DEPTH, NS, 15, 512], F32, O)
        o_conv_s = kb.dram("o_conv_s", [DEPTH, NS, 3, 1536], F32, O)
        o_ssm_s = kb.dram("o_ssm_s", [DEPTH, NS, 1024, 128], F32, O)
        hs = kb.dram("hs", [17 * 128, D], F32, "Internal")
        ktscr = kb.dram("ktscr", [DEPTH, 128, 1024], BF16, "Internal")
        hsB = [Buf(f"hs{i}", hs.t, partial=True) for i in range(17)]
        outs = [yp, ys, o_pool_p, o_conv_p, o_ssm_p, o_mk, o_mv, o_pool_s, o_conv_s, o_ssm_s]

        pad0 = kb.sb("pad0", [128, 16], F32)
        win = kb.sb("win", [128, 8, DIN], BF16)
        wout = kb.sb("wout", [128, 16, D], BF16)
        poolw = kb.sb("poolw", [128, 4, 128], BF16)
        ktl = kb.sb("ktl", [128, 4, 256], BF16)
        vl = kb.sb("vl", [128, 2, 512], BF16)
        fng = kb.sb("fng", [128, D], F32)
        colpar = kb.sb("colpar", [128, DEPTH * NCP], F32)
        rowpar = kb.sb("rowpar", [128, DEPTH * 48], F32)
        abc = kb.sb("abc", [128, DEPTH * 16], F32)
        cf = kb.sb("cf", [128, NCF], F32)
        cb = kb.sb("cb", [128, NCB], BF16)

        kb.pool("hbuf", [128, D], F32, 2)
        xn = kb.sb("xn", [128, D], BF16)
        xnT = kb.sb("xnT", [128, 8, 128], BF16)
        kb.pool("ubf", [128, 512], BF16, 2)
        sgpT = kb.sb("sgpT", [128, 4, 128], BF16)
        dT = kb.sb("dT", [128, 4, 128], BF16)
        qT = kb.sb("qT", [128, 4, 128], BF16)
        sgxT = kb.sb("sgxT", [128, 4, 128], BF16)
        sz = kb.sb("sz", [128, D], BF16)
        xb = kb.sb("xb", [128, 12 * 176], F32)
        carry = kb.sb("carry", [128, 12, 3], F32)
        big6 = kb.sb("big6", [128, 12 * 128], F32)
        big6b = Buf("big6b", big6.t)
        xbcsT = kb.sb("xbcsT", [128, 12, 128], BF16)
        xdt = kb.sb("xdt", [128, D], BF16)
        xd = kb.sb("xd", [128, D], BF16)
        xw = kb.sb("xw", [128, D], BF16)
        btok = kb.sb("btok", [128, 256], BF16)
        sm = kb.sb("sm", [128, 6 * 16], F32)
        sm2 = kb.sb("sm2", [128, 16], F32)
        Rb = kb.sb("Rb", [128, 4, 128], F32)
        kb.pool("E", [128, 4, 128], BF16, 2)
        kb.pool("MT", [128, 4, 128], BF16, 2)
        cbm = kb.sb("cbm", [128, 2, 128], BF16)
        Sb = kb.pool("Sb", [128, D], F32, 2)
        stb16 = kb.sb("stb16", [128, D], BF16)
        actT = kb.sb("actT", [128, 16, 128], BF16)
        scr = kb.sb("scr", [128, D], F32)
        kb.pool("Kb", [128, 2, 512], BF16, 2)
        kb.pool("Vb", [128, 2, 512], BF16, 2)
        kb.pool("Bs", [128, 256], BF16, 2)
        spool16 = kb.sb("spool16", [128, 2, 512], BF16)
        decT = kb.sb("decT", [128, 8, 16], F32)
        kb.pool("pts", [128, 64], BF16, 2)
        print("sbuf bytes remaining after alloc:", nc.sbuf_bytes_remaining)

        for i in range(8):
            kb.psfree.append(kb.ps(f"bank{i}", [128, 512], F32))

        def smt(i):
            return sm[:, i * 16:(i + 1) * 16]

        kb.dma("sp", cf[:, :], cstf_d[:, :], writes=[cf], sembuf=cf)
        kb.dma("pool", cb[:, :], cstb_d[:, :], writes=[cb], sembuf=cb)
        kb.dma("sp", colpar[:, :], colpar_d[:, :], writes=[colpar], sembuf=colpar)
        kb.dma("sp", rowpar[:, :], DA(rowpar_d, 0, [[0, 128], [1, DEPTH * 48]]), writes=[rowpar], sembuf=rowpar)
        kb.dma("sp", fng[:, :], DA(fng_d, 0, [[0, 128], [1, D]]), writes=[fng], sembuf=fng)
        for l in range(DEPTH):
            kb.op("act", lambda e, l=l: e.activation(out=abc[:, l * 16:(l + 1) * 16],
                                                      in_=rowpar[:, l * 48 + 16:l * 48 + 32], func=AF.Exp),
                  reads=[rowpar], writes=[abc])
        kb.op("dve", lambda e: e.tensor_scalar_mul(out=abc[:, :], in0=abc[:, :], scalar1=-1.0),
              reads=[abc], writes=[abc])

        identf = cf[:, CF_ID:CF_ID + 128]
        identb = cb[:, CB_ID:CB_ID + 128]
        onesb = cb[:, CB_ONE:CB_ONE + 128]

        def band(g, k):
            o = CB_BAND + (g * 6 + k) * 128
            return cb[:, o:o + 128]

        def cpcol(l, off, n=1):
            return colpar[:, l * NCP + off:l * NCP + off + n]

        def rmsnorm_T(hb, gcol_off, l):
            ss = sm2[:, 0:1]
            rs = sm2[:, 1:2]
            kb.op("act", lambda e: e.activation(out=xn[:, :], in_=hb[:, :], func=AF.Square, accum_out=ss),
                  reads=[hb], writes=[xn, sm2])
            kb.op("act", lambda e: e.activation(out=rs, in_=ss, func=AF.Ln, scale=1.0 / D, bias=EPS),
                  reads=[sm2], writes=[sm2])
            kb.op("act", lambda e: e.activation(out=rs, in_=rs, func=AF.Exp, scale=-0.5), reads=[sm2], writes=[sm2])
            kb.op("dve", lambda e: e.tensor_scalar_mul(out=xn[:, :], in0=hb[:, :], scalar1=rs),
                  reads=[hb, sm2], writes=[xn])
            pt = kb.psalloc()
            ptb = pt[:, :].bitcast(BF16)

            def tr(e):
                ins = None
                for k in range(8):
                    ins = e.transpose(out=ptb[:, k * 128:(k + 1) * 128], in_=xn[:, k * 128:(k + 1) * 128], identity=identb)
                return ins
            kb.op("pe", tr, reads=[xn, cb], writes=[pt])
            kb.op("dve", lambda e: e.tensor_tensor(out=xnT[:, :, :], in0=ptb[:, :].rearrange("p (k t) -> p k t", k=8),
                                                   in1=A(colpar, l * NCP + gcol_off, [[1, 8], [0, 128]]), op=ALU.mult),
                  reads=[pt, colpar], writes=[xnT])
            kb.psrel(pt)

        def mm_tok(pbank, ncols, wbuf, wap_fn):
            def f(e):
                ins = None
                for k in range(8):
                    ins = e.matmul(pbank[:, 0:ncols], lhsT=xnT[:, k, :], rhs=wap_fn(k), start=(k == 0), stop=(k == 7))
                return ins
            kb.op("pe", f, reads=[xnT, wbuf], writes=[pbank])

        def mm_feat(pbank, nchunk, wbuf, wap_fn):
            def f(e):
                ins = None
                for c in range(nchunk):
                    for k in range(8):
                        ins = e.matmul(pbank[:, c * 128:(c + 1) * 128], lhsT=wap_fn(k, c), rhs=xnT[:, k, :],
                                       start=(k == 0), stop=(k == 7))
                return ins
            kb.op("pe", f, reads=[xnT, wbuf], writes=[pbank])

        for l in range(DEPTH if cfg['mem'] else 0):
            s = l % 2
            for k in range(8):
                kb.dma("pool", wout[:, 8 * s + k, 0:512], w_mk[l, k * 128:(k + 1) * 128, :], writes=[wout],
                       first=None if (k == 0 and s == 0) else [], sembuf=wout)
                kb.dma("pool", wout[:, 8 * s + k, 512:1024], w_mv[l, k * 128:(k + 1) * 128, :], writes=[wout],
                       first=[], sembuf=wout)
            for mt in range(2):
                hb = kb.nxt("hbuf")
                kb.dma("sp", hb[:, :], mem[mt * 128:(mt + 1) * 128, :], writes=[hb], sembuf=hb)
                rmsnorm_T(hb, CP_MG, l)
                pk = kb.psalloc()
                mm_tok(pk, 512, wout, lambda k: wout[:, 8 * s + k, 0:512])
                kb.op("act", lambda e: e.copy(out=scr[:, 0:512], in_=pk[:, :]), reads=[pk], writes=[scr])
                kb.psrel(pk)
                pv = kb.psalloc()
                mm_tok(pv, 512, wout, lambda k: wout[:, 8 * s + k, 512:1024])
                kb.op("dve", lambda e: e.tensor_copy(out=scr[:, 512:1024], in_=pv[:, :]), reads=[pv], writes=[scr], first=[])
                kb.psrel(pv)
                kb.dma("sp", o_mk[l, mt * 128:(mt + 1) * 128, :], scr[:, 0:512], reads=[scr], writes=[o_mk], sembuf=scr)
                kb.dma("sp", o_mv[l, mt * 128:(mt + 1) * 128, :], scr[:, 512:1024], reads=[scr], writes=[o_mv], sembuf=scr)
                pkt = kb.psalloc()
                mm_feat(pkt, 4, wout, lambda k, c: wout[:, 8 * s + k, c * 128:(c + 1) * 128])
                kb.op("act", lambda e: e.copy(out=ktl[:, :, mt * 128:(mt + 1) * 128],
                                              in_=pkt[:, :].rearrange("p (h m) -> p h m", h=4)),
                      reads=[pkt], writes=[ktl], first=None if mt == 0 else [])
                kb.psrel(pkt)
            kb.dma("sp", ktscr[l, :, :], ktl[:, :, :].rearrange("p h m -> p (h m)"), reads=[ktl], writes=[ktscr], sembuf=ktl)

        def load_win(l):
            for k in range(8):
                kb.dma("pool", win[:, k, :], w_in[l, k * 128:(k + 1) * 128, :], writes=[win],
                       first=None if k == 0 else [], sembuf=win)

        def load_poolw(l):
            kb.dma("pool", poolw[:, :, :], pool_w[l, :, :, :].rearrange("g c d -> c g d"), writes=[poolw], sembuf=poolw)

        def load_attn(l):
            kb.dma("sp", ktl[:, :, :].rearrange("p h m -> p (h m)"), ktscr[l, :, :], reads=[ktscr], writes=[ktl], sembuf=ktl)
            kb.dma("pool", vl[:, :, :], o_mv[l, :, :].rearrange("(c p) n -> p c n", p=128), reads=[o_mv], writes=[vl], sembuf=vl)

        def load_wout(l):
            for k in range(16):
                kb.dma("pool", wout[:, k, :], w_out[l, k * 128:(k + 1) * 128, :], writes=[wout],
                       first=None if k == 0 else [], sembuf=wout)

        SCALE = 1.0 / math.sqrt(128.0)

        XB = [xb, Buf("xb1", xb.t), Buf("xb2", xb.t)]
        BIG = [big6, Buf("big1", big6.t), big6b]
        XS = [xbcsT, Buf("xs1", xbcsT.t), Buf("xs2", xbcsT.t)]
        sm2n = Buf("sm2n", sm2.t)

        def norm_stage(l, ti, sample):
            hidx = 16 if sample else ti
            hb = kb.nxt("hbuf")
            if l == 0:
                src = xs[:, :] if sample else xp[ti * 128:(ti + 1) * 128, :]
                kb.dma("sp", hb[:, :], src, writes=[hb], sembuf=hb)
            else:
                kb.dma("sp", hb[:, :], hs[hidx * 128:(hidx + 1) * 128, :], reads=[hsB[hidx]], writes=[hb], sembuf=hb)
            rmsnorm_T(hb, CP_NG, l)
            return hb

        def front_gen(l, ti, sample):
            hb = norm_stage(l, ti, sample)
            if sample:
                for half in range(2):
                    kb.dma("sp", scr[0:48, 0:768], st_conv[l, :, half * 768:(half + 1) * 768], writes=[scr], sembuf=scr)
                    pc = kb.psalloc()

                    def trc(e, pc=pc):
                        ins = None
                        for c in range(6):
                            ins = e.transpose(out=pc[:, c * 48:(c + 1) * 48], in_=scr[0:48, c * 128:(c + 1) * 128],
                                              identity=cf[0:48, CF_ID:CF_ID + 48])
                        return ins
                    kb.op("pe", trc, reads=[scr, cf], writes=[pc])
                    wr = [XB[0], XB[1]] if half == 0 else [XB[1], XB[2]]
                    kb.op("act", lambda e, pc=pc, half=half: e.copy(
                        out=A(xb, half * 6 * 176, [[176, 6], [11, 16], [1, 3]]),
                        in_=A(pc, 0, [[48, 6], [3, 16], [1, 3]])), reads=[pc], writes=wr,
                        first=None if half == 0 else [XB[2]])
                    kb.psrel(pc)
            else:
                if ti == 0:
                    kb.op("pool", lambda e: e.memset(carry[:, :, :], 0.0), writes=[carry])
                kb.op("pool", lambda e: e.tensor_copy(out=A(xb, 0, [[176, 12], [1, 3]]), in_=carry[:, :, :]),
                      reads=[carry], writes=XB)
            yield
            for b3 in range(3):
                pxb = kb.psalloc()
                mm_feat(pxb, 4, win, lambda k, c, b3=b3: win[:, k, C_XBC + (b3 * 4 + c) * 128:C_XBC + (b3 * 4 + c + 1) * 128])
                if sample:
                    o_ap = A(xb, b3 * 4 * 176 + 3, [[176, 4], [11, 16], [1, 8]])
                    i_ap = A(pxb, 0, [[128, 4], [8, 16], [1, 8]])
                else:
                    o_ap = A(xb, b3 * 4 * 176 + 3, [[176, 4], [1, 128]])
                    i_ap = A(pxb, 0, [[128, 4], [1, 128]])
                kb.op("act", lambda e, o_ap=o_ap, i_ap=i_ap: e.copy(out=o_ap, in_=i_ap), reads=[pxb], writes=[XB[b3]], first=[])
                kb.psrel(pxb)
                if b3 < 2:
                    yield
            if not sample:
                kb.op("pool", lambda e: e.tensor_copy(out=carry[:, :, :], in_=A(xb, 128, [[176, 12], [1, 3]])),
                      reads=XB, writes=[carry])
            yield
            pq = kb.psalloc()
            mm_feat(pq, 4, win, lambda k, c: win[:, k, C_Q + c * 128:C_Q + (c + 1) * 128])
            kb.op("act", lambda e: e.copy(out=qT[:, :, :].rearrange("p a b -> p (a b)"), in_=pq[:, :]),
                  reads=[pq], writes=[qT])
            kb.psrel(pq)
            yield
            pgx = kb.psalloc()
            mm_feat(pgx, 4, win, lambda k, c: win[:, k, C_GX + c * 128:C_GX + (c + 1) * 128])
            kb.op("act", lambda e: e.activation(out=sgxT[:, :, :].rearrange("p a b -> p (a b)"), in_=pgx[:, :], func=AF.Silu),
                  reads=[pgx], writes=[sgxT])
            kb.psrel(pgx)

            tile_state["pre"] = hb

        def front_stage(l, ti, sample):
            for _ in front_gen(l, ti, sample):
                pass
            return tile_state.pop("pre")

        def do_tile(l, ti, sample, nxt_tile):
            last_p = (not sample) and ti == cfg['tiles'] - 1
            LL = cfg['layers']
            special = sample or last_p
            hidx = 16 if sample else ti
            if tile_state.get("pre") is not None:
                hb = tile_state.pop("pre")
            else:
                hb = front_stage(l, ti, sample)
            rp = l * 48
            afirst = [None]

            def actT_first():
                f = afirst[0]
                afirst[0] = []
                return f

            if not sample:
                psc = [kb.psalloc(), kb.psalloc()]
                for hp in range(2):
                    def scmm(e, hp=hp):
                        ins = None
                        for hh in range(2):
                            h = hp * 2 + hh
                            for mc in range(2):
                                ins = e.matmul(psc[hp][:, (hh * 2 + mc) * 128:(hh * 2 + mc + 1) * 128],
                                               lhsT=ktl[:, h, mc * 128:(mc + 1) * 128], rhs=qT[:, h, :], start=True, stop=True)
                        return ins
                    kb.op("pe", scmm, reads=[ktl, qT], writes=[psc[hp]])
                PT = [kb.nxt("E"), kb.nxt("E")]
                for hp in range(2):
                    kb.op("act", lambda e, hp=hp: e.activation(out=PT[hp][:, :, :].rearrange("p a b -> p (a b)"),
                                                               in_=psc[hp][:, :], func=AF.Exp, scale=SCALE),
                          reads=[psc[hp]], writes=[PT[hp]])
                kb.psrel(psc[0], psc[1])
                psum_ = kb.psalloc()
                pxa = kb.psalloc()

                def summm(e):
                    ins = None
                    for h in range(4):
                        for mc in range(2):
                            ins = e.matmul(psum_[:, h * 128:(h + 1) * 128], lhsT=onesb, rhs=PT[h // 2][:, (h % 2) * 2 + mc, :],
                                           start=(mc == 0), stop=(mc == 1))
                    return ins
                kb.op("pe", summm, reads=[cb, PT[0], PT[1]], writes=[psum_])

                def pvmm(e):
                    ins = None
                    for h in range(4):
                        for mc in range(2):
                            ins = e.matmul(pxa[:, h * 128:(h + 1) * 128], lhsT=vl[:, mc, h * 128:(h + 1) * 128],
                                           rhs=PT[h // 2][:, (h % 2) * 2 + mc, :], start=(mc == 0), stop=(mc == 1))
                    return ins
                kb.op("pe", pvmm, reads=[vl, PT[0], PT[1]], writes=[pxa])

            for ch in range(12):
                gi = ch // 4
                if sample:
                    def xin(k, ch=ch):
                        return A(xb, ch * 176 + k, [[11, 16], [1, 8]])
                    tout = A(big6, ch * 128, [[8, 16], [1, 8]])
                else:
                    def xin(k, ch=ch):
                        return A(xb, ch * 176 + k, [[1, 128]])
                    tout = A(big6, ch * 128, [[1, 128]])
                kb.op("dve", lambda e, xin=xin, tout=tout, ch=ch: e.tensor_scalar_mul(
                    out=tout, in0=xin(0), scalar1=cpcol(l, CP_CW + ch * 4 + 0)),
                    reads=[XB[gi], colpar], writes=[BIG[gi]], first=None if ch % 4 == 0 else [])
                for k in range(1, 4):
                    kb.op("dve", lambda e, xin=xin, tout=tout, ch=ch, k=k: e.scalar_tensor_tensor(
                        out=tout, in0=xin(k), scalar=cpcol(l, CP_CW + ch * 4 + k), in1=tout, op0=ALU.mult, op1=ALU.add),
                        reads=[XB[gi], colpar, BIG[gi]], writes=[BIG[gi]], first=[])
                kb.op("act", lambda e, ch=ch: e.activation(out=xbcsT[:, ch, :], in_=big6[:, ch * 128:(ch + 1) * 128], func=AF.Silu,
                                                           bias=cpcol(l, CP_CB + ch)),
                      reads=[BIG[gi], colpar], writes=[XS[gi]], first=None if ch % 4 == 0 else [])

            def attn_tail(psum_, pxa):
                if sample:
                    kb.op("dve", lambda e: e.reciprocal(out=A(Rb, 0, [[128, 4], [8, 16], [1, 8]]),
                                                        in_=A(psum_, 0, [[8, 4], [32, 16], [1, 8]])), reads=[psum_], writes=[Rb])
                else:
                    kb.op("dve", lambda e: e.reciprocal(out=Rb[:, :, :].rearrange("p a b -> p (a b)"), in_=psum_[:, :]),
                          reads=[psum_], writes=[Rb])
                kb.op("dve", lambda e: e.tensor_tensor(out=Rb[:, :, :].rearrange("p a b -> p (a b)"),
                                                       in0=pxa[:, :], in1=Rb[:, :, :].rearrange("p a b -> p (a b)"), op=ALU.mult),
                      reads=[pxa, Rb], writes=[Rb])
                kb.psrel(psum_, pxa)
                kb.op("pool", lambda e: e.tensor_tensor(out=actT[:, 12:16, :], in0=Rb[:, :, :], in1=sgxT[:, :, :], op=ALU.mult),
                      reads=[Rb, sgxT], writes=[actT], first=actT_first())
            if not sample:
                attn_tail(psum_, pxa)

            pu = kb.psalloc()
            mm_tok(pu, 512, win, lambda k: win[:, k, C_U:C_U + 512])
            ub = kb.nxt("ubf")
            kb.op("act", lambda e: e.copy(out=ub[:, :], in_=pu[:, :]), reads=[pu], writes=[ub])
            if special:
                kb.op("act", lambda e: e.copy(out=scr[:, 0:512], in_=pu[:, :]), reads=[pu], writes=[scr])
                if sample:
                    for s_ in range(NS):
                        kb.dma("sp", o_pool_s[l, s_, 7:15, :], scr[s_ * 8:(s_ + 1) * 8, 0:512], reads=[scr],
                               writes=[o_pool_s], sembuf=scr)
                else:
                    kb.dma("sp", o_pool_p[l, :, :], scr[113:128, 0:512], reads=[scr], writes=[o_pool_p], sembuf=scr)
            kb.psrel(pu)
            pg = kb.psalloc()
            mm_feat(pg, 4, win, lambda k, c: win[:, k, C_GP + c * 128:C_GP + (c + 1) * 128])
            kb.op("act", lambda e: e.activation(out=sgpT[:, :, :].rearrange("p a b -> p (a b)"), in_=pg[:, :], func=AF.Silu),
                  reads=[pg], writes=[sgpT])
            kb.psrel(pg)
            for half in range(2):
                pz = kb.psalloc()
                mm_tok(pz, 512, win, lambda k, half=half: win[:, k, C_Z + half * 512:C_Z + (half + 1) * 512])
                kb.op("act", lambda e, half=half, pz=pz: e.activation(out=sz[:, half * 512:(half + 1) * 512], in_=pz[:, :],
                                                                      func=AF.Silu),
                      reads=[pz], writes=[sz], first=None if half == 0 else [])
                kb.psrel(pz)
            pd = kb.psalloc()
            mm_tok(pd, 16, win, lambda k: win[:, k, C_DT:C_DT + 16])
            dt = smt(0)
            kb.op("dve", lambda e: e.tensor_tensor(out=dt, in0=pd[:, 0:16], in1=rowpar[:, rp:rp + 16], op=ALU.add),
                  reads=[pd, rowpar], writes=[sm])
            kb.psrel(pd)
            kb.op("act", lambda e: e.activation(out=dt, in_=dt, func=AF.Exp), reads=[sm], writes=[sm])
            kb.op("act", lambda e: e.activation(out=dt, in_=dt, func=AF.Ln, bias=1.0), reads=[sm], writes=[sm])
            if special:
                for part in range(3):
                    px = kb.psalloc()
                    mm_tok(px, 512, win, lambda k, part=part: win[:, k, C_XBC + part * 512:C_XBC + (part + 1) * 512])
                    kb.op("act", lambda e, px=px: e.copy(out=scr[:, 512:1024], in_=px[:, :]), reads=[px], writes=[scr])
                    kb.psrel(px)
                    if sample:
                        for s_ in range(NS):
                            kb.dma("sp", o_conv_s[l, s_, :, part * 512:(part + 1) * 512],
                                   scr[s_ * 8 + 5:s_ * 8 + 8, 512:1024], reads=[scr], writes=[o_conv_s], sembuf=scr)
                    else:
                        kb.dma("sp", o_conv_p[l, :, part * 512:(part + 1) * 512], scr[125:128, 512:1024], reads=[scr],
                               writes=[o_conv_p], sembuf=scr)
            if sample and l < LL - 1:
                load_win(l + 1)

            pdT = kb.psalloc()

            def poolmm(e):
                ins = None
                for g in range(4):
                    o = pdT[:, g * 128:(g + 1) * 128]
                    ul = ub[:, g * 128:(g + 1) * 128]
                    if sample:
                        e.matmul(o, lhsT=ul, rhs=band(g, 3), start=True, stop=False)
                        e.matmul(o, lhsT=spool16[0:120, 0, g * 128:(g + 1) * 128], rhs=band(g, 4)[0:120, :],
                                 start=False, stop=False)
                        ins = e.matmul(o, lhsT=spool16[0:120, 1, g * 128:(g + 1) * 128], rhs=band(g, 5)[0:120, :],
                                       start=False, stop=True)
                    elif ti == 0:
                        ins = e.matmul(o, lhsT=ul, rhs=band(g, 2), start=True, stop=True)
                    else:
                        e.matmul(o, lhsT=ul, rhs=band(g, 0), start=True, stop=False)
                        ins = e.matmul(o, lhsT=uprev[64:128, g * 128:(g + 1) * 128], rhs=band(g, 1)[64:128, :],
                                       start=False, stop=True)
                return ins
            if sample:
                kb.dma("pool", spool16[0:120, 0, :], st_pool[l, 0:120, :], writes=[spool16], sembuf=spool16)
                kb.dma("pool", spool16[0:120, 1, :], st_pool[l, 120:240, :], writes=[spool16], first=[], sembuf=spool16)
                kb.dma("sp", o_pool_s[l, :, 0:7, :], st_pool[l, :, :].rearrange("(s r) c -> s r c", r=15)[:, 8:15, :],
                       writes=[o_pool_s], sembuf=d2d)
                kb.op("pe", poolmm, reads=[ub, cb, spool16], writes=[pdT])
            elif ti == 0:
                kb.op("pe", poolmm, reads=[ub, cb], writes=[pdT])
            else:
                uprev = tile_state["uprev"]
                kb.op("pe", poolmm, reads=[ub, cb, uprev], writes=[pdT])
            tile_state["uprev"] = ub
            for g in range(4):
                if (not sample) and ti == 0:
                    kb.op("dve", lambda e, g=g: e.tensor_tensor(out=dT[:, g, :], in0=pdT[:, g * 128:(g + 1) * 128],
                                                                in1=cf[:, CF_IC0 + g * 128:CF_IC0 + (g + 1) * 128], op=ALU.mult),
                          reads=[pdT, cf], writes=[dT], first=None if g == 0 else [])
                else:
                    kb.op("act", lambda e, g=g: e.mul(out=dT[:, g, :], in_=pdT[:, g * 128:(g + 1) * 128], mul=1.0 / POOL_W[g]),
                          reads=[pdT], writes=[dT], first=None if g == 0 else [])
            kb.psrel(pdT)
            py = kb.psalloc()

            def poolmm2(e):
                ins = None
                for g in range(4):
                    ins = e.matmul(py[:, g * 128:(g + 1) * 128], lhsT=poolw[:, g, :], rhs=dT[:, g, :], start=True, stop=True)
                return ins
            kb.op("pe", poolmm2, reads=[poolw, dT], writes=[py])
            for g in range(4):
                kb.op("dve", lambda e, g=g: e.scalar_tensor_tensor(out=actT[:, g, :], in0=py[:, g * 128:(g + 1) * 128],
                                                                   scalar=cpcol(l, CP_PS + g), in1=sgpT[:, g, :],
                                                                   op0=ALU.mult, op1=ALU.mult),
                      reads=[py, colpar, sgpT], writes=[actT], first=actT_first())
            kb.psrel(py)
            if sample and l < LL - 1:
                load_poolw(l + 1)

            tri = cf[:, (CF_TRIS if sample else CF_TRI):(CF_TRIS if sample else CF_TRI) + 128]
            onem = cf[:, (CF_BONE if sample else CF_ONE):(CF_BONE if sample else CF_ONE) + 128]
            su = cf[:, CF_SU:CF_SU + 128]
            a_ = smt(1)
            cum = smt(2)
            e_ = smt(3)
            wend = smt(4)
            dec = smt(5)
            kb.op("dve", lambda e: e.tensor_tensor(out=a_, in0=dt, in1=abc[:, l * 16:(l + 1) * 16], op=ALU.mult),
                  reads=[sm, abc], writes=[sm], first=[])
            pcm = kb.psalloc()

            def cummm(e):
                e.matmul(pcm[:, 0:16], lhsT=tri, rhs=a_, start=True, stop=True)
                return e.matmul(pcm[:, 16:32], lhsT=onem, rhs=a_, start=True, stop=True)
            kb.op("pe", cummm, reads=[cf, sm], writes=[pcm])
            kb.op("act", lambda e: e.copy(out=cum, in_=pcm[:, 0:16]), reads=[pcm], writes=[sm], first=[])
            kb.op("act", lambda e: e.activation(out=e_, in_=pcm[:, 0:16], func=AF.Exp), reads=[pcm], writes=[sm], first=[])
            kb.op("act", lambda e: e.activation(out=dec, in_=pcm[:, 16:32], func=AF.Exp), reads=[pcm], writes=[sm], first=[])
            kb.op("dve", lambda e: e.tensor_tensor(out=wend, in0=pcm[:, 16:32], in1=cum, op=ALU.subtract),
                  reads=[pcm, sm], writes=[sm], first=[])
            kb.psrel(pcm)
            kb.op("act", lambda e: e.activation(out=wend, in_=wend, func=AF.Exp), reads=[sm], writes=[sm], first=[])
            kb.op("dve", lambda e: e.tensor_tensor(out=wend, in0=wend, in1=dt, op=ALU.mult), reads=[sm], writes=[sm], first=[])

            ptx = kb.psalloc()
            ptxb = ptx[:, :].bitcast(BF16)

            def trx(e):
                ins = None
                for c in range(8):
                    ins = e.transpose(out=ptxb[:, c * 128:(c + 1) * 128], in_=xbcsT[:, c, :], identity=identb)
                return ins
            kb.op("pe", trx, reads=[XS[0], XS[1], cb], writes=[ptx])
            ptx3 = ptxb.rearrange("p (h d) -> p h d", h=16)

            def bc16(off):
                return A(sm, off, [[1, 16], [0, 64]])
            kb.op("dve", lambda e: e.tensor_tensor(out=xdt[:, :].rearrange("p (h d) -> p h d", h=16), in0=ptx3,
                                                   in1=bc16(0), op=ALU.mult), reads=[ptx, sm], writes=[xdt])
            kb.op("dve", lambda e: e.tensor_tensor(out=xw[:, :].rearrange("p (h d) -> p h d", h=16), in0=ptx3,
                                                   in1=bc16(4 * 16), op=ALU.mult), reads=[ptx, sm], writes=[xw])
            kb.op("dve", lambda e: e.tensor_tensor(out=xd[:, :].rearrange("p (h d) -> p h d", h=16), in0=ptx3,
                                                   in1=A(rowpar, rp + 32, [[1, 16], [0, 64]]), op=ALU.mult),
                  reads=[ptx, rowpar], writes=[xd])
            kb.psrel(ptx)
            ptb_ = kb.psalloc()
            ptbb = ptb_[:, :].bitcast(BF16)

            def trb(e):
                e.transpose(out=ptbb[:, 0:128], in_=xbcsT[:, 8, :], identity=identb)
                return e.transpose(out=ptbb[:, 128:256], in_=xbcsT[:, 9, :], identity=identb)
            kb.op("pe", trb, reads=[XS[2], cb], writes=[ptb_])
            kb.op("act", lambda e: e.copy(out=btok[:, :], in_=ptbb[:, 0:256]), reads=[ptb_], writes=[btok])
            kb.psrel(ptb_)
            pcb = kb.psalloc()

            def cbmm(e):
                e.matmul(pcb[:, 0:128], lhsT=xbcsT[:, 8, :], rhs=xbcsT[:, 10, :], start=True, stop=True)
                return e.matmul(pcb[:, 128:256], lhsT=xbcsT[:, 9, :], rhs=xbcsT[:, 11, :], start=True, stop=True)
            kb.op("pe", cbmm, reads=[XS[2]], writes=[pcb])
            tri_off = CF_TRIS if sample else CF_TRI
            kb.op("dve", lambda e: e.tensor_tensor(out=cbm[:, :, :], in0=pcb[:, 0:256].rearrange("p (g i) -> p g i", g=2),
                                                   in1=A(cf, tri_off, [[0, 2], [1, 128]]), op=ALU.mult),
                  reads=[pcb, cf], writes=[cbm])
            kb.psrel(pcb)
            if sample:
                pyit = sample_seq_phase(l)
                attn_tail(tile_state["psum_"], tile_state["pxa"])
            gen = front_gen(*nxt_tile) if nxt_tile is not None else iter(())

            def step():
                next(gen, None)
            step()
            pY = [kb.psalloc(), kb.psalloc()]
            for g in range(2):
                kb.op("pe", lambda e, g=g: e.matmul(pY[g][:, :], lhsT=identb, rhs=xd[:, g * 512:(g + 1) * 512],
                                                    start=True, stop=False), reads=[cb, xd], writes=[pY[g]])
            stage = {}

            def intra_a(q4):
                if q4 % 2 == 0:
                    Rbuf, R3, R2 = Rb, Rb[:, :, :], Rb[:, :, :].rearrange("p a b -> p (a b)")
                else:
                    Rbuf, R3, R2 = BIG[2], A(big6, 1024, [[128, 4], [1, 128]]), big6[:, 1024:1536]
                kb.op("pool", lambda e: e.tensor_tensor(out=R3, in0=A(sm, 16 + q4 * 4, [[1, 4], [0, 128]]),
                                                        in1=A(cf, tri_off, [[0, 4], [1, 128]]), op=ALU.mult),
                      reads=[sm, cf], writes=[Rbuf])
                pseg = kb.psalloc()
                kb.op("pe", lambda e: e.matmul(pseg[:, :], lhsT=su, rhs=R2, start=True, stop=True),
                      reads=[cf, Rbuf], writes=[pseg])
                stage[q4] = pseg

            def intra_b(q4):
                pseg = stage.pop(q4)
                Eb = kb.nxt("E")
                kb.op("act", lambda e: e.activation(out=Eb[:, :, :].rearrange("p a b -> p (a b)"),
                                                    in_=pseg[:, :], func=AF.Exp), reads=[pseg], writes=[Eb])
                kb.psrel(pseg)
                MTb = kb.nxt("MT")
                g = q4 // 2
                kb.op("dve", lambda e: e.tensor_tensor(
                    out=MTb[:, :, :], in0=Eb[:, :, :], in1=A(cbm, g * 128, [[0, 4], [1, 128]]), op=ALU.mult),
                    reads=[Eb, cbm], writes=[MTb])

                def ymm(e):
                    ins = None
                    for hh in range(4):
                        h = q4 * 4 + hh
                        o = pY[h // 8][:, (h % 8) * 64:(h % 8 + 1) * 64]
                        ins = e.matmul(o, lhsT=MTb[:, hh, :], rhs=xdt[:, h * 64:(h + 1) * 64], start=False, stop=(h % 8 == 7))
                    return ins
                kb.op("pe", ymm, reads=[MTb, xdt], writes=[pY[q4 // 2]], first=[])
            intra_a(0)
            for q4 in range(4):
                if q4 + 1 < 4:
                    intra_a(q4 + 1)
                step()
                intra_b(q4)
            for _ in gen:
                pass

            if sample:
                for g in range(2):
                    kb.op("act" if g == 0 else "dve",
                          (lambda e, g=g: e.copy(out=scr[:, g * 512:(g + 1) * 512], in_=pyit[g][:, :])) if g == 0 else
                          (lambda e, g=g: e.tensor_copy(out=scr[:, g * 512:(g + 1) * 512], in_=pyit[g][:, :])),
                          reads=[pyit[g]], writes=[scr], first=None if g == 0 else [])
                kb.psrel(pyit[0], pyit[1])
            pYI = [kb.psalloc(), kb.psalloc()]
            if sample:
                for g in range(2):
                    def tryi(e, g=g):
                        ins = None
                        for c in range(4):
                            cc = g * 4 + c
                            ins = e.transpose(out=pYI[g][:, c * 128:(c + 1) * 128], in_=scr[:, cc * 128:(cc + 1) * 128],
                                              identity=identf)
                        return ins
                    kb.op("pe", tryi, reads=[scr, cf], writes=[pYI[g]])
            if not sample:
                ST = Sb[0]
                if ti == 0:
                    kb.op("pool", lambda e: e.memset(ST[:, :], 0.0), writes=[ST])
                    kb.op("pool", lambda e: e.memset(stb16[:, :], 0.0), writes=[stb16])
                for g in range(2):
                    kb.op("pe", lambda e, g=g: e.matmul(pYI[g][:, :], lhsT=xbcsT[:, 10 + g, :],
                                                        rhs=stb16[:, g * 512:(g + 1) * 512], start=True, stop=True),
                          reads=[XS[2], stb16], writes=[pYI[g]])

            y = big6
            YB = [BIG[0], BIG[1]]
            for g in range(2):
                kb.op("dve", lambda e, g=g: e.tensor_tensor(
                    out=A(y, g * 512, [[64, 8], [1, 64]]), in0=pYI[g][:, :].rearrange("p (h d) -> p h d", h=8),
                    in1=A(sm, 3 * 16 + g * 8, [[1, 8], [0, 64]]), op=ALU.mult),
                    reads=[pYI[g], sm], writes=[YB[g]])
            for g in range(2):
                kb.op("dve", lambda e, g=g: e.tensor_tensor(out=y[:, g * 512:(g + 1) * 512], in0=y[:, g * 512:(g + 1) * 512],
                                                            in1=pY[g][:, :], op=ALU.add), reads=[YB[g], pY[g]], writes=[YB[g]])
            kb.psrel(pYI[0], pYI[1], pY[0], pY[1])
            for g in range(2):
                kb.op("pool", lambda e, g=g: e.tensor_tensor(out=y[:, g * 512:(g + 1) * 512], in0=y[:, g * 512:(g + 1) * 512],
                                                             in1=sz[:, g * 512:(g + 1) * 512], op=ALU.mult),
                      reads=[YB[g], sz], writes=[YB[g]])
            ynb = xn
            for g in range(2):
                kb.op("act", lambda e, g=g: e.activation(out=ynb[:, g * 512:(g + 1) * 512], in_=y[:, g * 512:(g + 1) * 512],
                                                         func=AF.Square, accum_out=sm2[:, 2 + g:3 + g]),
                      reads=[YB[g]], writes=[ynb, sm2], first=None if g == 0 else [])
            kb.op("act", lambda e: e.activation(out=sm2[:, 4:6], in_=sm2[:, 2:4], func=AF.Ln, scale=1.0 / 512, bias=EPS),
                  reads=[sm2], writes=[sm2], first=[])
            kb.op("act", lambda e: e.activation(out=sm2[:, 4:6], in_=sm2[:, 4:6], func=AF.Exp, scale=-0.5),
                  reads=[sm2], writes=[sm2], first=[])
            for g in range(2):
                kb.op("dve", lambda e, g=g: e.tensor_scalar_mul(
                    out=ynb[:, g * 512:(g + 1) * 512], in0=y[:, g * 512:(g + 1) * 512], scalar1=sm2[:, 4 + g:5 + g]),
                    reads=[YB[g], sm2], writes=[ynb], first=None if g == 0 else [])
            pyt = kb.psalloc()
            pytb = pyt[:, :].bitcast(BF16)

            def try_(e):
                ins = None
                for c in range(8):
                    ins = e.transpose(out=pytb[:, c * 128:(c + 1) * 128], in_=ynb[:, c * 128:(c + 1) * 128], identity=identb)
                return ins
            kb.op("pe", try_, reads=[ynb, cb], writes=[pyt])
            kb.op("dve", lambda e: e.tensor_tensor(out=actT[:, 4:12, :], in0=pytb[:, :].rearrange("p (c t) -> p c t", c=8),
                                                   in1=A(colpar, l * NCP + CP_SG, [[1, 8], [0, 128]]), op=ALU.mult),
                  reads=[pyt, colpar], writes=[actT], first=actT_first())
            kb.psrel(pyt)

            if not sample:
                ST = Sb[0]
                pup = [kb.psalloc(), kb.psalloc()]
                for g in range(2):
                    kb.op("pe", lambda e, g=g: e.matmul(pup[g][:, :], lhsT=btok[:, g * 128:(g + 1) * 128],
                                                        rhs=xw[:, g * 512:(g + 1) * 512], start=True, stop=True),
                          reads=[btok, xw], writes=[pup[g]])
                kb.op("pool", lambda e: e.tensor_tensor(out=ST[:, :].rearrange("p (h d) -> p h d", h=16),
                                                        in0=ST[:, :].rearrange("p (h d) -> p h d", h=16),
                                                        in1=A(sm, 5 * 16, [[1, 16], [0, 64]]), op=ALU.mult),
                      reads=[ST, sm], writes=[ST])
                for g in range(2):
                    kb.op("dve", lambda e, g=g: e.tensor_tensor(out=ST[:, g * 512:(g + 1) * 512], in0=ST[:, g * 512:(g + 1) * 512],
                                                                in1=pup[g][:, :], op=ALU.add), reads=[ST, pup[g]], writes=[ST])
                kb.psrel(pup[0], pup[1])
                if last_p:
                    for half in range(2):
                        pt_ = kb.psalloc()

                        def trs(e, pt_=pt_, half=half):
                            ins = None
                            for c in range(4):
                                cc = half * 4 + c
                                ins = e.transpose(out=pt_[:, c * 128:(c + 1) * 128], in_=ST[:, cc * 128:(cc + 1) * 128],
                                                  identity=identf)
                            return ins
                        kb.op("pe", trs, reads=[ST, cf], writes=[pt_])
                        kb.op("act", lambda e, pt_=pt_: e.copy(out=scr[:, 0:512], in_=pt_[:, :]), reads=[pt_], writes=[scr])
                        kb.psrel(pt_)
                        kb.dma("sp", o_ssm_p[l, half * 512:(half + 1) * 512, :].rearrange("(c p) n -> p c n", p=128),
                               scr[:, 0:512].rearrange("p (c n) -> p c n", c=4), reads=[scr], writes=[o_ssm_p], sembuf=scr)
                else:
                    kb.op("act", lambda e: e.copy(out=stb16[:, :], in_=ST[:, :]), reads=[ST], writes=[stb16])

            po = [kb.psalloc(), kb.psalloc()]
            for n in range(2):
                def omm(e, n=n):
                    ins = None
                    for k in range(16):
                        ins = e.matmul(po[n][:, :], lhsT=actT[:, k, :], rhs=wout[:, k, n * 512:(n + 1) * 512],
                                       start=(k == 0), stop=(k == 15))
                    return ins
                kb.op("pe", omm, reads=[actT, wout], writes=[po[n]])
            for n in range(2):
                kb.op("dve", lambda e, n=n: e.tensor_tensor(out=hb[:, n * 512:(n + 1) * 512], in0=hb[:, n * 512:(n + 1) * 512],
                                                            in1=po[n][:, :], op=ALU.add), reads=[hb, po[n]], writes=[hb])
            kb.psrel(po[0], po[1])
            if sample and l < LL - 1:
                load_wout(l + 1)
                load_attn(l + 1)
            if l < LL - 1:
                kb.dma("sp", hs[hidx * 128:(hidx + 1) * 128, :], hb[:, :], reads=[hb], writes=[hsB[hidx]], sembuf=hb)
            else:
                ss = sm2[:, 8:9]
                rs = sm2[:, 9:10]
                fj = scr if not special else xd
                kb.op("act", lambda e: e.activation(out=fj[:, 0:1024], in_=hb[:, :], func=AF.Square, accum_out=ss),
                      reads=[hb], writes=[fj, sm2n])
                kb.op("act", lambda e: e.activation(out=rs, in_=ss, func=AF.Ln, scale=1.0 / D, bias=EPS),
                      reads=[sm2n], writes=[sm2n])
                kb.op("act", lambda e: e.activation(out=rs, in_=rs, func=AF.Exp, scale=-0.5), reads=[sm2n], writes=[sm2n])
                kb.op("dve", lambda e: e.scalar_tensor_tensor(out=hb[:, :], in0=hb[:, :], scalar=rs, in1=fng[:, :],
                                                              op0=ALU.mult, op1=ALU.mult), reads=[hb, sm2n, fng], writes=[hb])
                dst = ys[:, :] if sample else yp[ti * 128:(ti + 1) * 128, :]
                kb.dma("sp", dst, hb[:, :], reads=[hb], writes=[ys if sample else yp], sembuf=hb)

        def sample_seq_phase(l):
            abcast = big6
            kb.op("pool", lambda e: e.tensor_copy(out=abcast[:, 0:1024].rearrange("p (h d) -> p h d", h=16),
                                                  in_=A(sm, 16, [[1, 16], [0, 64]])), reads=[sm], writes=[abcast])
            pdc = kb.psalloc()

            def dcmm(e):
                ins = None
                for c in range(8):
                    ins = e.matmul(pdc[:, c * 16:(c + 1) * 16], lhsT=abcast[:, c * 128:(c + 1) * 128],
                                   rhs=cf[:, CF_BM:CF_BM + 16], start=True, stop=True)
                return ins
            kb.op("pe", dcmm, reads=[abcast, cf], writes=[pdc])
            kb.op("act", lambda e: e.activation(out=decT[:, :, :].rearrange("p a b -> p (a b)"), in_=pdc[:, 0:128], func=AF.Exp),
                  reads=[pdc], writes=[decT])
            kb.psrel(pdc)
            pyit = [kb.psalloc(), kb.psalloc()]
            psum_ = kb.psalloc()
            pxa = kb.psalloc()
            tile_state["psum_"], tile_state["pxa"] = psum_, pxa
            KT2 = [ktl[:, :, :].rearrange("p h m -> p (h m)"), vl[:, :, :].rearrange("p a b -> p (a b)")]
            KTB = [ktl, vl]
            STB = [stb16, xn]
            st = {}

            def stage_a(s_):
                Sin = kb.nxt("Sb")
                kb.dma("sp", Sin[:, :].rearrange("p (c n) -> p c n", c=8),
                       st_ssm[l, s_, :, :].rearrange("(c p) n -> p c n", p=128), writes=[Sin], sembuf=Sin)
                Kb = kb.nxt("Kb")
                Vb = kb.nxt("Vb")
                kb.dma("pool", Kb[:, :, :], ck[l, s_, :, :].rearrange("(c p) n -> p c n", p=128), writes=[Kb], sembuf=Kb)
                kb.dma("pool", Vb[:, :, :], cv[l, s_, :, :].rearrange("(c p) n -> p c n", p=128), writes=[Vb], sembuf=Vb)
                stb = STB[s_ % 2]
                for half in range(2):
                    pts_ = kb.psalloc()

                    def trs(e, pts_=pts_, half=half):
                        ins = None
                        for c in range(4):
                            cc = half * 4 + c
                            ins = e.transpose(out=pts_[:, c * 128:(c + 1) * 128], in_=Sin[:, cc * 128:(cc + 1) * 128],
                                              identity=identf)
                        return ins
                    kb.op("pe", trs, reads=[Sin, cf], writes=[pts_])
                    if half == 0:
                        kb.op("act", lambda e, pts_=pts_: e.copy(out=stb[:, 0:512], in_=pts_[:, :]), reads=[pts_], writes=[stb])
                    else:
                        kb.op("dve", lambda e, pts_=pts_: e.tensor_copy(out=stb[:, 512:1024], in_=pts_[:, :]),
                              reads=[pts_], writes=[stb], first=[])
                    kb.psrel(pts_)
                pkt = kb.psalloc()
                pktb = pkt[:, :].bitcast(BF16)

                def trk(e):
                    ins = None
                    for h in range(4):
                        for mc in range(2):
                            ins = e.transpose(out=pktb[:, (h * 2 + mc) * 128:(h * 2 + mc + 1) * 128],
                                              in_=Kb[:, mc, h * 128:(h + 1) * 128], identity=identb)
                    return ins
                kb.op("pe", trk, reads=[Kb, cb], writes=[pkt])
                kb.op("act", lambda e: e.copy(out=KT2[s_ % 2], in_=pktb[:, :]), reads=[pkt], writes=[KTB[s_ % 2]])
                kb.psrel(pkt)
                Bs = kb.nxt("Bs")
                kb.op("pool", lambda e: e.tensor_scalar_mul(out=Bs[:, :], in0=btok[:, :],
                                                            scalar1=cf[:, CF_BM + s_:CF_BM + s_ + 1]), reads=[btok, cf], writes=[Bs])
                st[s_] = (Sin, Vb, Bs)

            def stage_b(s_):
                c0 = s_ * 8
                Sin, Vb, Bs = st.pop(s_)
                stb = STB[s_ % 2]
                kt = KT2[s_ % 2].rearrange("p (h m) -> p h m", h=4)
                ktb = KTB[s_ % 2]

                def yimm(e):
                    ins = None
                    for c in range(8):
                        ins = e.matmul(pyit[c // 4][:, (c % 4) * 128 + c0:(c % 4) * 128 + c0 + 8],
                                       lhsT=stb[:, c * 128:(c + 1) * 128], rhs=xbcsT[:, 10 + c // 4, c0:c0 + 8],
                                       start=True, stop=True)
                    return ins
                kb.op("pe", yimm, reads=[stb, XS[2]], writes=[pyit[0], pyit[1]], first=None if s_ == 0 else [])
                pup = [kb.psalloc(), kb.psalloc()]

                def upmm(e):
                    ins = None
                    for c in range(8):
                        g = c // 4
                        ins = e.matmul(pup[g][:, (c % 4) * 128:(c % 4 + 1) * 128], lhsT=xw[:, c * 128:(c + 1) * 128],
                                       rhs=Bs[:, g * 128:(g + 1) * 128], start=True, stop=True)
                    return ins
                kb.op("pe", upmm, reads=[xw, Bs], writes=[pup[0], pup[1]])
                for c in range(8):
                    kb.op("dve", lambda e, c=c: e.scalar_tensor_tensor(
                        out=Sin[:, c * 128:(c + 1) * 128], in0=Sin[:, c * 128:(c + 1) * 128], scalar=decT[:, c, s_:s_ + 1],
                        in1=pup[c // 4][:, (c % 4) * 128:(c % 4 + 1) * 128], op0=ALU.mult, op1=ALU.add),
                        reads=[Sin, decT, pup[c // 4]], writes=[Sin])
                kb.psrel(pup[0], pup[1])
                kb.dma("sp", o_ssm_s[l, s_, :, :].rearrange("(c p) n -> p c n", p=128),
                       Sin[:, :].rearrange("p (c n) -> p c n", c=8), reads=[Sin], writes=[o_ssm_s], sembuf=Sin)
                psc = kb.psalloc()

                def scmm(e):
                    ins = None
                    for h in range(4):
                        for mc in range(2):
                            ins = e.matmul(psc[:, mc * 32 + h * 8:mc * 32 + h * 8 + 8], lhsT=kt[:, h, mc * 128:(mc + 1) * 128],
                                           rhs=qT[:, h, c0:c0 + 8], start=True, stop=True)
                    return ins
                kb.op("pe", scmm, reads=[ktb, qT], writes=[psc])
                pts = kb.nxt("pts")
                kb.op("act", lambda e: e.activation(out=pts[:, :], in_=psc[:, 0:64], func=AF.Exp, scale=SCALE),
                      reads=[psc], writes=[pts])
                kb.psrel(psc)

                def smm(e):
                    o = psum_[:, s_ * 32:(s_ + 1) * 32]
                    e.matmul(o, lhsT=onesb, rhs=pts[:, 0:32], start=True, stop=False)
                    ins = e.matmul(o, lhsT=onesb, rhs=pts[:, 32:64], start=False, stop=True)
                    for h in range(4):
                        for mc in range(2):
                            ins = e.matmul(pxa[:, h * 128 + c0:h * 128 + c0 + 8], lhsT=Vb[:, mc, h * 128:(h + 1) * 128],
                                           rhs=pts[:, mc * 32 + h * 8:mc * 32 + h * 8 + 8], start=(mc == 0), stop=(mc == 1))
                    return ins
                kb.op("pe", smm, reads=[cb, pts, Vb], writes=[psum_, pxa], first=None if s_ == 0 else [])

            stage_a(0)
            for s_ in range(NS):
                if s_ + 1 < NS:
                    stage_a(s_ + 1)
                stage_b(s_)
            return pyit

        tile_state = {}
        d2d = Buf("d2d", None)
        order = []
        for l in range(cfg['layers']):
            for ti in range(cfg['tiles']):
                order.append((l, ti, False))
            if cfg['sample']:
                order.append((l, 0, True))
        if order:
            load_win(0)
            load_poolw(0)
            load_attn(0)
            load_wout(0)
        for i, (l, ti, smp) in enumerate(order):
            do_tile(l, ti, smp, order[i + 1] if i + 1 < len(order) else None)
        kb.finish(outs)
        print("total ops", kb.nops)
    return nc


def _constants():
    cf = np.zeros((128, NCF), np.float32)
    r = np.arange(128)
    cf[:, CF_ID:CF_ID + 128] = np.eye(128)
    tri = (r[:, None] <= r[None, :]).astype(np.float32)
    same = (r[:, None] // 8 == r[None, :] // 8).astype(np.float32)
    cf[:, CF_TRI:CF_TRI + 128] = tri
    cf[:, CF_TRIS:CF_TRIS + 128] = tri * same
    cf[:, CF_SU:CF_SU + 128] = (r[:, None] > r[None, :]).astype(np.float32)
    cf[:, CF_ONE:CF_ONE + 128] = 1.0
    cf[:, CF_BONE:CF_BONE + 128] = same
    cf[:, CF_BM:CF_BM + 16] = (r[:, None] // 8 == np.arange(16)[None, :]).astype(np.float32)
    cb = np.zeros((128, NCB), np.float32)
    cb[:, CB_ID:CB_ID + 128] = np.eye(128)
    cb[:, CB_ONE:CB_ONE + 128] = 1.0
    for g, w in enumerate(POOL_W):
        s = r[:, None]
        t = r[None, :]
        cur = ((s <= t) & (s > t - w)).astype(np.float32) - w * (s == t)
        prev = np.zeros((128, 128), np.float32)
        srel = s - 128
        prev[:, :] = ((srel > t - w)).astype(np.float32)
        prev[:64, :] = 0.0
        cnt0 = np.minimum(t + 1, w).astype(np.float32)
        cur0 = ((s <= t) & (s > t - w)).astype(np.float32) - cnt0 * (s == t)
        cf[:, CF_IC0 + g * 128:CF_IC0 + (g + 1) * 128] = np.broadcast_to(1.0 / cnt0, (128, 128))
        ss_, ts_ = s // 8, s % 8
        sc_, tc_ = t // 8, t % 8
        bs = ((ss_ == sc_) & (ts_ <= tc_) & (ts_ > tc_ - w)).astype(np.float32) - w * (s == t)
        sta = np.zeros((128, 128), np.float32)
        stbm = np.zeros((128, 128), np.float32)
        rows = np.arange(120)
        sq, rr = rows // 15, rows % 15
        for half, m in ((0, sta), (1, stbm)):
            m[:120, :] = ((sq[:, None] + 8 * half == sc_) & ((rr[:, None] - 15) > (tc_ - w))).astype(np.float32)
        for k, m in enumerate((cur, prev, cur0, bs, sta, stbm)):
            o = CB_BAND + (g * 6 + k) * 128
            cb[:, o:o + 128] = m
    return cf, cb


_NC_CACHE = {}
_DBG_CFG = None


def kernel(x_prompt, x_sample, mem_prompt, state_pool, state_conv, state_ssm, cache_mem_k, cache_mem_v,
           norm_g, w_in, pool_w, pool_scale, conv_w, conv_b, dt_bias, a_log, d_skip, ssd_norm_g,
           mem_norm_g, w_mem_k, w_mem_v, w_out, final_norm_g):
    f = lambda a: np.ascontiguousarray(np.asarray(a, dtype=np.float32))
    x_prompt, x_sample, mem_prompt = f(x_prompt), f(x_sample), f(mem_prompt)
    state_pool, state_conv, state_ssm = f(state_pool), f(state_conv), f(state_ssm)
    cache_mem_k, cache_mem_v = f(cache_mem_k), f(cache_mem_v)
    cf, cb = _constants()
    colpar = np.zeros((128, DEPTH, NCP), np.float32)
    for l in range(DEPTH):
        colpar[:, l, CP_NG:CP_NG + 8] = f(norm_g)[l].reshape(8, 128).T
        colpar[:, l, CP_MG:CP_MG + 8] = f(mem_norm_g)[l].reshape(8, 128).T
        colpar[:, l, CP_SG:CP_SG + 8] = f(ssd_norm_g)[l].reshape(8, 128).T
        colpar[:, l, CP_PS:CP_PS + 4] = f(pool_scale)[l].reshape(4, 128).T
        colpar[:, l, CP_CW:CP_CW + 48] = f(conv_w)[l].reshape(4, 12, 128).transpose(2, 1, 0).reshape(128, 48)
        colpar[:, l, CP_CB:CP_CB + 12] = f(conv_b)[l].reshape(12, 128).T
    colpar = np.ascontiguousarray(colpar.reshape(128, DEPTH * NCP))
    rowpar = np.ascontiguousarray(np.concatenate([f(dt_bias), f(a_log), f(d_skip)], axis=1).reshape(-1))
    shared = {
        "w_in": f(w_in), "w_out": f(w_out), "pool_w": f(pool_w), "w_mk": f(w_mem_k), "w_mv": f(w_mem_v),
        "colpar": colpar, "rowpar": rowpar, "fng": f(final_norm_g), "cstf": cf, "cstb": cb,
    }
    in_maps = []
    for c in range(NCORES):
        sl = slice(c * NS, (c + 1) * NS)
        m = dict(shared)
        m["xp"] = x_prompt[c]
        m["xs"] = np.ascontiguousarray(x_sample[sl].reshape(128, D))
        m["mem"] = mem_prompt[c]
        m["st_pool"] = np.ascontiguousarray(state_pool[:, sl].reshape(DEPTH, NS * 15, 512))
        m["st_conv"] = np.ascontiguousarray(state_conv[:, sl].reshape(DEPTH, NS * 3, 1536))
        m["st_ssm"] = np.ascontiguousarray(state_ssm[:, sl].reshape(DEPTH, NS, 1024, 128))
        m["ck"] = np.ascontiguousarray(cache_mem_k[:, sl].reshape(DEPTH, NS, 256, 512))
        m["cv"] = np.ascontiguousarray(cache_mem_v[:, sl].reshape(DEPTH, NS, 256, 512))
        in_maps.append(m)
    if "nc" not in _NC_CACHE:
        _NC_CACHE["nc"] = build_nc(_DBG_CFG)
    nc = _NC_CACHE["nc"]
    res = run_bass_kernel_spmd(nc, in_maps, core_ids=list(range(NCORES)))
    R = res.results
    g = lambda name, c: np.asarray(R[c][name], dtype=np.float32)
    y_prompt = np.stack([g("yp", c) for c in range(NCORES)]).reshape(8, 2048, D)
    y_sample = np.concatenate([g("ys", c).reshape(NS, 8, D) for c in range(NCORES)], axis=0)
    new_pool_p = np.stack([g("o_pool_p", c) for c in range(NCORES)], axis=1)
    new_conv_p = np.stack([g("o_conv_p", c) for c in range(NCORES)], axis=1)
    new_ssm_p = np.stack([g("o_ssm_p", c).reshape(DEPTH, 16, 64, 128) for c in range(NCORES)], axis=1)
    new_mk = np.stack([g("o_mk", c).reshape(DEPTH, 256, 4, 128) for c in range(NCORES)], axis=1)
    new_mv = np.stack([g("o_mv", c).reshape(DEPTH, 256, 4, 128) for c in range(NCORES)], axis=1)
    new_pool_s = np.concatenate([g("o_pool_s", c) for c in range(NCORES)], axis=1)
    new_conv_s = np.concatenate([g("o_conv_s", c) for c in range(NCORES)], axis=1)
    new_ssm_s = np.concatenate([g("o_ssm_s", c).reshape(DEPTH, NS, 16, 64, 128) for c in range(NCORES)], axis=1)
    return (y_prompt, y_sample, new_pool_p, new_conv_p, new_ssm_p, new_mk, new_mv, new_pool_s, new_conv_s, new_ssm_s)
```

```python
import contextlib
import math
import numpy as np
import concourse.bass as bass
import concourse.mybir as mybir
from concourse.bass_utils import run_bass_kernel_spmd

F32 = mybir.dt.float32
BF16 = mybir.dt.bfloat16
AF = mybir.ActivationFunctionType
ALU = mybir.AluOpType

NCORES = 8
DEPTH = 4
D = 1024
NT = 16
NS = 16
DIN = 4624
C_U, C_GP, C_Z, C_XBC, C_DT, C_Q, C_GX = 0, 512, 1024, 2048, 3584, 3600, 4112
EPS = 1e-6
POOL_W = (2, 4, 8, 16)
SEM_MAX = 3600

CP_NG, CP_MG, CP_SG, CP_PS, CP_CW, CP_CB, NCP = 0, 8, 16, 24, 28, 76, 88
CF_ID, CF_TRI, CF_TRIS, CF_SU, CF_ONE, CF_BONE, CF_BM, CF_IC0, NCF = 0, 128, 256, 384, 512, 640, 768, 784, 1296
CB_ID, CB_ONE, CB_BAND, NCB = 0, 128, 256, 256 + 24 * 128


class Buf:
    __slots__ = ("name", "t", "writes", "reads", "old", "dsem", "dval", "partial")

    def __init__(self, name, t, partial=False):
        self.name = name
        self.t = t
        self.writes = {}
        self.reads = {}
        self.old = {}
        self.dsem = None
        self.dval = 0
        self.partial = partial

    def __getitem__(self, k):
        return self.t[k]


def _merge(d, ev):
    for k, v in ev.items():
        if d.get(k, 0) < v:
            d[k] = v


class KB:
    def __init__(self, nc, stack):
        self.nc = nc
        self.stack = stack
        self.E = {"pe": nc.tensor, "act": nc.scalar, "dve": nc.vector, "pool": nc.gpsimd, "sp": nc.sync}
        self.sem, self.cnt, self.seen = {}, {}, {}
        for e in self.E:
            self.sem[e] = stack.enter_context(nc.semaphore("s_" + e))
            self.cnt[e] = 0
            self.seen[e] = {}
        self.nbuf = 0
        self.rr = {}
        self.psfree = []
        self.maxwaited = {}
        self.dmafinal = {}
        self.nops = 0
        self.limit = 1 << 60

    def sb(self, name, shape, dtype):
        self.nbuf += 1
        t = self.stack.enter_context(self.nc.sbuf_tensor(f"{name}_{self.nbuf}", list(shape), dtype))
        return Buf(name, t)

    def ps(self, name, shape, dtype=F32):
        self.nbuf += 1
        t = self.stack.enter_context(self.nc.psum_tensor(f"{name}_{self.nbuf}", list(shape), dtype))
        return Buf(name, t)

    def dram(self, name, shape, dtype, kind):
        t = self.nc.dram_tensor(name, list(shape), dtype, kind=kind)
        return Buf(name, t, partial=True)

    def pool(self, name, shape, dtype, n):
        bufs = [self.sb(f"{name}{i}", shape, dtype) for i in range(n)]
        self.rr[name] = [bufs, 0]
        return bufs

    def nxt(self, name):
        r = self.rr[name]
        b = r[0][r[1] % len(r[0])]
        r[1] += 1
        return b

    def psalloc(self):
        assert self.psfree, "out of PSUM banks"
        return self.psfree.pop(0)

    def psrel(self, *bs):
        for b in bs:
            self.psfree.append(b)

    def _isfirst(self, b, first):
        if b.partial:
            return False
        return first is None or b in first

    def _deps(self, reads, writes, first):
        dep = {}
        for b in reads:
            _merge(dep, b.writes)
        for b in writes:
            if self._isfirst(b, first):
                _merge(dep, b.writes)
                _merge(dep, b.reads)
            else:
                _merge(dep, b.old)
        return dep

    def _wait(self, e, dep):
        eng = self.E[e]
        seen = self.seen[e]
        for sem, val in dep.items():
            if seen.get(sem, 0) < val:
                eng.wait_ge(sem, val)
                seen[sem] = val
            if self.maxwaited.get(sem, 0) < val:
                self.maxwaited[sem] = val

    def _commit(self, ev, reads, writes, first):
        for b in writes:
            if self._isfirst(b, first):
                old = {}
                _merge(old, b.writes)
                _merge(old, b.reads)
                b.old = old
                b.writes = dict(ev)
                b.reads = {}
            else:
                _merge(b.writes, ev)
        for b in reads:
            if b in writes:
                continue
            _merge(b.reads, ev)

    def op(self, e, fn, reads=(), writes=(), first=None):
        self.nops += 1
        if self.nops > self.limit:
            return
        dep = self._deps(reads, writes, first)
        self._wait(e, dep)
        ins = fn(self.E[e])
        self.cnt[e] += 1
        ins.then_inc(self.sem[e], 1)
        self._commit({self.sem[e]: self.cnt[e]}, reads, writes, first)
        if self.cnt[e] >= SEM_MAX:
            self.nbuf += 1
            self.sem[e] = self.stack.enter_context(self.nc.semaphore(f"s_{e}_{self.nbuf}"))
            self.cnt[e] = 0

    def dma(self, q, out_ap, in_ap, reads=(), writes=(), first=None, sembuf=None, **kw):
        self.nops += 1
        if self.nops > self.limit:
            return
        dep = self._deps(reads, writes, first)
        b = sembuf
        if b.dsem is None or b.dval + 16 > SEM_MAX:
            self.nbuf += 1
            b.dsem = self.stack.enter_context(self.nc.semaphore(f"d_{b.name}_{self.nbuf}"))
            b.dval = 0
        if b.dsem is not None and self.maxwaited.get(b.dsem, 0) > 0:
            _merge(dep, {b.dsem: self.maxwaited[b.dsem]})
        self._wait(q, dep)
        ins = self.E[q].dma_start(out=out_ap, in_=in_ap, **kw)
        b.dval += 16
        ins.then_inc(b.dsem, 16)
        self.dmafinal[b.dsem] = b.dval
        self._commit({b.dsem: b.dval}, reads, writes, first)

    def finish(self, outs, e="sp"):
        dep = {}
        for b in outs:
            _merge(dep, b.writes)
        _merge(dep, self.dmafinal)
        self._wait(e, dep)


def A(buf, off, dims, p0=0, np_=128):
    row = int(np.prod(buf.t.shape[1:]))
    return bass.AP(buf.t, p0 * row + off, [[row, np_]] + [list(d) for d in dims])


def DA(buf, off, dims):
    return bass.AP(buf.t, off, [list(d) for d in dims])


def build_nc(cfg=None):
    cfg = dict(mem=1, layers=DEPTH, tiles=NT, sample=1) if cfg is None else cfg
    nc = bass.Bass("TRN2", target_bir_lowering=False)
    with contextlib.ExitStack() as st:
        kb = KB(nc, st)
        kb.limit = cfg.get('stop', 1 << 60)
        I, O = "ExternalInput", "ExternalOutput"
        xp = kb.dram("xp", [2048, D], F32, I)
        xs = kb.dram("xs", [128, D], F32, I)
        mem = kb.dram("mem", [256, D], F32, I)
        st_pool = kb.dram("st_pool", [DEPTH, NS * 15, 512], F32, I)
        st_conv = kb.dram("st_conv", [DEPTH, NS * 3, 1536], F32, I)
        st_ssm = kb.dram("st_ssm", [DEPTH, NS, 1024, 128], F32, I)
        ck = kb.dram("ck", [DEPTH, NS, 256, 512], F32, I)
        cv = kb.dram("cv", [DEPTH, NS, 256, 512], F32, I)
        w_in = kb.dram("w_in", [DEPTH, D, DIN], F32, I)
        w_out = kb.dram("w_out", [DEPTH, 2048, D], F32, I)
        pool_w = kb.dram("pool_w", [DEPTH, 4, 128, 128], F32, I)
        w_mk = kb.dram("w_mk", [DEPTH, D, 512], F32, I)
        w_mv = kb.dram("w_mv", [DEPTH, D, 512], F32, I)
        colpar_d = kb.dram("colpar", [128, DEPTH * NCP], F32, I)
        rowpar_d = kb.dram("rowpar", [DEPTH * 48], F32, I)
        fng_d = kb.dram("fng", [D], F32, I)
        cstf_d = kb.dram("cstf", [128, NCF], F32, I)
        cstb_d = kb.dram("cstb", [128, NCB], F32, I)

        yp = kb.dram("yp", [2048, D], F32, O)
        ys = kb.dram("ys", [128, D], F32, O)
        o_pool_p = kb.dram("o_pool_p", [DEPTH, 15, 512], F32, O)
        o_conv_p = kb.dram("o_conv_p", [DEPTH, 3, 1536], F32, O)
        o_ssm_p = kb.dram("o_ssm_p", [DEPTH, 1024, 128], F32, O)
        o_mk = kb.dram("o_mk", [DEPTH, 256, 512], F32, O)
        o_mv = kb.dram("o_mv", [DEPTH, 256, 512], F32, O)
        o_pool_s = kb.dram("o_pool_s", [DEPTH, NS, 15, 512], F32, O)
        o_conv_s = kb.dram("o_conv_s", [DEPTH, NS, 3, 1536], F32, O)
        o_ssm_s = kb.dram("o_ssm_s", [DEPTH, NS, 1024, 128], F32, O)
        hs = kb.dram("hs", [17 * 128, D], F32, "Internal")
        ktscr = kb.dram("ktscr", [DEPTH, 128, 1024], BF16, "Internal")
        hsB = [Buf(f"hs{i}", hs.t, partial=True) for i in range(17)]
        outs = [yp, ys, o_pool_p, o_conv_p, o_ssm_p, o_mk, o_mv, o_pool_s, o_conv_s, o_ssm_s]

        pad0 = kb.sb("pad0", [128, 16], F32)
        win = kb.sb("win", [128, 8, DIN], BF16)
        wout = kb.sb("wout", [128, 16, D], BF16)
        poolw = kb.sb("poolw", [128, 4, 128], BF16)
        ktl = kb.sb("ktl", [128, 4, 256], BF16)
        vl = kb.sb("vl", [128, 2, 512], BF16)
        fng = kb.sb("fng", [128, D], F32)
        colpar = kb.sb("colpar", [128, DEPTH * NCP], F32)
        rowpar = kb.sb("rowpar", [128, DEPTH * 48], F32)
        abc = kb.sb("abc", [128, DEPTH * 16], F32)
        cf = kb.sb("cf", [128, NCF], F32)
        cb = kb.sb("cb", [128, NCB], BF16)

        kb.pool("hbuf", [128, D], F32, 2)
        xn = kb.sb("xn", [128, D], BF16)
        xnT = kb.sb("xnT", [128, 8, 128], BF16)
        kb.pool("ubf", [128, 512], BF16, 2)
        sgpT = kb.sb("sgpT", [128, 4, 128], BF16)
        dT = kb.sb("dT", [128, 4, 128], BF16)
        qT = kb.sb("qT", [128, 4, 128], BF16)
        sgxT = kb.sb("sgxT", [128, 4, 128], BF16)
        sz = kb.sb("sz", [128, D], BF16)
        xb = kb.sb("xb", [128, 12 * 176], F32)
        carry = kb.sb("carry", [128, 12, 3], F32)
        big6 = kb.sb("big6", [128, 12 * 128], F32)
        big6b = Buf("big6b", big6.t)
        xbcsT = kb.sb("xbcsT", [128, 12, 128], BF16)
        xdt = kb.sb("xdt", [128, D], BF16)
        xd = kb.sb("xd", [128, D], BF16)
        xw = kb.sb("xw", [128, D], BF16)
        btok = kb.sb("btok", [128, 256], BF16)
        sm = kb.sb("sm", [128, 6 * 16], F32)
        sm2 = kb.sb("sm2", [128, 16], F32)
        Rb = kb.sb("Rb", [128, 4, 128], F32)
        kb.pool("E", [128, 4, 128], BF16, 2)
        kb.pool("MT", [128, 4, 128], BF16, 2)
        cbm = kb.sb("cbm", [128, 2, 128], BF16)
        Sb = kb.pool("Sb", [128, D], F32, 2)
        stb16 = kb.sb("stb16", [128, D], BF16)
        actT = kb.sb("actT", [128, 16, 128], BF16)
        scr = kb.sb("scr", [128, D], F32)
        kb.pool("Kb", [128, 2, 512], BF16, 2)
        kb.pool("Vb", [128, 2, 512], BF16, 2)
        kb.pool("Bs", [128, 256], BF16, 1)
        spool16 = kb.sb("spool16", [128, 2, 512], BF16)
        decT = kb.sb("decT", [128, 8, 16], F32)
        pts = kb.sb("pts", [128, 64], BF16)
        ctmp = kb.sb("ctmp", [128, 128], F32)
        print("sbuf bytes remaining after alloc:", nc.sbuf_bytes_remaining)

        for i in range(8):
            kb.psfree.append(kb.ps(f"bank{i}", [128, 512], F32))

        def smt(i):
            return sm[:, i * 16:(i + 1) * 16]

        kb.dma("sp", cf[:, :], cstf_d[:, :], writes=[cf], sembuf=cf)
        kb.dma("pool", cb[:, :], cstb_d[:, :], writes=[cb], sembuf=cb)
        kb.dma("sp", colpar[:, :], colpar_d[:, :], writes=[colpar], sembuf=colpar)
        kb.dma("sp", rowpar[:, :], DA(rowpar_d, 0, [[0, 128], [1, DEPTH * 48]]), writes=[rowpar], sembuf=rowpar)
        kb.dma("sp", fng[:, :], DA(fng_d, 0, [[0, 128], [1, D]]), writes=[fng], sembuf=fng)
        for l in range(DEPTH):
            kb.op("act", lambda e, l=l: e.activation(out=abc[:, l * 16:(l + 1) * 16],
                                                      in_=rowpar[:, l * 48 + 16:l * 48 + 32], func=AF.Exp),
                  reads=[rowpar], writes=[abc])
        kb.op("dve", lambda e: e.tensor_scalar_mul(out=abc[:, :], in0=abc[:, :], scalar1=-1.0),
              reads=[abc], writes=[abc])

        identf = cf[:, CF_ID:CF_ID + 128]
        identb = cb[:, CB_ID:CB_ID + 128]
        onesb = cb[:, CB_ONE:CB_ONE + 128]

        def band(g, k):
            o = CB_BAND + (g * 6 + k) * 128
            return cb[:, o:o + 128]

        def cpcol(l, off, n=1):
            return colpar[:, l * NCP + off:l * NCP + off + n]

        def rmsnorm_T(hb, gcol_off, l):
            ss = sm2[:, 0:1]
            rs = sm2[:, 1:2]
            kb.op("act", lambda e: e.activation(out=xn[:, :], in_=hb[:, :], func=AF.Square, accum_out=ss),
                  reads=[hb], writes=[xn, sm2])
            kb.op("act", lambda e: e.activation(out=rs, in_=ss, func=AF.Ln, scale=1.0 / D, bias=EPS),
                  reads=[sm2], writes=[sm2])
            kb.op("act", lambda e: e.activation(out=rs, in_=rs, func=AF.Exp, scale=-0.5), reads=[sm2], writes=[sm2])
            kb.op("dve", lambda e: e.tensor_scalar_mul(out=xn[:, :], in0=hb[:, :], scalar1=rs),
                  reads=[hb, sm2], writes=[xn])
            pt = kb.psalloc()
            ptb = pt[:, :].bitcast(BF16)

            def tr(e):
                ins = None
                for k in range(8):
                    ins = e.transpose(out=ptb[:, k * 128:(k + 1) * 128], in_=xn[:, k * 128:(k + 1) * 128], identity=identb)
                return ins
            kb.op("pe", tr, reads=[xn, cb], writes=[pt])
            kb.op("dve", lambda e: e.tensor_tensor(out=xnT[:, :, :], in0=ptb[:, :].rearrange("p (k t) -> p k t", k=8),
                                                   in1=A(colpar, l * NCP + gcol_off, [[1, 8], [0, 128]]), op=ALU.mult),
                  reads=[pt, colpar], writes=[xnT])
            kb.psrel(pt)

        def mm_tok(pbank, ncols, wbuf, wap_fn):
            def f(e):
                ins = None
                for k in range(8):
                    ins = e.matmul(pbank[:, 0:ncols], lhsT=xnT[:, k, :], rhs=wap_fn(k), start=(k == 0), stop=(k == 7))
                return ins
            kb.op("pe", f, reads=[xnT, wbuf], writes=[pbank])

        def mm_feat(pbank, nchunk, wbuf, wap_fn):
            def f(e):
                ins = None
                for c in range(nchunk):
                    for k in range(8):
                        ins = e.matmul(pbank[:, c * 128:(c + 1) * 128], lhsT=wap_fn(k, c), rhs=xnT[:, k, :],
                                       start=(k == 0), stop=(k == 7))
                return ins
            kb.op("pe", f, reads=[xnT, wbuf], writes=[pbank])

        for l in range(DEPTH if cfg['mem'] else 0):
            s = l % 2
            for k in range(8):
                kb.dma("pool", wout[:, 8 * s + k, 0:512], w_mk[l, k * 128:(k + 1) * 128, :], writes=[wout],
                       first=None if (k == 0 and s == 0) else [], sembuf=wout)
                kb.dma("pool", wout[:, 8 * s + k, 512:1024], w_mv[l, k * 128:(k + 1) * 128, :], writes=[wout],
                       first=[], sembuf=wout)
            for mt in range(2):
                hb = kb.nxt("hbuf")
                kb.dma("sp", hb[:, :], mem[mt * 128:(mt + 1) * 128, :], writes=[hb], sembuf=hb)
                rmsnorm_T(hb, CP_MG, l)
                pk = kb.psalloc()
                mm_tok(pk, 512, wout, lambda k: wout[:, 8 * s + k, 0:512])
                kb.op("act", lambda e: e.copy(out=scr[:, 0:512], in_=pk[:, :]), reads=[pk], writes=[scr])
                kb.psrel(pk)
                pv = kb.psalloc()
                mm_tok(pv, 512, wout, lambda k: wout[:, 8 * s + k, 512:1024])
                kb.op("dve", lambda e: e.tensor_copy(out=scr[:, 512:1024], in_=pv[:, :]), reads=[pv], writes=[scr], first=[])
                kb.psrel(pv)
                kb.dma("sp", o_mk[l, mt * 128:(mt + 1) * 128, :], scr[:, 0:512], reads=[scr], writes=[o_mk], sembuf=scr)
                kb.dma("sp", o_mv[l, mt * 128:(mt + 1) * 128, :], scr[:, 512:1024], reads=[scr], writes=[o_mv], sembuf=scr)
                pkt = kb.psalloc()
                mm_feat(pkt, 4, wout, lambda k, c: wout[:, 8 * s + k, c * 128:(c + 1) * 128])
                kb.op("act", lambda e: e.copy(out=ktl[:, :, mt * 128:(mt + 1) * 128],
                                              in_=pkt[:, :].rearrange("p (h m) -> p h m", h=4)),
                      reads=[pkt], writes=[ktl], first=None if mt == 0 else [])
                kb.psrel(pkt)
            kb.dma("sp", ktscr[l, :, :], ktl[:, :, :].rearrange("p h m -> p (h m)"), reads=[ktl], writes=[ktscr], sembuf=ktl)

        def load_win(l):
            for k in range(8):
                kb.dma("pool", win[:, k, :], w_in[l, k * 128:(k + 1) * 128, :], writes=[win],
                       first=None if k == 0 else [], sembuf=win)

        def load_poolw(l):
            kb.dma("pool", poolw[:, :, :], pool_w[l, :, :, :].rearrange("g c d -> c g d"), writes=[poolw], sembuf=poolw)

        def load_attn(l):
            kb.dma("sp", ktl[:, :, :].rearrange("p h m -> p (h m)"), ktscr[l, :, :], reads=[ktscr], writes=[ktl], sembuf=ktl)
            kb.dma("pool", vl[:, :, :], o_mv[l, :, :].rearrange("(c p) n -> p c n", p=128), reads=[o_mv], writes=[vl], sembuf=vl)

        def load_wout(l):
            for k in range(16):
                kb.dma("pool", wout[:, k, :], w_out[l, k * 128:(k + 1) * 128, :], writes=[wout],
                       first=None if k == 0 else [], sembuf=wout)

        SCALE = 1.0 / math.sqrt(128.0)

        XB = [xb, Buf("xb1", xb.t), Buf("xb2", xb.t)]
        BIG = [big6, Buf("big1", big6.t), big6b]
        XS = [xbcsT, Buf("xs1", xbcsT.t), Buf("xs2", xbcsT.t)]
        sm2n = Buf("sm2n", sm2.t)

        def norm_stage(l, ti, sample):
            hidx = 16 if sample else ti
            hb = kb.nxt("hbuf")
            if l == 0:
                src = xs[:, :] if sample else xp[ti * 128:(ti + 1) * 128, :]
                kb.dma("sp", hb[:, :], src, writes=[hb], sembuf=hb)
            else:
                kb.dma("sp", hb[:, :], hs[hidx * 128:(hidx + 1) * 128, :], reads=[hsB[hidx]], writes=[hb], sembuf=hb)
            rmsnorm_T(hb, CP_NG, l)
            return hb

        def do_tile(l, ti, sample, nxt_tile):
            last_p = (not sample) and ti == cfg['tiles'] - 1
            LL = cfg['layers']
            special = sample or last_p
            hidx = 16 if sample else ti
            if tile_state.get("pre") is not None:
                hb = tile_state.pop("pre")
            else:
                hb = norm_stage(l, ti, sample)
            rp = l * 48
            afirst = [None]

            def actT_first():
                f = afirst[0]
                afirst[0] = []
                return f

            if sample:
                for half in range(2):
                    kb.dma("sp", scr[0:48, 0:768], st_conv[l, :, half * 768:(half + 1) * 768], writes=[scr], sembuf=scr)
                    pc = kb.psalloc()

                    def trc(e, pc=pc):
                        ins = None
                        for c in range(6):
                            ins = e.transpose(out=pc[:, c * 48:(c + 1) * 48], in_=scr[0:48, c * 128:(c + 1) * 128],
                                              identity=cf[0:48, CF_ID:CF_ID + 48])
                        return ins
                    kb.op("pe", trc, reads=[scr, cf], writes=[pc])
                    wr = [XB[0], XB[1]] if half == 0 else [XB[1], XB[2]]
                    kb.op("act", lambda e, pc=pc, half=half: e.copy(
                        out=A(xb, half * 6 * 176, [[176, 6], [11, 16], [1, 3]]),
                        in_=A(pc, 0, [[48, 6], [3, 16], [1, 3]])), reads=[pc], writes=wr,
                        first=None if half == 0 else [XB[2]])
                    kb.psrel(pc)
            else:
                if ti == 0:
                    kb.op("pool", lambda e: e.memset(carry[:, :, :], 0.0), writes=[carry])
                kb.op("pool", lambda e: e.tensor_copy(out=A(xb, 0, [[176, 12], [1, 3]]), in_=carry[:, :, :]),
                      reads=[carry], writes=XB)
            for b3 in range(3):
                pxb = kb.psalloc()
                mm_feat(pxb, 4, win, lambda k, c, b3=b3: win[:, k, C_XBC + (b3 * 4 + c) * 128:C_XBC + (b3 * 4 + c + 1) * 128])
                if sample:
                    o_ap = A(xb, b3 * 4 * 176 + 3, [[176, 4], [11, 16], [1, 8]])
                    i_ap = A(pxb, 0, [[128, 4], [8, 16], [1, 8]])
                else:
                    o_ap = A(xb, b3 * 4 * 176 + 3, [[176, 4], [1, 128]])
                    i_ap = A(pxb, 0, [[128, 4], [1, 128]])
                kb.op("act", lambda e, o_ap=o_ap, i_ap=i_ap: e.copy(out=o_ap, in_=i_ap), reads=[pxb], writes=[XB[b3]], first=[])
                kb.psrel(pxb)
            if not sample:
                kb.op("pool", lambda e: e.tensor_copy(out=carry[:, :, :], in_=A(xb, 128, [[176, 12], [1, 3]])),
                      reads=XB, writes=[carry])
            pq = kb.psalloc()
            mm_feat(pq, 4, win, lambda k, c: win[:, k, C_Q + c * 128:C_Q + (c + 1) * 128])
            kb.op("act", lambda e: e.copy(out=qT[:, :, :].rearrange("p a b -> p (a b)"), in_=pq[:, :]),
                  reads=[pq], writes=[qT])
            kb.psrel(pq)
            pgx = kb.psalloc()
            mm_feat(pgx, 4, win, lambda k, c: win[:, k, C_GX + c * 128:C_GX + (c + 1) * 128])
            kb.op("act", lambda e: e.activation(out=sgxT[:, :, :].rearrange("p a b -> p (a b)"), in_=pgx[:, :], func=AF.Silu),
                  reads=[pgx], writes=[sgxT])
            kb.psrel(pgx)

            if not sample:
                psc = [kb.psalloc(), kb.psalloc()]
                for hp in range(2):
                    def scmm(e, hp=hp):
                        ins = None
                        for hh in range(2):
                            h = hp * 2 + hh
                            for mc in range(2):
                                ins = e.matmul(psc[hp][:, (hh * 2 + mc) * 128:(hh * 2 + mc + 1) * 128],
                                               lhsT=ktl[:, h, mc * 128:(mc + 1) * 128], rhs=qT[:, h, :], start=True, stop=True)
                        return ins
                    kb.op("pe", scmm, reads=[ktl, qT], writes=[psc[hp]])
                PT = [kb.nxt("E"), kb.nxt("E")]
                for hp in range(2):
                    kb.op("act", lambda e, hp=hp: e.activation(out=PT[hp][:, :, :].rearrange("p a b -> p (a b)"),
                                                               in_=psc[hp][:, :], func=AF.Exp, scale=SCALE),
                          reads=[psc[hp]], writes=[PT[hp]])
                kb.psrel(psc[0], psc[1])
                psum_ = kb.psalloc()
                pxa = kb.psalloc()

                def summm(e):
                    ins = None
                    for h in range(4):
                        for mc in range(2):
                            ins = e.matmul(psum_[:, h * 128:(h + 1) * 128], lhsT=onesb, rhs=PT[h // 2][:, (h % 2) * 2 + mc, :],
                                           start=(mc == 0), stop=(mc == 1))
                    return ins
                kb.op("pe", summm, reads=[cb, PT[0], PT[1]], writes=[psum_])

                def pvmm(e):
                    ins = None
                    for h in range(4):
                        for mc in range(2):
                            ins = e.matmul(pxa[:, h * 128:(h + 1) * 128], lhsT=vl[:, mc, h * 128:(h + 1) * 128],
                                           rhs=PT[h // 2][:, (h % 2) * 2 + mc, :], start=(mc == 0), stop=(mc == 1))
                    return ins
                kb.op("pe", pvmm, reads=[vl, PT[0], PT[1]], writes=[pxa])

            for ch in range(12):
                gi = ch // 4
                if sample:
                    def xin(k, ch=ch):
                        return A(xb, ch * 176 + k, [[11, 16], [1, 8]])
                    tout = A(big6, ch * 128, [[8, 16], [1, 8]])
                else:
                    def xin(k, ch=ch):
                        return A(xb, ch * 176 + k, [[1, 128]])
                    tout = A(big6, ch * 128, [[1, 128]])
                if ch < 8:
                    kb.op("dve", lambda e, xin=xin, tout=tout, ch=ch: e.tensor_scalar_mul(
                        out=tout, in0=xin(0), scalar1=cpcol(l, CP_CW + ch * 4 + 0)),
                        reads=[XB[gi], colpar], writes=[BIG[gi]], first=None if ch % 4 == 0 else [])
                    for k in range(1, 4):
                        kb.op("dve", lambda e, xin=xin, tout=tout, ch=ch, k=k: e.scalar_tensor_tensor(
                            out=tout, in0=xin(k), scalar=cpcol(l, CP_CW + ch * 4 + k), in1=tout, op0=ALU.mult, op1=ALU.add),
                            reads=[XB[gi], colpar, BIG[gi]], writes=[BIG[gi]], first=[])
                else:
                    bdims = [[0, 16], [0, 8]] if sample else [[0, 128]]
                    t2 = A(ctmp, 0, [[8, 16], [1, 8]]) if sample else ctmp[:, :]

                    def wbc(k, ch=ch, bdims=bdims):
                        return A(colpar, l * NCP + CP_CW + ch * 4 + k, bdims)
                    kb.op("pool", lambda e, xin=xin, tout=tout, wbc=wbc: e.tensor_tensor(out=tout, in0=xin(0), in1=wbc(0), op=ALU.mult),
                          reads=[XB[gi], colpar], writes=[BIG[gi]], first=None if ch % 4 == 0 else [])
                    for k in range(1, 4):
                        kb.op("pool", lambda e, xin=xin, t2=t2, wbc=wbc, k=k: e.tensor_tensor(out=t2, in0=xin(k), in1=wbc(k), op=ALU.mult),
                              reads=[XB[gi], colpar], writes=[ctmp])
                        kb.op("pool", lambda e, tout=tout, t2=t2: e.tensor_tensor(out=tout, in0=tout, in1=t2, op=ALU.add),
                              reads=[BIG[gi], ctmp], writes=[BIG[gi]], first=[])
                kb.op("act", lambda e, ch=ch: e.activation(out=xbcsT[:, ch, :], in_=big6[:, ch * 128:(ch + 1) * 128], func=AF.Silu,
                                                           bias=cpcol(l, CP_CB + ch)),
                      reads=[BIG[gi], colpar], writes=[XS[gi]], first=None if ch % 4 == 0 else [])

            def attn_tail(psum_, pxa):
                if sample:
                    kb.op("dve", lambda e: e.reciprocal(out=A(Rb, 0, [[128, 4], [8, 16], [1, 8]]),
                                                        in_=A(psum_, 0, [[8, 4], [32, 16], [1, 8]])), reads=[psum_], writes=[Rb])
                else:
                    kb.op("dve", lambda e: e.reciprocal(out=Rb[:, :, :].rearrange("p a b -> p (a b)"), in_=psum_[:, :]),
                          reads=[psum_], writes=[Rb])
                kb.op("dve", lambda e: e.tensor_tensor(out=Rb[:, :, :].rearrange("p a b -> p (a b)"),
                                                       in0=pxa[:, :], in1=Rb[:, :, :].rearrange("p a b -> p (a b)"), op=ALU.mult),
                      reads=[pxa, Rb], writes=[Rb])
                kb.psrel(psum_, pxa)
                kb.op("pool", lambda e: e.tensor_tensor(out=actT[:, 12:16, :], in0=Rb[:, :, :], in1=sgxT[:, :, :], op=ALU.mult),
                      reads=[Rb, sgxT], writes=[actT], first=actT_first())
            if not sample:
                attn_tail(psum_, pxa)

            pu = kb.psalloc()
            mm_tok(pu, 512, win, lambda k: win[:, k, C_U:C_U + 512])
            ub = kb.nxt("ubf")
            kb.op("act", lambda e: e.copy(out=ub[:, :], in_=pu[:, :]), reads=[pu], writes=[ub])
            if special:
                kb.op("act", lambda e: e.copy(out=scr[:, 0:512], in_=pu[:, :]), reads=[pu], writes=[scr])
                if sample:
                    for s_ in range(NS):
                        kb.dma("sp", o_pool_s[l, s_, 7:15, :], scr[s_ * 8:(s_ + 1) * 8, 0:512], reads=[scr],
                               writes=[o_pool_s], sembuf=scr)
                else:
                    kb.dma("sp", o_pool_p[l, :, :], scr[113:128, 0:512], reads=[scr], writes=[o_pool_p], sembuf=scr)
            kb.psrel(pu)
            pg = kb.psalloc()
            mm_feat(pg, 4, win, lambda k, c: win[:, k, C_GP + c * 128:C_GP + (c + 1) * 128])
            kb.op("act", lambda e: e.activation(out=sgpT[:, :, :].rearrange("p a b -> p (a b)"), in_=pg[:, :], func=AF.Silu),
                  reads=[pg], writes=[sgpT])
            kb.psrel(pg)
            for half in range(2):
                pz = kb.psalloc()
                mm_tok(pz, 512, win, lambda k, half=half: win[:, k, C_Z + half * 512:C_Z + (half + 1) * 512])
                kb.op("act", lambda e, half=half, pz=pz: e.activation(out=sz[:, half * 512:(half + 1) * 512], in_=pz[:, :],
                                                                      func=AF.Silu),
                      reads=[pz], writes=[sz], first=None if half == 0 else [])
                kb.psrel(pz)
            pd = kb.psalloc()
            mm_tok(pd, 16, win, lambda k: win[:, k, C_DT:C_DT + 16])
            dt = smt(0)
            kb.op("dve", lambda e: e.tensor_tensor(out=dt, in0=pd[:, 0:16], in1=rowpar[:, rp:rp + 16], op=ALU.add),
                  reads=[pd, rowpar], writes=[sm])
            kb.psrel(pd)
            kb.op("act", lambda e: e.activation(out=dt, in_=dt, func=AF.Exp), reads=[sm], writes=[sm])
            kb.op("act", lambda e: e.activation(out=dt, in_=dt, func=AF.Ln, bias=1.0), reads=[sm], writes=[sm])
            if special:
                for part in range(3):
                    px = kb.psalloc()
                    mm_tok(px, 512, win, lambda k, part=part: win[:, k, C_XBC + part * 512:C_XBC + (part + 1) * 512])
                    kb.op("act", lambda e, px=px: e.copy(out=scr[:, 512:1024], in_=px[:, :]), reads=[px], writes=[scr])
                    kb.psrel(px)
                    if sample:
                        for s_ in range(NS):
                            kb.dma("sp", o_conv_s[l, s_, :, part * 512:(part + 1) * 512],
                                   scr[s_ * 8 + 5:s_ * 8 + 8, 512:1024], reads=[scr], writes=[o_conv_s], sembuf=scr)
                    else:
                        kb.dma("sp", o_conv_p[l, :, part * 512:(part + 1) * 512], scr[125:128, 512:1024], reads=[scr],
                               writes=[o_conv_p], sembuf=scr)
            if sample and l < LL - 1:
                load_win(l + 1)

            pdT = kb.psalloc()

            def poolmm(e):
                ins = None
                for g in range(4):
                    o = pdT[:, g * 128:(g + 1) * 128]
                    ul = ub[:, g * 128:(g + 1) * 128]
                    if sample:
                        e.matmul(o, lhsT=ul, rhs=band(g, 3), start=True, stop=False)
                        e.matmul(o, lhsT=spool16[0:120, 0, g * 128:(g + 1) * 128], rhs=band(g, 4)[0:120, :],
                                 start=False, stop=False)
                        ins = e.matmul(o, lhsT=spool16[0:120, 1, g * 128:(g + 1) * 128], rhs=band(g, 5)[0:120, :],
                                       start=False, stop=True)
                    elif ti == 0:
                        ins = e.matmul(o, lhsT=ul, rhs=band(g, 2), start=True, stop=True)
                    else:
                        e.matmul(o, lhsT=ul, rhs=band(g, 0), start=True, stop=False)
                        ins = e.matmul(o, lhsT=uprev[64:128, g * 128:(g + 1) * 128], rhs=band(g, 1)[64:128, :],
                                       start=False, stop=True)
                return ins
            if sample:
                kb.dma("pool", spool16[0:120, 0, :], st_pool[l, 0:120, :], writes=[spool16], sembuf=spool16)
                kb.dma("pool", spool16[0:120, 1, :], st_pool[l, 120:240, :], writes=[spool16], first=[], sembuf=spool16)
                kb.dma("sp", o_pool_s[l, :, 0:7, :], st_pool[l, :, :].rearrange("(s r) c -> s r c", r=15)[:, 8:15, :],
                       writes=[o_pool_s], sembuf=d2d)
                kb.op("pe", poolmm, reads=[ub, cb, spool16], writes=[pdT])
            elif ti == 0:
                kb.op("pe", poolmm, reads=[ub, cb], writes=[pdT])
            else:
                uprev = tile_state["uprev"]
                kb.op("pe", poolmm, reads=[ub, cb, uprev], writes=[pdT])
            tile_state["uprev"] = ub
            for g in range(4):
                if (not sample) and ti == 0:
                    kb.op("dve", lambda e, g=g: e.tensor_tensor(out=dT[:, g, :], in0=pdT[:, g * 128:(g + 1) * 128],
                                                                in1=cf[:, CF_IC0 + g * 128:CF_IC0 + (g + 1) * 128], op=ALU.mult),
                          reads=[pdT, cf], writes=[dT], first=None if g == 0 else [])
                else:
                    kb.op("act", lambda e, g=g: e.mul(out=dT[:, g, :], in_=pdT[:, g * 128:(g + 1) * 128], mul=1.0 / POOL_W[g]),
                          reads=[pdT], writes=[dT], first=None if g == 0 else [])
            kb.psrel(pdT)
            py = kb.psalloc()

            def poolmm2(e):
                ins = None
                for g in range(4):
                    ins = e.matmul(py[:, g * 128:(g + 1) * 128], lhsT=poolw[:, g, :], rhs=dT[:, g, :], start=True, stop=True)
                return ins
            kb.op("pe", poolmm2, reads=[poolw, dT], writes=[py])
            for g in range(4):
                kb.op("dve", lambda e, g=g: e.scalar_tensor_tensor(out=actT[:, g, :], in0=py[:, g * 128:(g + 1) * 128],
                                                                   scalar=cpcol(l, CP_PS + g), in1=sgpT[:, g, :],
                                                                   op0=ALU.mult, op1=ALU.mult),
                      reads=[py, colpar, sgpT], writes=[actT], first=actT_first())
            kb.psrel(py)
            if sample and l < LL - 1:
                load_poolw(l + 1)

            tri = cf[:, (CF_TRIS if sample else CF_TRI):(CF_TRIS if sample else CF_TRI) + 128]
            onem = cf[:, (CF_BONE if sample else CF_ONE):(CF_BONE if sample else CF_ONE) + 128]
            su = cf[:, CF_SU:CF_SU + 128]
            a_ = smt(1)
            cum = smt(2)
            e_ = smt(3)
            wend = smt(4)
            dec = smt(5)
            kb.op("dve", lambda e: e.tensor_tensor(out=a_, in0=dt, in1=abc[:, l * 16:(l + 1) * 16], op=ALU.mult),
                  reads=[sm, abc], writes=[sm], first=[])
            pcm = kb.psalloc()

            def cummm(e):
                e.matmul(pcm[:, 0:16], lhsT=tri, rhs=a_, start=True, stop=True)
                return e.matmul(pcm[:, 16:32], lhsT=onem, rhs=a_, start=True, stop=True)
            kb.op("pe", cummm, reads=[cf, sm], writes=[pcm])
            kb.op("act", lambda e: e.copy(out=cum, in_=pcm[:, 0:16]), reads=[pcm], writes=[sm], first=[])
            kb.op("act", lambda e: e.activation(out=e_, in_=pcm[:, 0:16], func=AF.Exp), reads=[pcm], writes=[sm], first=[])
            kb.op("act", lambda e: e.activation(out=dec, in_=pcm[:, 16:32], func=AF.Exp), reads=[pcm], writes=[sm], first=[])
            kb.op("dve", lambda e: e.tensor_tensor(out=wend, in0=pcm[:, 16:32], in1=cum, op=ALU.subtract),
                  reads=[pcm, sm], writes=[sm], first=[])
            kb.psrel(pcm)
            kb.op("act", lambda e: e.activation(out=wend, in_=wend, func=AF.Exp), reads=[sm], writes=[sm], first=[])
            kb.op("dve", lambda e: e.tensor_tensor(out=wend, in0=wend, in1=dt, op=ALU.mult), reads=[sm], writes=[sm], first=[])

            ptx = kb.psalloc()
            ptxb = ptx[:, :].bitcast(BF16)

            def trx(e):
                ins = None
                for c in range(8):
                    ins = e.transpose(out=ptxb[:, c * 128:(c + 1) * 128], in_=xbcsT[:, c, :], identity=identb)
                return ins
            kb.op("pe", trx, reads=[XS[0], XS[1], cb], writes=[ptx])
            ptx3 = ptxb.rearrange("p (h d) -> p h d", h=16)

            def bc16(off):
                return A(sm, off, [[1, 16], [0, 64]])
            kb.op("dve", lambda e: e.tensor_tensor(out=xdt[:, :].rearrange("p (h d) -> p h d", h=16), in0=ptx3,
                                                   in1=bc16(0), op=ALU.mult), reads=[ptx, sm], writes=[xdt])
            kb.op("dve", lambda e: e.tensor_tensor(out=xw[:, :].rearrange("p (h d) -> p h d", h=16), in0=ptx3,
                                                   in1=bc16(4 * 16), op=ALU.mult), reads=[ptx, sm], writes=[xw])
            kb.op("dve", lambda e: e.tensor_tensor(out=xd[:, :].rearrange("p (h d) -> p h d", h=16), in0=ptx3,
                                                   in1=A(rowpar, rp + 32, [[1, 16], [0, 64]]), op=ALU.mult),
                  reads=[ptx, rowpar], writes=[xd])
            kb.psrel(ptx)
            ptb_ = kb.psalloc()
            ptbb = ptb_[:, :].bitcast(BF16)

            def trb(e):
                e.transpose(out=ptbb[:, 0:128], in_=xbcsT[:, 8, :], identity=identb)
                return e.transpose(out=ptbb[:, 128:256], in_=xbcsT[:, 9, :], identity=identb)
            kb.op("pe", trb, reads=[XS[2], cb], writes=[ptb_])
            kb.op("act", lambda e: e.copy(out=btok[:, :], in_=ptbb[:, 0:256]), reads=[ptb_], writes=[btok])
            kb.psrel(ptb_)
            pcb = kb.psalloc()

            def cbmm(e):
                e.matmul(pcb[:, 0:128], lhsT=xbcsT[:, 8, :], rhs=xbcsT[:, 10, :], start=True, stop=True)
                return e.matmul(pcb[:, 128:256], lhsT=xbcsT[:, 9, :], rhs=xbcsT[:, 11, :], start=True, stop=True)
            kb.op("pe", cbmm, reads=[XS[2]], writes=[pcb])
            tri_off = CF_TRIS if sample else CF_TRI
            kb.op("dve", lambda e: e.tensor_tensor(out=cbm[:, :, :], in0=pcb[:, 0:256].rearrange("p (g i) -> p g i", g=2),
                                                   in1=A(cf, tri_off, [[0, 2], [1, 128]]), op=ALU.mult),
                  reads=[pcb, cf], writes=[cbm])
            kb.psrel(pcb)
            if sample:
                pyit = sample_seq_phase(l)
            pY = [kb.psalloc(), kb.psalloc()]
            for g in range(2):
                kb.op("pe", lambda e, g=g: e.matmul(pY[g][:, :], lhsT=identb, rhs=xd[:, g * 512:(g + 1) * 512],
                                                    start=True, stop=False), reads=[cb, xd], writes=[pY[g]])
            stage = {}

            def intra_a(q4):
                if q4 % 2 == 0:
                    Rbuf, R3, R2 = Rb, Rb[:, :, :], Rb[:, :, :].rearrange("p a b -> p (a b)")
                else:
                    Rbuf, R3, R2 = BIG[2], A(big6, 1024, [[128, 4], [1, 128]]), big6[:, 1024:1536]
                kb.op("pool", lambda e: e.tensor_tensor(out=R3, in0=A(sm, 16 + q4 * 4, [[1, 4], [0, 128]]),
                                                        in1=A(cf, tri_off, [[0, 4], [1, 128]]), op=ALU.mult),
                      reads=[sm, cf], writes=[Rbuf])
                pseg = kb.psalloc()
                kb.op("pe", lambda e: e.matmul(pseg[:, :], lhsT=su, rhs=R2, start=True, stop=True),
                      reads=[cf, Rbuf], writes=[pseg])
                stage[q4] = pseg

            def intra_b(q4):
                pseg = stage.pop(q4)
                Eb = kb.nxt("E")
                kb.op("act", lambda e: e.activation(out=Eb[:, :, :].rearrange("p a b -> p (a b)"),
                                                    in_=pseg[:, :], func=AF.Exp), reads=[pseg], writes=[Eb])
                kb.psrel(pseg)
                MTb = kb.nxt("MT")
                g = q4 // 2
                kb.op("dve", lambda e: e.tensor_tensor(
                    out=MTb[:, :, :], in0=Eb[:, :, :], in1=A(cbm, g * 128, [[0, 4], [1, 128]]), op=ALU.mult),
                    reads=[Eb, cbm], writes=[MTb])

                def ymm(e):
                    ins = None
                    for hh in range(4):
                        h = q4 * 4 + hh
                        o = pY[h // 8][:, (h % 8) * 64:(h % 8 + 1) * 64]
                        ins = e.matmul(o, lhsT=MTb[:, hh, :], rhs=xdt[:, h * 64:(h + 1) * 64], start=False, stop=(h % 8 == 7))
                    return ins
                kb.op("pe", ymm, reads=[MTb, xdt], writes=[pY[q4 // 2]], first=[])
            intra_a(0)
            for q4 in range(4):
                if q4 + 1 < 4:
                    intra_a(q4 + 1)
                intra_b(q4)

            if sample:
                for g in range(2):
                    kb.op("act" if g == 0 else "dve",
                          (lambda e, g=g: e.copy(out=scr[:, g * 512:(g + 1) * 512], in_=pyit[g][:, :])) if g == 0 else
                          (lambda e, g=g: e.tensor_copy(out=scr[:, g * 512:(g + 1) * 512], in_=pyit[g][:, :])),
                          reads=[pyit[g]], writes=[scr], first=None if g == 0 else [])
                kb.psrel(pyit[0], pyit[1])
            pYI = [kb.psalloc(), kb.psalloc()]
            if sample:
                for g in range(2):
                    def tryi(e, g=g):
                        ins = None
                        for c in range(4):
                            cc = g * 4 + c
                            ins = e.transpose(out=pYI[g][:, c * 128:(c + 1) * 128], in_=scr[:, cc * 128:(cc + 1) * 128],
                                              identity=identf)
                        return ins
                    kb.op("pe", tryi, reads=[scr, cf], writes=[pYI[g]])
            if not sample:
                ST = Sb[0]
                if ti == 0:
                    kb.op("pool", lambda e: e.memset(ST[:, :], 0.0), writes=[ST])
                    kb.op("pool", lambda e: e.memset(stb16[:, :], 0.0), writes=[stb16])
                for g in range(2):
                    kb.op("pe", lambda e, g=g: e.matmul(pYI[g][:, :], lhsT=xbcsT[:, 10 + g, :],
                                                        rhs=stb16[:, g * 512:(g + 1) * 512], start=True, stop=True),
                          reads=[XS[2], stb16], writes=[pYI[g]])

            y = big6
            YB = [BIG[0], BIG[1]]
            for g in range(2):
                kb.op("dve", lambda e, g=g: e.tensor_tensor(
                    out=A(y, g * 512, [[64, 8], [1, 64]]), in0=pYI[g][:, :].rearrange("p (h d) -> p h d", h=8),
                    in1=A(sm, 3 * 16 + g * 8, [[1, 8], [0, 64]]), op=ALU.mult),
                    reads=[pYI[g], sm], writes=[YB[g]])
            for g in range(2):
                kb.op("dve", lambda e, g=g: e.tensor_tensor(out=y[:, g * 512:(g + 1) * 512], in0=y[:, g * 512:(g + 1) * 512],
                                                            in1=pY[g][:, :], op=ALU.add), reads=[YB[g], pY[g]], writes=[YB[g]])
            kb.psrel(pYI[0], pYI[1], pY[0], pY[1])
            for g in range(2):
                kb.op("pool", lambda e, g=g: e.tensor_tensor(out=y[:, g * 512:(g + 1) * 512], in0=y[:, g * 512:(g + 1) * 512],
                                                             in1=sz[:, g * 512:(g + 1) * 512], op=ALU.mult),
                      reads=[YB[g], sz], writes=[YB[g]])
            ynb = xn
            for g in range(2):
                kb.op("act", lambda e, g=g: e.activation(out=ynb[:, g * 512:(g + 1) * 512], in_=y[:, g * 512:(g + 1) * 512],
                                                         func=AF.Square, accum_out=sm2[:, 2 + g:3 + g]),
                      reads=[YB[g]], writes=[ynb, sm2], first=None if g == 0 else [])
            kb.op("act", lambda e: e.activation(out=sm2[:, 4:6], in_=sm2[:, 2:4], func=AF.Ln, scale=1.0 / 512, bias=EPS),
                  reads=[sm2], writes=[sm2], first=[])
            kb.op("act", lambda e: e.activation(out=sm2[:, 4:6], in_=sm2[:, 4:6], func=AF.Exp, scale=-0.5),
                  reads=[sm2], writes=[sm2], first=[])
            for g in range(2):
                kb.op("dve", lambda e, g=g: e.tensor_scalar_mul(
                    out=ynb[:, g * 512:(g + 1) * 512], in0=y[:, g * 512:(g + 1) * 512], scalar1=sm2[:, 4 + g:5 + g]),
                    reads=[YB[g], sm2], writes=[ynb], first=None if g == 0 else [])
            pyt = kb.psalloc()
            pytb = pyt[:, :].bitcast(BF16)

            def try_(e):
                ins = None
                for c in range(8):
                    ins = e.transpose(out=pytb[:, c * 128:(c + 1) * 128], in_=ynb[:, c * 128:(c + 1) * 128], identity=identb)
                return ins
            kb.op("pe", try_, reads=[ynb, cb], writes=[pyt])
            kb.op("dve", lambda e: e.tensor_tensor(out=actT[:, 4:12, :], in0=pytb[:, :].rearrange("p (c t) -> p c t", c=8),
                                                   in1=A(colpar, l * NCP + CP_SG, [[1, 8], [0, 128]]), op=ALU.mult),
                  reads=[pyt, colpar], writes=[actT], first=actT_first())
            kb.psrel(pyt)

            if nxt_tile is not None:
                tile_state["pre"] = norm_stage(*nxt_tile)

            if not sample:
                ST = Sb[0]
                pup = [kb.psalloc(), kb.psalloc()]
                for g in range(2):
                    kb.op("pe", lambda e, g=g: e.matmul(pup[g][:, :], lhsT=btok[:, g * 128:(g + 1) * 128],
                                                        rhs=xw[:, g * 512:(g + 1) * 512], start=True, stop=True),
                          reads=[btok, xw], writes=[pup[g]])
                kb.op("pool", lambda e: e.tensor_tensor(out=ST[:, :].rearrange("p (h d) -> p h d", h=16),
                                                        in0=ST[:, :].rearrange("p (h d) -> p h d", h=16),
                                                        in1=A(sm, 5 * 16, [[1, 16], [0, 64]]), op=ALU.mult),
                      reads=[ST, sm], writes=[ST])
                for g in range(2):
                    kb.op("dve", lambda e, g=g: e.tensor_tensor(out=ST[:, g * 512:(g + 1) * 512], in0=ST[:, g * 512:(g + 1) * 512],
                                                                in1=pup[g][:, :], op=ALU.add), reads=[ST, pup[g]], writes=[ST])
                kb.psrel(pup[0], pup[1])
                if last_p:
                    for half in range(2):
                        pt_ = kb.psalloc()

                        def trs(e, pt_=pt_, half=half):
                            ins = None
                            for c in range(4):
                                cc = half * 4 + c
                                ins = e.transpose(out=pt_[:, c * 128:(c + 1) * 128], in_=ST[:, cc * 128:(cc + 1) * 128],
                                                  identity=identf)
                            return ins
                        kb.op("pe", trs, reads=[ST, cf], writes=[pt_])
                        kb.op("act", lambda e, pt_=pt_: e.copy(out=scr[:, 0:512], in_=pt_[:, :]), reads=[pt_], writes=[scr])
                        kb.psrel(pt_)
                        kb.dma("sp", o_ssm_p[l, half * 512:(half + 1) * 512, :].rearrange("(c p) n -> p c n", p=128),
                               scr[:, 0:512].rearrange("p (c n) -> p c n", c=4), reads=[scr], writes=[o_ssm_p], sembuf=scr)
                else:
                    kb.op("act", lambda e: e.copy(out=stb16[:, :], in_=ST[:, :]), reads=[ST], writes=[stb16])
            else:
                attn_tail(tile_state["psum_"], tile_state["pxa"])

            po = [kb.psalloc(), kb.psalloc()]
            for n in range(2):
                def omm(e, n=n):
                    ins = None
                    for k in range(16):
                        ins = e.matmul(po[n][:, :], lhsT=actT[:, k, :], rhs=wout[:, k, n * 512:(n + 1) * 512],
                                       start=(k == 0), stop=(k == 15))
                    return ins
                kb.op("pe", omm, reads=[actT, wout], writes=[po[n]])
            for n in range(2):
                kb.op("dve", lambda e, n=n: e.tensor_tensor(out=hb[:, n * 512:(n + 1) * 512], in0=hb[:, n * 512:(n + 1) * 512],
                                                            in1=po[n][:, :], op=ALU.add), reads=[hb, po[n]], writes=[hb])
            kb.psrel(po[0], po[1])
            if sample and l < LL - 1:
                load_wout(l + 1)
                load_attn(l + 1)
            if l < LL - 1:
                kb.dma("sp", hs[hidx * 128:(hidx + 1) * 128, :], hb[:, :], reads=[hb], writes=[hsB[hidx]], sembuf=hb)
            else:
                ss = sm2[:, 8:9]
                rs = sm2[:, 9:10]
                fj = scr if not special else xd
                kb.op("act", lambda e: e.activation(out=fj[:, 0:1024], in_=hb[:, :], func=AF.Square, accum_out=ss),
                      reads=[hb], writes=[fj, sm2n])
                kb.op("act", lambda e: e.activation(out=rs, in_=ss, func=AF.Ln, scale=1.0 / D, bias=EPS),
                      reads=[sm2n], writes=[sm2n])
                kb.op("act", lambda e: e.activation(out=rs, in_=rs, func=AF.Exp, scale=-0.5), reads=[sm2n], writes=[sm2n])
                kb.op("dve", lambda e: e.scalar_tensor_tensor(out=hb[:, :], in0=hb[:, :], scalar=rs, in1=fng[:, :],
                                                              op0=ALU.mult, op1=ALU.mult), reads=[hb, sm2n, fng], writes=[hb])
                dst = ys[:, :] if sample else yp[ti * 128:(ti + 1) * 128, :]
                kb.dma("sp", dst, hb[:, :], reads=[hb], writes=[ys if sample else yp], sembuf=hb)

        def sample_seq_phase(l):
            abcast = big6
            kb.op("pool", lambda e: e.tensor_copy(out=abcast[:, 0:1024].rearrange("p (h d) -> p h d", h=16),
                                                  in_=A(sm, 16, [[1, 16], [0, 64]])), reads=[sm], writes=[abcast])
            pdc = kb.psalloc()

            def dcmm(e):
                ins = None
                for c in range(8):
                    ins = e.matmul(pdc[:, c * 16:(c + 1) * 16], lhsT=abcast[:, c * 128:(c + 1) * 128],
                                   rhs=cf[:, CF_BM:CF_BM + 16], start=True, stop=True)
                return ins
            kb.op("pe", dcmm, reads=[abcast, cf], writes=[pdc])
            kb.op("act", lambda e: e.activation(out=decT[:, :, :].rearrange("p a b -> p (a b)"), in_=pdc[:, 0:128], func=AF.Exp),
                  reads=[pdc], writes=[decT])
            kb.psrel(pdc)
            pyit = [kb.psalloc(), kb.psalloc()]
            psum_ = kb.psalloc()
            pxa = kb.psalloc()
            tile_state["psum_"], tile_state["pxa"] = psum_, pxa
            for s_ in range(NS):
                c0 = s_ * 8
                Sin = kb.nxt("Sb")
                kb.dma("sp", Sin[:, :].rearrange("p (c n) -> p c n", c=8),
                       st_ssm[l, s_, :, :].rearrange("(c p) n -> p c n", p=128), writes=[Sin], sembuf=Sin)
                for half in range(2):
                    pts_ = kb.psalloc()

                    def trs(e, pts_=pts_, half=half):
                        ins = None
                        for c in range(4):
                            cc = half * 4 + c
                            ins = e.transpose(out=pts_[:, c * 128:(c + 1) * 128], in_=Sin[:, cc * 128:(cc + 1) * 128],
                                              identity=identf)
                        return ins
                    kb.op("pe", trs, reads=[Sin, cf], writes=[pts_])
                    kb.op("act" if half == 0 else "dve",
                          (lambda e, pts_=pts_, half=half: e.copy(out=stb16[:, half * 512:(half + 1) * 512], in_=pts_[:, :]))
                          if half == 0 else
                          (lambda e, pts_=pts_, half=half: e.tensor_copy(out=stb16[:, half * 512:(half + 1) * 512], in_=pts_[:, :])),
                          reads=[pts_], writes=[stb16], first=None if half == 0 else [])
                    kb.psrel(pts_)

                def yimm(e):
                    ins = None
                    for c in range(8):
                        ins = e.matmul(pyit[c // 4][:, (c % 4) * 128 + c0:(c % 4) * 128 + c0 + 8],
                                       lhsT=stb16[:, c * 128:(c + 1) * 128], rhs=xbcsT[:, 10 + c // 4, c0:c0 + 8],
                                       start=True, stop=True)
                    return ins
                kb.op("pe", yimm, reads=[stb16, xbcsT], writes=[pyit[0], pyit[1]], first=None if s_ == 0 else [])
                Bs = kb.nxt("Bs")
                kb.op("pool", lambda e, Bs=Bs, s_=s_: e.tensor_scalar_mul(out=Bs[:, :], in0=btok[:, :],
                                                                      scalar1=cf[:, CF_BM + s_:CF_BM + s_ + 1]), reads=[btok, cf], writes=[Bs])
                pup = [kb.psalloc(), kb.psalloc()]

                def upmm(e, Bs=Bs, pup=pup):
                    ins = None
                    for c in range(8):
                        g = c // 4
                        ins = e.matmul(pup[g][:, (c % 4) * 128:(c % 4 + 1) * 128], lhsT=xw[:, c * 128:(c + 1) * 128],
                                       rhs=Bs[:, g * 128:(g + 1) * 128], start=True, stop=True)
                    return ins
                kb.op("pe", upmm, reads=[xw, Bs], writes=[pup[0], pup[1]])
                for c in range(8):
                    kb.op("dve", lambda e, c=c, s_=s_, Sin=Sin, pup=pup: e.scalar_tensor_tensor(
                        out=Sin[:, c * 128:(c + 1) * 128], in0=Sin[:, c * 128:(c + 1) * 128], scalar=decT[:, c, s_:s_ + 1],
                        in1=pup[c // 4][:, (c % 4) * 128:(c % 4 + 1) * 128], op0=ALU.mult, op1=ALU.add),
                        reads=[Sin, decT, pup[c // 4]], writes=[Sin])
                kb.psrel(pup[0], pup[1])
                kb.dma("sp", o_ssm_s[l, s_, :, :].rearrange("(c p) n -> p c n", p=128),
                       Sin[:, :].rearrange("p (c n) -> p c n", c=8), reads=[Sin], writes=[o_ssm_s], sembuf=Sin)
                Kb = kb.nxt("Kb")
                Vb = kb.nxt("Vb")
                kb.dma("pool", Kb[:, :, :], ck[l, s_, :, :].rearrange("(c p) n -> p c n", p=128), writes=[Kb], sembuf=Kb)
                kb.dma("pool", Vb[:, :, :], cv[l, s_, :, :].rearrange("(c p) n -> p c n", p=128), writes=[Vb], sembuf=Vb)
                pkt = kb.psalloc()
                pktb = pkt[:, :].bitcast(BF16)

                def trk(e, Kb=Kb):
                    ins = None
                    for h in range(4):
                        for mc in range(2):
                            ins = e.transpose(out=pktb[:, (h * 2 + mc) * 128:(h * 2 + mc + 1) * 128],
                                              in_=Kb[:, mc, h * 128:(h + 1) * 128], identity=identb)
                    return ins
                kb.op("pe", trk, reads=[Kb, cb], writes=[pkt])
                kb.op("act", lambda e: e.copy(out=ktl[:, :, :].rearrange("p h m -> p (h m)"), in_=pktb[:, :]),
                      reads=[pkt], writes=[ktl])
                kb.psrel(pkt)
                psc = kb.psalloc()

                def scmm(e):
                    ins = None
                    for h in range(4):
                        for mc in range(2):
                            ins = e.matmul(psc[:, mc * 32 + h * 8:mc * 32 + h * 8 + 8], lhsT=ktl[:, h, mc * 128:(mc + 1) * 128],
                                           rhs=qT[:, h, c0:c0 + 8], start=True, stop=True)
                    return ins
                kb.op("pe", scmm, reads=[ktl, qT], writes=[psc])
                kb.op("act", lambda e: e.activation(out=pts[:, :], in_=psc[:, 0:64], func=AF.Exp, scale=SCALE),
                      reads=[psc], writes=[pts])
                kb.psrel(psc)

                def smm(e, s_=s_, c0=c0, Vb=Vb):
                    o = psum_[:, s_ * 32:(s_ + 1) * 32]
                    e.matmul(o, lhsT=onesb, rhs=pts[:, 0:32], start=True, stop=False)
                    ins = e.matmul(o, lhsT=onesb, rhs=pts[:, 32:64], start=False, stop=True)
                    for h in range(4):
                        for mc in range(2):
                            ins = e.matmul(pxa[:, h * 128 + c0:h * 128 + c0 + 8], lhsT=Vb[:, mc, h * 128:(h + 1) * 128],
                                           rhs=pts[:, mc * 32 + h * 8:mc * 32 + h * 8 + 8], start=(mc == 0), stop=(mc == 1))
                    return ins
                kb.op("pe", smm, reads=[cb, pts, Vb], writes=[psum_, pxa], first=None if s_ == 0 else [])
            return pyit

        tile_state = {}
        d2d = Buf("d2d", None)
        order = []
        for l in range(cfg['layers']):
            for ti in range(cfg['tiles']):
                order.append((l, ti, False))
            if cfg['sample']:
                order.append((l, 0, True))
        if order:
            load_win(0)
            load_poolw(0)
            load_attn(0)
            load_wout(0)
        for i, (l, ti, smp) in enumerate(order):
            do_tile(l, ti, smp, order[i + 1] if i + 1 < len(order) else None)
        kb.finish(outs)
        print("total ops", kb.nops)
    return nc


def _constants():
    cf = np.zeros((128, NCF), np.float32)
    r = np.arange(128)
    cf[:, CF_ID:CF_ID + 128] = np.eye(128)
    tri = (r[:, None] <= r[None, :]).astype(np.float32)
    same = (r[:, None] // 8 == r[None, :] // 8).astype(np.float32)
    cf[:, CF_TRI:CF_TRI + 128] = tri
    cf[:, CF_TRIS:CF_TRIS + 128] = tri * same
    cf[:, CF_SU:CF_SU + 128] = (r[:, None] > r[None, :]).astype(np.float32)
    cf[:, CF_ONE:CF_ONE + 128] = 1.0
    cf[:, CF_BONE:CF_BONE + 128] = same
    cf[:, CF_BM:CF_BM + 16] = (r[:, None] // 8 == np.arange(16)[None, :]).astype(np.float32)
    cb = np.zeros((128, NCB), np.float32)
    cb[:, CB_ID:CB_ID + 128] = np.eye(128)
    cb[:, CB_ONE:CB_ONE + 128] = 1.0
    for g, w in enumerate(POOL_W):
        s = r[:, None]
        t = r[None, :]
        cur = ((s <= t) & (s > t - w)).astype(np.float32) - w * (s == t)
        prev = np.zeros((128, 128), np.float32)
        srel = s - 128
        prev[:, :] = ((srel > t - w)).astype(np.float32)
        prev[:64, :] = 0.0
        cnt0 = np.minimum(t + 1, w).astype(np.float32)
        cur0 = ((s <= t) & (s > t - w)).astype(np.float32) - cnt0 * (s == t)
        cf[:, CF_IC0 + g * 128:CF_IC0 + (g + 1) * 128] = np.broadcast_to(1.0 / cnt0, (128, 128))
        ss_, ts_ = s // 8, s % 8
        sc_, tc_ = t // 8, t % 8
        bs = ((ss_ == sc_) & (ts_ <= tc_) & (ts_ > tc_ - w)).astype(np.float32) - w * (s == t)
        sta = np.zeros((128, 128), np.float32)
        stbm = np.zeros((128, 128), np.float32)
        rows = np.arange(120)
        sq, rr = rows // 15, rows % 15
        for half, m in ((0, sta), (1, stbm)):
            m[:120, :] = ((sq[:, None] + 8 * half == sc_) & ((rr[:, None] - 15) > (tc_ - w))).astype(np.float32)
        for k, m in enumerate((cur, prev, cur0, bs, sta, stbm)):
            o = CB_BAND + (g * 6 + k) * 128
            cb[:, o:o + 128] = m
    return cf, cb


_NC_CACHE = {}
_DBG_CFG = None


def kernel(x_prompt, x_sample, mem_prompt, state_pool, state_conv, state_ssm, cache_mem_k, cache_mem_v,
           norm_g, w_in, pool_w, pool_scale, conv_w, conv_b, dt_bias, a_log, d_skip, ssd_norm_g,
           mem_norm_g, w_mem_k, w_mem_v, w_out, final_norm_g):
    f = lambda a: np.ascontiguousarray(np.asarray(a, dtype=np.float32))
    x_prompt, x_sample, mem_prompt = f(x_prompt), f(x_sample), f(mem_prompt)
    state_pool, state_conv, state_ssm = f(state_pool), f(state_conv), f(state_ssm)
    cache_mem_k, cache_mem_v = f(cache_mem_k), f(cache_mem_v)
    cf, cb = _constants()
    colpar = np.zeros((128, DEPTH, NCP), np.float32)
    for l in range(DEPTH):
        colpar[:, l, CP_NG:CP_NG + 8] = f(norm_g)[l].reshape(8, 128).T
        colpar[:, l, CP_MG:CP_MG + 8] = f(mem_norm_g)[l].reshape(8, 128).T
        colpar[:, l, CP_SG:CP_SG + 8] = f(ssd_norm_g)[l].reshape(8, 128).T
        colpar[:, l, CP_PS:CP_PS + 4] = f(pool_scale)[l].reshape(4, 128).T
        colpar[:, l, CP_CW:CP_CW + 48] = f(conv_w)[l].reshape(4, 12, 128).transpose(2, 1, 0).reshape(128, 48)
        colpar[:, l, CP_CB:CP_CB + 12] = f(conv_b)[l].reshape(12, 128).T
    colpar = np.ascontiguousarray(colpar.reshape(128, DEPTH * NCP))
    rowpar = np.ascontiguousarray(np.concatenate([f(dt_bias), f(a_log), f(d_skip)], axis=1).reshape(-1))
    shared = {
        "w_in": f(w_in), "w_out": f(w_out), "pool_w": f(pool_w), "w_mk": f(w_mem_k), "w_mv": f(w_mem_v),
        "colpar": colpar, "rowpar": rowpar, "fng": f(final_norm_g), "cstf": cf, "cstb": cb,
    }
    in_maps = []
    for c in range(NCORES):
        sl = slice(c * NS, (c + 1) * NS)
        m = dict(shared)
        m["xp"] = x_prompt[c]
        m["xs"] = np.ascontiguousarray(x_sample[sl].reshape(128, D))
        m["mem"] = mem_prompt[c]
        m["st_pool"] = np.ascontiguousarray(state_pool[:, sl].reshape(DEPTH, NS * 15, 512))
        m["st_conv"] = np.ascontiguousarray(state_conv[:, sl].reshape(DEPTH, NS * 3, 1536))
        m["st_ssm"] = np.ascontiguousarray(state_ssm[:, sl].reshape(DEPTH, NS, 1024, 128))
        m["ck"] = np.ascontiguousarray(cache_mem_k[:, sl].reshape(DEPTH, NS, 256, 512))
        m["cv"] = np.ascontiguousarray(cache_mem_v[:, sl].reshape(DEPTH, NS, 256, 512))
        in_maps.append(m)
    if "nc" not in _NC_CACHE:
        _NC_CACHE["nc"] = build_nc(_DBG_CFG)
    nc = _NC_CACHE["nc"]
    res = run_bass_kernel_spmd(nc, in_maps, core_ids=list(range(NCORES)))
    R = res.results
    g = lambda name, c: np.asarray(R[c][name], dtype=np.float32)
    y_prompt = np.stack([g("yp", c) for c in range(NCORES)]).reshape(8, 2048, D)
    y_sample = np.concatenate([g("ys", c).reshape(NS, 8, D) for c in range(NCORES)], axis=0)
    new_pool_p = np.stack([g("o_pool_p", c) for c in range(NCORES)], axis=1)
    new_conv_p = np.stack([g("o_conv_p", c) for c in range(NCORES)], axis=1)
    new_ssm_p = np.stack([g("o_ssm_p", c).reshape(DEPTH, 16, 64, 128) for c in range(NCORES)], axis=1)
    new_mk = np.stack([g("o_mk", c).reshape(DEPTH, 256, 4, 128) for c in range(NCORES)], axis=1)
    new_mv = np.stack([g("o_mv", c).reshape(DEPTH, 256, 4, 128) for c in range(NCORES)], axis=1)
    new_pool_s = np.concatenate([g("o_pool_s", c) for c in range(NCORES)], axis=1)
    new_conv_s = np.concatenate([g("o_conv_s", c) for c in range(NCORES)], axis=1)
    new_ssm_s = np.concatenate([g("o_ssm_s", c).reshape(DEPTH, NS, 16, 64, 128) for c in range(NCORES)], axis=1)
    return (y_prompt, y_sample, new_pool_p, new_conv_p, new_ssm_p, new_mk, new_mv, new_pool_s, new_conv_s, new_ssm_s)
```

```python
import contextlib
import math
import numpy as np
import concourse.bass as bass
import concourse.mybir as mybir
from concourse.bass_utils import run_bass_kernel_spmd

F32 = mybir.dt.float32
BF16 = mybir.dt.bfloat16
AF = mybir.ActivationFunctionType
ALU = mybir.AluOpType

NCORES = 8
DEPTH = 4
D = 1024
NT = 16
NS = 16
DIN = 4624
C_U, C_GP, C_Z, C_XBC, C_DT, C_Q, C_GX = 0, 512, 1024, 2048, 3584, 3600, 4112
EPS = 1e-6
POOL_W = (2, 4, 8, 16)
SEM_MAX = 3600

CP_NG, CP_MG, CP_SG, CP_PS, CP_CW, CP_CB, NCP = 0, 8, 16, 24, 28, 76, 88
CF_ID, CF_TRI, CF_TRIS, CF_SU, CF_ONE, CF_BONE, CF_BM, CF_IC0, NCF = 0, 128, 256, 384, 512, 640, 768, 784, 1296
CB_ID, CB_ONE, CB_BAND, NCB = 0, 128, 256, 256 + 24 * 128


class Buf:
    __slots__ = ("name", "t", "writes", "reads", "old", "dsem", "dval", "partial")

    def __init__(self, name, t, partial=False):
        self.name = name
        self.t = t
        self.writes = {}
        self.reads = {}
        self.old = {}
        self.dsem = None
        self.dval = 0
        self.partial = partial

    def __getitem__(self, k):
        return self.t[k]


def _merge(d, ev):
    for k, v in ev.items():
        if d.get(k, 0) < v:
            d[k] = v


class KB:
    def __init__(self, nc, stack):
        self.nc = nc
        self.stack = stack
        self.E = {"pe": nc.tensor, "act": nc.scalar, "dve": nc.vector, "pool": nc.gpsimd, "sp": nc.sync}
        self.sem, self.cnt, self.seen = {}, {}, {}
        for e in self.E:
            self.sem[e] = stack.enter_context(nc.semaphore("s_" + e))
            self.cnt[e] = 0
            self.seen[e] = {}
        self.nbuf = 0
        self.rr = {}
        self.psfree = []
        self.maxwaited = {}
        self.dmafinal = {}
        self.nops = 0
        self.limit = 1 << 60

    def sb(self, name, shape, dtype):
        self.nbuf += 1
        t = self.stack.enter_context(self.nc.sbuf_tensor(f"{name}_{self.nbuf}", list(shape), dtype))
        return Buf(name, t)

    def ps(self, name, shape, dtype=F32):
        self.nbuf += 1
        t = self.stack.enter_context(self.nc.psum_tensor(f"{name}_{self.nbuf}", list(shape), dtype))
        return Buf(name, t)

    def dram(self, name, shape, dtype, kind):
        t = self.nc.dram_tensor(name, list(shape), dtype, kind=kind)
        return Buf(name, t, partial=True)

    def pool(self, name, shape, dtype, n):
        bufs = [self.sb(f"{name}{i}", shape, dtype) for i in range(n)]
        self.rr[name] = [bufs, 0]
        return bufs

    def nxt(self, name):
        r = self.rr[name]
        b = r[0][r[1] % len(r[0])]
        r[1] += 1
        return b

    def psalloc(self):
        assert self.psfree, "out of PSUM banks"
        return self.psfree.pop(0)

    def psrel(self, *bs):
        for b in bs:
            self.psfree.append(b)

    def _isfirst(self, b, first):
        if b.partial:
            return False
        return first is None or b in first

    def _deps(self, reads, writes, first):
        dep = {}
        for b in reads:
            _merge(dep, b.writes)
        for b in writes:
            if self._isfirst(b, first):
                _merge(dep, b.writes)
                _merge(dep, b.reads)
            else:
                _merge(dep, b.old)
        return dep

    def _wait(self, e, dep):
        eng = self.E[e]
        seen = self.seen[e]
        for sem, val in dep.items():
            if seen.get(sem, 0) < val:
                eng.wait_ge(sem, val)
                seen[sem] = val
            if self.maxwaited.get(sem, 0) < val:
                self.maxwaited[sem] = val

    def _commit(self, ev, reads, writes, first):
        for b in writes:
            if self._isfirst(b, first):
                old = {}
                _merge(old, b.writes)
                _merge(old, b.reads)
                b.old = old
                b.writes = dict(ev)
                b.reads = {}
            else:
                _merge(b.writes, ev)
        for b in reads:
            if b in writes:
                continue
            _merge(b.reads, ev)

    def op(self, e, fn, reads=(), writes=(), first=None):
        self.nops += 1
        if self.nops > self.limit:
            return
        dep = self._deps(reads, writes, first)
        self._wait(e, dep)
        ins = fn(self.E[e])
        self.cnt[e] += 1
        ins.then_inc(self.sem[e], 1)
        self._commit({self.sem[e]: self.cnt[e]}, reads, writes, first)
        if self.cnt[e] >= SEM_MAX:
            self.nbuf += 1
            self.sem[e] = self.stack.enter_context(self.nc.semaphore(f"s_{e}_{self.nbuf}"))
            self.cnt[e] = 0

    def dma(self, q, out_ap, in_ap, reads=(), writes=(), first=None, sembuf=None, **kw):
        self.nops += 1
        if self.nops > self.limit:
            return
        dep = self._deps(reads, writes, first)
        b = sembuf
        if b.dsem is None or b.dval + 16 > SEM_MAX:
            self.nbuf += 1
            b.dsem = self.stack.enter_context(self.nc.semaphore(f"d_{b.name}_{self.nbuf}"))
            b.dval = 0
        if b.dsem is not None and self.maxwaited.get(b.dsem, 0) > 0:
            _merge(dep, {b.dsem: self.maxwaited[b.dsem]})
        self._wait(q, dep)
        ins = self.E[q].dma_start(out=out_ap, in_=in_ap, **kw)
        b.dval += 16
        ins.then_inc(b.dsem, 16)
        self.dmafinal[b.dsem] = b.dval
        self._commit({b.dsem: b.dval}, reads, writes, first)

    def finish(self, outs, e="sp"):
        dep = {}
        for b in outs:
            _merge(dep, b.writes)
        _merge(dep, self.dmafinal)
        self._wait(e, dep)


def A(buf, off, dims, p0=0, np_=128):
    row = int(np.prod(buf.t.shape[1:]))
    return bass.AP(buf.t, p0 * row + off, [[row, np_]] + [list(d) for d in dims])


def DA(buf, off, dims):
    return bass.AP(buf.t, off, [list(d) for d in dims])


def build_nc(cfg=None):
    cfg = dict(mem=1, layers=DEPTH, tiles=NT, sample=1) if cfg is None else cfg
    nc = bass.Bass("TRN2", target_bir_lowering=False)
    with contextlib.ExitStack() as st:
        kb = KB(nc, st)
        kb.limit = cfg.get('stop', 1 << 60)
        I, O = "ExternalInput", "ExternalOutput"
        xp = kb.dram("xp", [2048, D], F32, I)
        xs = kb.dram("xs", [128, D], F32, I)
        mem = kb.dram("mem", [256, D], F32, I)
        st_pool = kb.dram("st_pool", [DEPTH, NS * 15, 512], F32, I)
        st_conv = kb.dram("st_conv", [DEPTH, NS * 3, 1536], F32, I)
        st_ssm = kb.dram("st_ssm", [DEPTH, NS, 1024, 128], F32, I)
        ck = kb.dram("ck", [DEPTH, NS, 256, 512], F32, I)
        cv = kb.dram("cv", [DEPTH, NS, 256, 512], F32, I)
        w_in = kb.dram("w_in", [DEPTH, D, DIN], F32, I)
        w_out = kb.dram("w_out", [DEPTH, 2048, D], F32, I)
        pool_w = kb.dram("pool_w", [DEPTH, 4, 128, 128], F32, I)
        w_mk = kb.dram("w_mk", [DEPTH, D, 512], F32, I)
        w_mv = kb.dram("w_mv", [DEPTH, D, 512], F32, I)
        colpar_d = kb.dram("colpar", [128, DEPTH * NCP], F32, I)
        rowpar_d = kb.dram("rowpar", [DEPTH * 48], F32, I)
        fng_d = kb.dram("fng", [D], F32, I)
        cstf_d = kb.dram("cstf", [128, NCF], F32, I)
        cstb_d = kb.dram("cstb", [128, NCB], F32, I)

        yp = kb.dram("yp", [2048, D], F32, O)
        ys = kb.dram("ys", [128, D], F32, O)
        o_pool_p = kb.dram("o_pool_p", [DEPTH, 15, 512], F32, O)
        o_conv_p = kb.dram("o_conv_p", [DEPTH, 3, 1536], F32, O)
        o_ssm_p = kb.dram("o_ssm_p", [DEPTH, 1024, 128], F32, O)
        o_mk = kb.dram("o_mk", [DEPTH, 256, 512], F32, O)
        o_mv = kb.dram("o_mv", [DEPTH, 256, 512], F32, O)
        o_pool_s = kb.dram("o_pool_s", [DEPTH, NS, 15, 512], F32, O)
        o_conv_s = kb.dram("o_conv_s", [DEPTH, NS, 3, 1536], F32, O)
        o_ssm_s = kb.dram("o_ssm_s", [DEPTH, NS, 1024, 128], F32, O)
        hs = kb.dram("hs", [17 * 128, D], F32, "Internal")
        ktscr = kb.dram("ktscr", [DEPTH, 128, 1024], BF16, "Internal")
        hsB = [Buf(f"hs{i}", hs.t, partial=True) for i in range(17)]
        outs = [yp, ys, o_pool_p, o_conv_p, o_ssm_p, o_mk, o_mv, o_pool_s, o_conv_s, o_ssm_s]

        pad0 = kb.sb("pad0", [128, 16], F32)
        win = kb.sb("win", [128, 8, DIN], BF16)
        wout = kb.sb("wout", [128, 16, D], BF16)
        poolw = kb.sb("poolw", [128, 4, 128], BF16)
        ktl = kb.sb("ktl", [128, 4, 256], BF16)
        vl = kb.sb("vl", [128, 2, 512], BF16)
        fng = kb.sb("fng", [128, D], F32)
        colpar = kb.sb("colpar", [128, DEPTH * NCP], F32)
        rowpar = kb.sb("rowpar", [128, DEPTH * 48], F32)
        abc = kb.sb("abc", [128, DEPTH * 16], F32)
        cf = kb.sb("cf", [128, NCF], F32)
        cb = kb.sb("cb", [128, NCB], BF16)

        kb.pool("hbuf", [128, D], F32, 2)
        xn = kb.sb("xn", [128, D], BF16)
        xnT = kb.sb("xnT", [128, 8, 128], BF16)
        kb.pool("ubf", [128, 512], BF16, 2)
        sgpT = kb.sb("sgpT", [128, 4, 128], BF16)
        dT = kb.sb("dT", [128, 4, 128], BF16)
        qT = kb.sb("qT", [128, 4, 128], BF16)
        sgxT = kb.sb("sgxT", [128, 4, 128], BF16)
        sz = kb.sb("sz", [128, D], BF16)
        xb = kb.sb("xb", [128, 12 * 176], F32)
        carry = kb.sb("carry", [128, 12, 3], F32)
        big6 = kb.sb("big6", [128, 12 * 128], F32)
        big6b = Buf("big6b", big6.t)
        xbcsT = kb.sb("xbcsT", [128, 12, 128], BF16)
        xdt = kb.sb("xdt", [128, D], BF16)
        xd = kb.sb("xd", [128, D], BF16)
        xw = kb.sb("xw", [128, D], BF16)
        btok = kb.sb("btok", [128, 256], BF16)
        sm = kb.sb("sm", [128, 6 * 16], F32)
        sm2 = kb.sb("sm2", [128, 16], F32)
        Rb = kb.sb("Rb", [128, 4, 128], F32)
        kb.pool("E", [128, 4, 128], BF16, 2)
        kb.pool("MT", [128, 4, 128], BF16, 2)
        cbm = kb.sb("cbm", [128, 2, 128], BF16)
        Sb = kb.pool("Sb", [128, D], F32, 2)
        stb16 = kb.sb("stb16", [128, D], BF16)
        actT = kb.sb("actT", [128, 16, 128], BF16)
        scr = kb.sb("scr", [128, D], F32)
        kb.pool("Kb", [128, 2, 512], BF16, 2)
        kb.pool("Vb", [128, 2, 512], BF16, 2)
        kb.pool("Bs", [128, 256], BF16, 1)
        spool16 = kb.sb("spool16", [128, 2, 512], BF16)
        decT = kb.sb("decT", [128, 8, 16], F32)
        pts = kb.sb("pts", [128, 64], BF16)
        ctmp = kb.sb("ctmp", [128, 128], F32)
        print("sbuf bytes remaining after alloc:", nc.sbuf_bytes_remaining)

        for i in range(8):
            kb.psfree.append(kb.ps(f"bank{i}", [128, 512], F32))

        def smt(i):
            return sm[:, i * 16:(i + 1) * 16]

        kb.dma("sp", cf[:, :], cstf_d[:, :], writes=[cf], sembuf=cf)
        kb.dma("pool", cb[:, :], cstb_d[:, :], writes=[cb], sembuf=cb)
        kb.dma("sp", colpar[:, :], colpar_d[:, :], writes=[colpar], sembuf=colpar)
        kb.dma("sp", rowpar[:, :], DA(rowpar_d, 0, [[0, 128], [1, DEPTH * 48]]), writes=[rowpar], sembuf=rowpar)
        kb.dma("sp", fng[:, :], DA(fng_d, 0, [[0, 128], [1, D]]), writes=[fng], sembuf=fng)
        for l in range(DEPTH):
            kb.op("act", lambda e, l=l: e.activation(out=abc[:, l * 16:(l + 1) * 16],
                                                      in_=rowpar[:, l * 48 + 16:l * 48 + 32], func=AF.Exp),
                  reads=[rowpar], writes=[abc])
        kb.op("dve", lambda e: e.tensor_scalar_mul(out=abc[:, :], in0=abc[:, :], scalar1=-1.0),
              reads=[abc], writes=[abc])

        identf = cf[:, CF_ID:CF_ID + 128]
        identb = cb[:, CB_ID:CB_ID + 128]
        onesb = cb[:, CB_ONE:CB_ONE + 128]

        def band(g, k):
            o = CB_BAND + (g * 6 + k) * 128
            return cb[:, o:o + 128]

        def cpcol(l, off, n=1):
            return colpar[:, l * NCP + off:l * NCP + off + n]

        def rmsnorm_T(hb, gcol_off, l):
            ss = sm2[:, 0:1]
            rs = sm2[:, 1:2]
            kb.op("act", lambda e: e.activation(out=xn[:, :], in_=hb[:, :], func=AF.Square, accum_out=ss),
                  reads=[hb], writes=[xn, sm2])
            kb.op("act", lambda e: e.activation(out=rs, in_=ss, func=AF.Ln, scale=1.0 / D, bias=EPS),
                  reads=[sm2], writes=[sm2])
            kb.op("act", lambda e: e.activation(out=rs, in_=rs, func=AF.Exp, scale=-0.5), reads=[sm2], writes=[sm2])
            kb.op("dve", lambda e: e.tensor_scalar_mul(out=xn[:, :], in0=hb[:, :], scalar1=rs),
                  reads=[hb, sm2], writes=[xn])
            pt = kb.psalloc()
            ptb = pt[:, :].bitcast(BF16)

            def tr(e):
                ins = None
                for k in range(8):
                    ins = e.transpose(out=ptb[:, k * 128:(k + 1) * 128], in_=xn[:, k * 128:(k + 1) * 128], identity=identb)
                return ins
            kb.op("pe", tr, reads=[xn, cb], writes=[pt])
            kb.op("dve", lambda e: e.tensor_tensor(out=xnT[:, :, :], in0=ptb[:, :].rearrange("p (k t) -> p k t", k=8),
                                                   in1=A(colpar, l * NCP + gcol_off, [[1, 8], [0, 128]]), op=ALU.mult),
                  reads=[pt, colpar], writes=[xnT])
            kb.psrel(pt)

        def mm_tok(pbank, ncols, wbuf, wap_fn):
            def f(e):
                ins = None
                for k in range(8):
                    ins = e.matmul(pbank[:, 0:ncols], lhsT=xnT[:, k, :], rhs=wap_fn(k), start=(k == 0), stop=(k == 7))
                return ins
            kb.op("pe", f, reads=[xnT, wbuf], writes=[pbank])

        def mm_feat(pbank, nchunk, wbuf, wap_fn):
            def f(e):
                ins = None
                for c in range(nchunk):
                    for k in range(8):
                        ins = e.matmul(pbank[:, c * 128:(c + 1) * 128], lhsT=wap_fn(k, c), rhs=xnT[:, k, :],
                                       start=(k == 0), stop=(k == 7))
                return ins
            kb.op("pe", f, reads=[xnT, wbuf], writes=[pbank])

        for l in range(DEPTH if cfg['mem'] else 0):
            s = l % 2
            for k in range(8):
                kb.dma("pool", wout[:, 8 * s + k, 0:512], w_mk[l, k * 128:(k + 1) * 128, :], writes=[wout],
                       first=None if (k == 0 and s == 0) else [], sembuf=wout)
                kb.dma("pool", wout[:, 8 * s + k, 512:1024], w_mv[l, k * 128:(k + 1) * 128, :], writes=[wout],
                       first=[], sembuf=wout)
            for mt in range(2):
                hb = kb.nxt("hbuf")
                kb.dma("sp", hb[:, :], mem[mt * 128:(mt + 1) * 128, :], writes=[hb], sembuf=hb)
                rmsnorm_T(hb, CP_MG, l)
                pk = kb.psalloc()
                mm_tok(pk, 512, wout, lambda k: wout[:, 8 * s + k, 0:512])
                kb.op("act", lambda e: e.copy(out=scr[:, 0:512], in_=pk[:, :]), reads=[pk], writes=[scr])
                kb.psrel(pk)
                pv = kb.psalloc()
                mm_tok(pv, 512, wout, lambda k: wout[:, 8 * s + k, 512:1024])
                kb.op("dve", lambda e: e.tensor_copy(out=scr[:, 512:1024], in_=pv[:, :]), reads=[pv], writes=[scr], first=[])
                kb.psrel(pv)
                kb.dma("sp", o_mk[l, mt * 128:(mt + 1) * 128, :], scr[:, 0:512], reads=[scr], writes=[o_mk], sembuf=scr)
                kb.dma("sp", o_mv[l, mt * 128:(mt + 1) * 128, :], scr[:, 512:1024], reads=[scr], writes=[o_mv], sembuf=scr)
                pkt = kb.psalloc()
                mm_feat(pkt, 4, wout, lambda k, c: wout[:, 8 * s + k, c * 128:(c + 1) * 128])
                kb.op("act", lambda e: e.copy(out=ktl[:, :, mt * 128:(mt + 1) * 128],
                                              in_=pkt[:, :].rearrange("p (h m) -> p h m", h=4)),
                      reads=[pkt], writes=[ktl], first=None if mt == 0 else [])
                kb.psrel(pkt)
            kb.dma("sp", ktscr[l, :, :], ktl[:, :, :].rearrange("p h m -> p (h m)"), reads=[ktl], writes=[ktscr], sembuf=ktl)

        def load_win(l):
            for k in range(8):
                kb.dma("pool", win[:, k, :], w_in[l, k * 128:(k + 1) * 128, :], writes=[win],
                       first=None if k == 0 else [], sembuf=win)

        def load_poolw(l):
            kb.dma("pool", poolw[:, :, :], pool_w[l, :, :, :].rearrange("g c d -> c g d"), writes=[poolw], sembuf=poolw)

        def load_attn(l):
            kb.dma("sp", ktl[:, :, :].rearrange("p h m -> p (h m)"), ktscr[l, :, :], reads=[ktscr], writes=[ktl], sembuf=ktl)
            kb.dma("pool", vl[:, :, :], o_mv[l, :, :].rearrange("(c p) n -> p c n", p=128), reads=[o_mv], writes=[vl], sembuf=vl)

        def load_wout(l):
            for k in range(16):
                kb.dma("pool", wout[:, k, :], w_out[l, k * 128:(k + 1) * 128, :], writes=[wout],
                       first=None if k == 0 else [], sembuf=wout)

        SCALE = 1.0 / math.sqrt(128.0)

        XB = [xb, Buf("xb1", xb.t), Buf("xb2", xb.t)]
        BIG = [big6, Buf("big1", big6.t), big6b]
        XS = [xbcsT, Buf("xs1", xbcsT.t), Buf("xs2", xbcsT.t)]
        sm2n = Buf("sm2n", sm2.t)

        def norm_stage(l, ti, sample):
            hidx = 16 if sample else ti
            hb = kb.nxt("hbuf")
            if l == 0:
                src = xs[:, :] if sample else xp[ti * 128:(ti + 1) * 128, :]
                kb.dma("sp", hb[:, :], src, writes=[hb], sembuf=hb)
            else:
                kb.dma("sp", hb[:, :], hs[hidx * 128:(hidx + 1) * 128, :], reads=[hsB[hidx]], writes=[hb], sembuf=hb)
            rmsnorm_T(hb, CP_NG, l)
            return hb

        def do_tile(l, ti, sample, nxt_tile):
            last_p = (not sample) and ti == cfg['tiles'] - 1
            LL = cfg['layers']
            special = sample or last_p
            hidx = 16 if sample else ti
            pre = tile_state.pop("pre", None)
            if pre is not None:
                hb, xbc_done = pre
            else:
                hb, xbc_done = norm_stage(l, ti, sample), False
            rp = l * 48
            afirst = [None]

            def actT_first():
                f = afirst[0]
                afirst[0] = []
                return f

            if not xbc_done:
                if sample:
                    for half in range(2):
                        kb.dma("sp", scr[0:48, 0:768], st_conv[l, :, half * 768:(half + 1) * 768], writes=[scr], sembuf=scr)
                        pc = kb.psalloc()

                        def trc(e, pc=pc):
                            ins = None
                            for c in range(6):
                                ins = e.transpose(out=pc[:, c * 48:(c + 1) * 48], in_=scr[0:48, c * 128:(c + 1) * 128],
                                                  identity=cf[0:48, CF_ID:CF_ID + 48])
                            return ins
                        kb.op("pe", trc, reads=[scr, cf], writes=[pc])
                        wr = [XB[0], XB[1]] if half == 0 else [XB[1], XB[2]]
                        kb.op("act", lambda e, pc=pc, half=half: e.copy(
                            out=A(xb, half * 6 * 176, [[176, 6], [11, 16], [1, 3]]),
                            in_=A(pc, 0, [[48, 6], [3, 16], [1, 3]])), reads=[pc], writes=wr,
                            first=None if half == 0 else [XB[2]])
                        kb.psrel(pc)
                else:
                    if ti == 0:
                        kb.op("pool", lambda e: e.memset(carry[:, :, :], 0.0), writes=[carry])
                    kb.op("pool", lambda e: e.tensor_copy(out=A(xb, 0, [[176, 12], [1, 3]]), in_=carry[:, :, :]),
                          reads=[carry], writes=XB)
                for b3 in range(3):
                    pxb = kb.psalloc()
                    mm_feat(pxb, 4, win, lambda k, c, b3=b3: win[:, k, C_XBC + (b3 * 4 + c) * 128:C_XBC + (b3 * 4 + c + 1) * 128])
                    if sample:
                        o_ap = A(xb, b3 * 4 * 176 + 3, [[176, 4], [11, 16], [1, 8]])
                        i_ap = A(pxb, 0, [[128, 4], [8, 16], [1, 8]])
                    else:
                        o_ap = A(xb, b3 * 4 * 176 + 3, [[176, 4], [1, 128]])
                        i_ap = A(pxb, 0, [[128, 4], [1, 128]])
                    kb.op("act", lambda e, o_ap=o_ap, i_ap=i_ap: e.copy(out=o_ap, in_=i_ap), reads=[pxb], writes=[XB[b3]], first=[])
                    kb.psrel(pxb)
                if not sample:
                    kb.op("pool", lambda e: e.tensor_copy(out=carry[:, :, :], in_=A(xb, 128, [[176, 12], [1, 3]])),
                          reads=XB, writes=[carry])
            pq = kb.psalloc()
            mm_feat(pq, 4, win, lambda k, c: win[:, k, C_Q + c * 128:C_Q + (c + 1) * 128])
            kb.op("act", lambda e: e.copy(out=qT[:, :, :].rearrange("p a b -> p (a b)"), in_=pq[:, :]),
                  reads=[pq], writes=[qT])
            kb.psrel(pq)
            pgx = kb.psalloc()
            mm_feat(pgx, 4, win, lambda k, c: win[:, k, C_GX + c * 128:C_GX + (c + 1) * 128])
            kb.op("act", lambda e: e.activation(out=sgxT[:, :, :].rearrange("p a b -> p (a b)"), in_=pgx[:, :], func=AF.Silu),
                  reads=[pgx], writes=[sgxT])
            kb.psrel(pgx)

            if not sample:
                psc = [kb.psalloc(), kb.psalloc()]
                for hp in range(2):
                    def scmm(e, hp=hp):
                        ins = None
                        for hh in range(2):
                            h = hp * 2 + hh
                            for mc in range(2):
                                ins = e.matmul(psc[hp][:, (hh * 2 + mc) * 128:(hh * 2 + mc + 1) * 128],
                                               lhsT=ktl[:, h, mc * 128:(mc + 1) * 128], rhs=qT[:, h, :], start=True, stop=True)
                        return ins
                    kb.op("pe", scmm, reads=[ktl, qT], writes=[psc[hp]])
                PT = [kb.nxt("E"), kb.nxt("E")]
                for hp in range(2):
                    kb.op("act", lambda e, hp=hp: e.activation(out=PT[hp][:, :, :].rearrange("p a b -> p (a b)"),
                                                               in_=psc[hp][:, :], func=AF.Exp, scale=SCALE),
                          reads=[psc[hp]], writes=[PT[hp]])
                kb.psrel(psc[0], psc[1])
                psum_ = kb.psalloc()
                pxa = kb.psalloc()

                def summm(e):
                    ins = None
                    for h in range(4):
                        for mc in range(2):
                            ins = e.matmul(psum_[:, h * 128:(h + 1) * 128], lhsT=onesb, rhs=PT[h // 2][:, (h % 2) * 2 + mc, :],
                                           start=(mc == 0), stop=(mc == 1))
                    return ins
                kb.op("pe", summm, reads=[cb, PT[0], PT[1]], writes=[psum_])

                def pvmm(e):
                    ins = None
                    for h in range(4):
                        for mc in range(2):
                            ins = e.matmul(pxa[:, h * 128:(h + 1) * 128], lhsT=vl[:, mc, h * 128:(h + 1) * 128],
                                           rhs=PT[h // 2][:, (h % 2) * 2 + mc, :], start=(mc == 0), stop=(mc == 1))
                    return ins
                kb.op("pe", pvmm, reads=[vl, PT[0], PT[1]], writes=[pxa])

            for ch in range(12):
                gi = ch // 4
                if sample:
                    def xin(k, ch=ch):
                        return A(xb, ch * 176 + k, [[11, 16], [1, 8]])
                    tout = A(big6, ch * 128, [[8, 16], [1, 8]])
                else:
                    def xin(k, ch=ch):
                        return A(xb, ch * 176 + k, [[1, 128]])
                    tout = A(big6, ch * 128, [[1, 128]])
                if ch < 8:
                    kb.op("dve", lambda e, xin=xin, tout=tout, ch=ch: e.tensor_scalar_mul(
                        out=tout, in0=xin(0), scalar1=cpcol(l, CP_CW + ch * 4 + 0)),
                        reads=[XB[gi], colpar], writes=[BIG[gi]], first=None if ch % 4 == 0 else [])
                    for k in range(1, 4):
                        kb.op("dve", lambda e, xin=xin, tout=tout, ch=ch, k=k: e.scalar_tensor_tensor(
                            out=tout, in0=xin(k), scalar=cpcol(l, CP_CW + ch * 4 + k), in1=tout, op0=ALU.mult, op1=ALU.add),
                            reads=[XB[gi], colpar, BIG[gi]], writes=[BIG[gi]], first=[])
                else:
                    bdims = [[0, 16], [0, 8]] if sample else [[0, 128]]
                    t2 = A(ctmp, 0, [[8, 16], [1, 8]]) if sample else ctmp[:, :]

                    def wbc(k, ch=ch, bdims=bdims):
                        return A(colpar, l * NCP + CP_CW + ch * 4 + k, bdims)
                    kb.op("pool", lambda e, xin=xin, tout=tout, wbc=wbc: e.tensor_tensor(out=tout, in0=xin(0), in1=wbc(0), op=ALU.mult),
                          reads=[XB[gi], colpar], writes=[BIG[gi]], first=None if ch % 4 == 0 else [])
                    for k in range(1, 4):
                        kb.op("pool", lambda e, xin=xin, t2=t2, wbc=wbc, k=k: e.tensor_tensor(out=t2, in0=xin(k), in1=wbc(k), op=ALU.mult),
                              reads=[XB[gi], colpar], writes=[ctmp])
                        kb.op("pool", lambda e, tout=tout, t2=t2: e.tensor_tensor(out=tout, in0=tout, in1=t2, op=ALU.add),
                              reads=[BIG[gi], ctmp], writes=[BIG[gi]], first=[])
                kb.op("act", lambda e, ch=ch: e.activation(out=xbcsT[:, ch, :], in_=big6[:, ch * 128:(ch + 1) * 128], func=AF.Silu,
                                                           bias=cpcol(l, CP_CB + ch)),
                      reads=[BIG[gi], colpar], writes=[XS[gi]], first=None if ch % 4 == 0 else [])

            def attn_tail(psum_, pxa):
                if sample:
                    kb.op("dve", lambda e: e.reciprocal(out=A(Rb, 0, [[128, 4], [8, 16], [1, 8]]),
                                                        in_=A(psum_, 0, [[8, 4], [32, 16], [1, 8]])), reads=[psum_], writes=[Rb])
                else:
                    kb.op("dve", lambda e: e.reciprocal(out=Rb[:, :, :].rearrange("p a b -> p (a b)"), in_=psum_[:, :]),
                          reads=[psum_], writes=[Rb])
                kb.op("dve", lambda e: e.tensor_tensor(out=Rb[:, :, :].rearrange("p a b -> p (a b)"),
                                                       in0=pxa[:, :], in1=Rb[:, :, :].rearrange("p a b -> p (a b)"), op=ALU.mult),
                      reads=[pxa, Rb], writes=[Rb])
                kb.psrel(psum_, pxa)
                kb.op("pool", lambda e: e.tensor_tensor(out=actT[:, 12:16, :], in0=Rb[:, :, :], in1=sgxT[:, :, :], op=ALU.mult),
                      reads=[Rb, sgxT], writes=[actT], first=actT_first())
            if not sample:
                attn_tail(psum_, pxa)

            pu = kb.psalloc()
            mm_tok(pu, 512, win, lambda k: win[:, k, C_U:C_U + 512])
            ub = kb.nxt("ubf")
            kb.op("act", lambda e: e.copy(out=ub[:, :], in_=pu[:, :]), reads=[pu], writes=[ub])
            if special:
                kb.op("act", lambda e: e.copy(out=scr[:, 0:512], in_=pu[:, :]), reads=[pu], writes=[scr])
                if sample:
                    for s_ in range(NS):
                        kb.dma("sp", o_pool_s[l, s_, 7:15, :], scr[s_ * 8:(s_ + 1) * 8, 0:512], reads=[scr],
                               writes=[o_pool_s], sembuf=scr)
                else:
                    kb.dma("sp", o_pool_p[l, :, :], scr[113:128, 0:512], reads=[scr], writes=[o_pool_p], sembuf=scr)
            kb.psrel(pu)
            pg = kb.psalloc()
            mm_feat(pg, 4, win, lambda k, c: win[:, k, C_GP + c * 128:C_GP + (c + 1) * 128])
            kb.op("act", lambda e: e.activation(out=sgpT[:, :, :].rearrange("p a b -> p (a b)"), in_=pg[:, :], func=AF.Silu),
                  reads=[pg], writes=[sgpT])
            kb.psrel(pg)
            for half in range(2):
                pz = kb.psalloc()
                mm_tok(pz, 512, win, lambda k, half=half: win[:, k, C_Z + half * 512:C_Z + (half + 1) * 512])
                kb.op("act", lambda e, half=half, pz=pz: e.activation(out=sz[:, half * 512:(half + 1) * 512], in_=pz[:, :],
                                                                      func=AF.Silu),
                      reads=[pz], writes=[sz], first=None if half == 0 else [])
                kb.psrel(pz)
            pd = kb.psalloc()
            mm_tok(pd, 16, win, lambda k: win[:, k, C_DT:C_DT + 16])
            dt = smt(0)
            kb.op("dve", lambda e: e.tensor_tensor(out=dt, in0=pd[:, 0:16], in1=rowpar[:, rp:rp + 16], op=ALU.add),
                  reads=[pd, rowpar], writes=[sm])
            kb.psrel(pd)
            kb.op("act", lambda e: e.activation(out=dt, in_=dt, func=AF.Exp), reads=[sm], writes=[sm])
            kb.op("act", lambda e: e.activation(out=dt, in_=dt, func=AF.Ln, bias=1.0), reads=[sm], writes=[sm])
            if special:
                for part in range(3):
                    px = kb.psalloc()
                    mm_tok(px, 512, win, lambda k, part=part: win[:, k, C_XBC + part * 512:C_XBC + (part + 1) * 512])
                    kb.op("act", lambda e, px=px: e.copy(out=scr[:, 512:1024], in_=px[:, :]), reads=[px], writes=[scr])
                    kb.psrel(px)
                    if sample:
                        for s_ in range(NS):
                            kb.dma("sp", o_conv_s[l, s_, :, part * 512:(part + 1) * 512],
                                   scr[s_ * 8 + 5:s_ * 8 + 8, 512:1024], reads=[scr], writes=[o_conv_s], sembuf=scr)
                    else:
                        kb.dma("sp", o_conv_p[l, :, part * 512:(part + 1) * 512], scr[125:128, 512:1024], reads=[scr],
                               writes=[o_conv_p], sembuf=scr)
            if sample and l < LL - 1:
                load_win(l + 1)

            pdT = kb.psalloc()

            def poolmm(e):
                ins = None
                for g in range(4):
                    o = pdT[:, g * 128:(g + 1) * 128]
                    ul = ub[:, g * 128:(g + 1) * 128]
                    if sample:
                        e.matmul(o, lhsT=ul, rhs=band(g, 3), start=True, stop=False)
                        e.matmul(o, lhsT=spool16[0:120, 0, g * 128:(g + 1) * 128], rhs=band(g, 4)[0:120, :],
                                 start=False, stop=False)
                        ins = e.matmul(o, lhsT=spool16[0:120, 1, g * 128:(g + 1) * 128], rhs=band(g, 5)[0:120, :],
                                       start=False, stop=True)
                    elif ti == 0:
                        ins = e.matmul(o, lhsT=ul, rhs=band(g, 2), start=True, stop=True)
                    else:
                        e.matmul(o, lhsT=ul, rhs=band(g, 0), start=True, stop=False)
                        ins = e.matmul(o, lhsT=uprev[64:128, g * 128:(g + 1) * 128], rhs=band(g, 1)[64:128, :],
                                       start=False, stop=True)
                return ins
            if sample:
                kb.dma("pool", spool16[0:120, 0, :], st_pool[l, 0:120, :], writes=[spool16], sembuf=spool16)
                kb.dma("pool", spool16[0:120, 1, :], st_pool[l, 120:240, :], writes=[spool16], first=[], sembuf=spool16)
                kb.dma("sp", o_pool_s[l, :, 0:7, :], st_pool[l, :, :].rearrange("(s r) c -> s r c", r=15)[:, 8:15, :],
                       writes=[o_pool_s], sembuf=d2d)
                kb.op("pe", poolmm, reads=[ub, cb, spool16], writes=[pdT])
            elif ti == 0:
                kb.op("pe", poolmm, reads=[ub, cb], writes=[pdT])
            else:
                uprev = tile_state["uprev"]
                kb.op("pe", poolmm, reads=[ub, cb, uprev], writes=[pdT])
            tile_state["uprev"] = ub
            for g in range(4):
                if (not sample) and ti == 0:
                    kb.op("dve", lambda e, g=g: e.tensor_tensor(out=dT[:, g, :], in0=pdT[:, g * 128:(g + 1) * 128],
                                                                in1=cf[:, CF_IC0 + g * 128:CF_IC0 + (g + 1) * 128], op=ALU.mult),
                          reads=[pdT, cf], writes=[dT], first=None if g == 0 else [])
                else:
                    kb.op("act", lambda e, g=g: e.mul(out=dT[:, g, :], in_=pdT[:, g * 128:(g + 1) * 128], mul=1.0 / POOL_W[g]),
                          reads=[pdT], writes=[dT], first=None if g == 0 else [])
            kb.psrel(pdT)
            py = kb.psalloc()

            def poolmm2(e):
                ins = None
                for g in range(4):
                    ins = e.matmul(py[:, g * 128:(g + 1) * 128], lhsT=poolw[:, g, :], rhs=dT[:, g, :], start=True, stop=True)
                return ins
            kb.op("pe", poolmm2, reads=[poolw, dT], writes=[py])
            for g in range(4):
                kb.op("dve", lambda e, g=g: e.scalar_tensor_tensor(out=actT[:, g, :], in0=py[:, g * 128:(g + 1) * 128],
                                                                   scalar=cpcol(l, CP_PS + g), in1=sgpT[:, g, :],
                                                                   op0=ALU.mult, op1=ALU.mult),
                      reads=[py, colpar, sgpT], writes=[actT], first=actT_first())
            kb.psrel(py)
            if sample and l < LL - 1:
                load_poolw(l + 1)

            tri = cf[:, (CF_TRIS if sample else CF_TRI):(CF_TRIS if sample else CF_TRI) + 128]
            onem = cf[:, (CF_BONE if sample else CF_ONE):(CF_BONE if sample else CF_ONE) + 128]
            su = cf[:, CF_SU:CF_SU + 128]
            a_ = smt(1)
            cum = smt(2)
            e_ = smt(3)
            wend = smt(4)
            dec = smt(5)
            kb.op("dve", lambda e: e.tensor_tensor(out=a_, in0=dt, in1=abc[:, l * 16:(l + 1) * 16], op=ALU.mult),
                  reads=[sm, abc], writes=[sm], first=[])
            pcm = kb.psalloc()

            def cummm(e):
                e.matmul(pcm[:, 0:16], lhsT=tri, rhs=a_, start=True, stop=True)
                return e.matmul(pcm[:, 16:32], lhsT=onem, rhs=a_, start=True, stop=True)
            kb.op("pe", cummm, reads=[cf, sm], writes=[pcm])
            kb.op("act", lambda e: e.copy(out=cum, in_=pcm[:, 0:16]), reads=[pcm], writes=[sm], first=[])
            kb.op("act", lambda e: e.activation(out=e_, in_=pcm[:, 0:16], func=AF.Exp), reads=[pcm], writes=[sm], first=[])
            kb.op("act", lambda e: e.activation(out=dec, in_=pcm[:, 16:32], func=AF.Exp), reads=[pcm], writes=[sm], first=[])
            kb.op("dve", lambda e: e.tensor_tensor(out=wend, in0=pcm[:, 16:32], in1=cum, op=ALU.subtract),
                  reads=[pcm, sm], writes=[sm], first=[])
            kb.psrel(pcm)
            kb.op("act", lambda e: e.activation(out=wend, in_=wend, func=AF.Exp), reads=[sm], writes=[sm], first=[])
            kb.op("dve", lambda e: e.tensor_tensor(out=wend, in0=wend, in1=dt, op=ALU.mult), reads=[sm], writes=[sm], first=[])

            ptx = kb.psalloc()
            ptxb = ptx[:, :].bitcast(BF16)

            def trx(e):
                ins = None
                for c in range(8):
                    ins = e.transpose(out=ptxb[:, c * 128:(c + 1) * 128], in_=xbcsT[:, c, :], identity=identb)
                return ins
            kb.op("pe", trx, reads=[XS[0], XS[1], cb], writes=[ptx])
            ptx3 = ptxb.rearrange("p (h d) -> p h d", h=16)

            def bc16(off):
                return A(sm, off, [[1, 16], [0, 64]])
            kb.op("dve", lambda e: e.tensor_tensor(out=xdt[:, :].rearrange("p (h d) -> p h d", h=16), in0=ptx3,
                                                   in1=bc16(0), op=ALU.mult), reads=[ptx, sm], writes=[xdt])
            kb.op("dve", lambda e: e.tensor_tensor(out=xw[:, :].rearrange("p (h d) -> p h d", h=16), in0=ptx3,
                                                   in1=bc16(4 * 16), op=ALU.mult), reads=[ptx, sm], writes=[xw])
            kb.op("dve", lambda e: e.tensor_tensor(out=xd[:, :].rearrange("p (h d) -> p h d", h=16), in0=ptx3,
                                                   in1=A(rowpar, rp + 32, [[1, 16], [0, 64]]), op=ALU.mult),
                  reads=[ptx, rowpar], writes=[xd])
            kb.psrel(ptx)
            ptb_ = kb.psalloc()
            ptbb = ptb_[:, :].bitcast(BF16)

            def trb(e):
                e.transpose(out=ptbb[:, 0:128], in_=xbcsT[:, 8, :], identity=identb)
                return e.transpose(out=ptbb[:, 128:256], in_=xbcsT[:, 9, :], identity=identb)
            kb.op("pe", trb, reads=[XS[2], cb], writes=[ptb_])
            kb.op("act", lambda e: e.copy(out=btok[:, :], in_=ptbb[:, 0:256]), reads=[ptb_], writes=[btok])
            kb.psrel(ptb_)
            pcb = kb.psalloc()

            def cbmm(e):
                e.matmul(pcb[:, 0:128], lhsT=xbcsT[:, 8, :], rhs=xbcsT[:, 10, :], start=True, stop=True)
                return e.matmul(pcb[:, 128:256], lhsT=xbcsT[:, 9, :], rhs=xbcsT[:, 11, :], start=True, stop=True)
            kb.op("pe", cbmm, reads=[XS[2]], writes=[pcb])
            tri_off = CF_TRIS if sample else CF_TRI
            kb.op("dve", lambda e: e.tensor_tensor(out=cbm[:, :, :], in0=pcb[:, 0:256].rearrange("p (g i) -> p g i", g=2),
                                                   in1=A(cf, tri_off, [[0, 2], [1, 128]]), op=ALU.mult),
                  reads=[pcb, cf], writes=[cbm])
            kb.psrel(pcb)
            if sample:
                pyit = sample_seq_phase(l)
            pY = [kb.psalloc(), kb.psalloc()]
            for g in range(2):
                kb.op("pe", lambda e, g=g: e.matmul(pY[g][:, :], lhsT=identb, rhs=xd[:, g * 512:(g + 1) * 512],
                                                    start=True, stop=False), reads=[cb, xd], writes=[pY[g]])
            stage = {}

            def intra_a(q4):
                if q4 % 2 == 0:
                    Rbuf, R3, R2 = Rb, Rb[:, :, :], Rb[:, :, :].rearrange("p a b -> p (a b)")
                else:
                    Rbuf, R3, R2 = BIG[2], A(big6, 1024, [[128, 4], [1, 128]]), big6[:, 1024:1536]
                kb.op("pool", lambda e: e.tensor_tensor(out=R3, in0=A(sm, 16 + q4 * 4, [[1, 4], [0, 128]]),
                                                        in1=A(cf, tri_off, [[0, 4], [1, 128]]), op=ALU.mult),
                      reads=[sm, cf], writes=[Rbuf])
                pseg = kb.psalloc()
                kb.op("pe", lambda e: e.matmul(pseg[:, :], lhsT=su, rhs=R2, start=True, stop=True),
                      reads=[cf, Rbuf], writes=[pseg])
                stage[q4] = pseg

            def intra_b(q4):
                pseg = stage.pop(q4)
                Eb = kb.nxt("E")
                kb.op("act", lambda e: e.activation(out=Eb[:, :, :].rearrange("p a b -> p (a b)"),
                                                    in_=pseg[:, :], func=AF.Exp), reads=[pseg], writes=[Eb])
                kb.psrel(pseg)
                MTb = kb.nxt("MT")
                g = q4 // 2
                kb.op("dve", lambda e: e.tensor_tensor(
                    out=MTb[:, :, :], in0=Eb[:, :, :], in1=A(cbm, g * 128, [[0, 4], [1, 128]]), op=ALU.mult),
                    reads=[Eb, cbm], writes=[MTb])

                def ymm(e):
                    ins = None
                    for hh in range(4):
                        h = q4 * 4 + hh
                        o = pY[h // 8][:, (h % 8) * 64:(h % 8 + 1) * 64]
                        ins = e.matmul(o, lhsT=MTb[:, hh, :], rhs=xdt[:, h * 64:(h + 1) * 64], start=False, stop=(h % 8 == 7))
                    return ins
                kb.op("pe", ymm, reads=[MTb, xdt], writes=[pY[q4 // 2]], first=[])
            pf = (not sample) and nxt_tile is not None and (not nxt_tile[2])
            pfb = []
            if pf:
                hb_n = norm_stage(*nxt_tile)
                kb.op("pool", lambda e: e.tensor_copy(out=A(xb, 0, [[176, 12], [1, 3]]), in_=carry[:, :, :]),
                      reads=[carry], writes=XB)
            intra_a(0)
            for q4 in range(4):
                if q4 + 1 < 4:
                    intra_a(q4 + 1)
                if pf and q4 < 3:
                    pxb = kb.psalloc()
                    mm_feat(pxb, 4, win, lambda k, c, b3=q4: win[:, k, C_XBC + (b3 * 4 + c) * 128:C_XBC + (b3 * 4 + c + 1) * 128])
                    pfb.append(pxb)
                intra_b(q4)
            if pf:
                for b3, pxb in enumerate(pfb):
                    kb.op("act", lambda e, b3=b3, pxb=pxb: e.copy(out=A(xb, b3 * 4 * 176 + 3, [[176, 4], [1, 128]]),
                                                                  in_=A(pxb, 0, [[128, 4], [1, 128]])),
                          reads=[pxb], writes=[XB[b3]], first=[])
                    kb.psrel(pxb)
                kb.op("pool", lambda e: e.tensor_copy(out=carry[:, :, :], in_=A(xb, 128, [[176, 12], [1, 3]])),
                      reads=XB, writes=[carry])
                tile_state["pre"] = (hb_n, True)

            if sample:
                for g in range(2):
                    kb.op("act" if g == 0 else "dve",
                          (lambda e, g=g: e.copy(out=scr[:, g * 512:(g + 1) * 512], in_=pyit[g][:, :])) if g == 0 else
                          (lambda e, g=g: e.tensor_copy(out=scr[:, g * 512:(g + 1) * 512], in_=pyit[g][:, :])),
                          reads=[pyit[g]], writes=[scr], first=None if g == 0 else [])
                kb.psrel(pyit[0], pyit[1])
            pYI = [kb.psalloc(), kb.psalloc()]
            if sample:
                for g in range(2):
                    def tryi(e, g=g):
                        ins = None
                        for c in range(4):
                            cc = g * 4 + c
                            ins = e.transpose(out=pYI[g][:, c * 128:(c + 1) * 128], in_=scr[:, cc * 128:(cc + 1) * 128],
                                              identity=identf)
                        return ins
                    kb.op("pe", tryi, reads=[scr, cf], writes=[pYI[g]])
            if not sample:
                ST = Sb[0]
                if ti == 0:
                    kb.op("pool", lambda e: e.memset(ST[:, :], 0.0), writes=[ST])
                    kb.op("pool", lambda e: e.memset(stb16[:, :], 0.0), writes=[stb16])
                for g in range(2):
                    kb.op("pe", lambda e, g=g: e.matmul(pYI[g][:, :], lhsT=xbcsT[:, 10 + g, :],
                                                        rhs=stb16[:, g * 512:(g + 1) * 512], start=True, stop=True),
                          reads=[XS[2], stb16], writes=[pYI[g]])

            y = big6
            YB = [BIG[0], BIG[1]]
            for g in range(2):
                kb.op("dve", lambda e, g=g: e.tensor_tensor(
                    out=A(y, g * 512, [[64, 8], [1, 64]]), in0=pYI[g][:, :].rearrange("p (h d) -> p h d", h=8),
                    in1=A(sm, 3 * 16 + g * 8, [[1, 8], [0, 64]]), op=ALU.mult),
                    reads=[pYI[g], sm], writes=[YB[g]])
            for g in range(2):
                kb.op("dve", lambda e, g=g: e.tensor_tensor(out=y[:, g * 512:(g + 1) * 512], in0=y[:, g * 512:(g + 1) * 512],
                                                            in1=pY[g][:, :], op=ALU.add), reads=[YB[g], pY[g]], writes=[YB[g]])
            kb.psrel(pYI[0], pYI[1], pY[0], pY[1])
            for g in range(2):
                kb.op("pool", lambda e, g=g: e.tensor_tensor(out=y[:, g * 512:(g + 1) * 512], in0=y[:, g * 512:(g + 1) * 512],
                                                             in1=sz[:, g * 512:(g + 1) * 512], op=ALU.mult),
                      reads=[YB[g], sz], writes=[YB[g]])
            ynb = xn
            for g in range(2):
                kb.op("act", lambda e, g=g: e.activation(out=ynb[:, g * 512:(g + 1) * 512], in_=y[:, g * 512:(g + 1) * 512],
                                                         func=AF.Square, accum_out=sm2[:, 2 + g:3 + g]),
                      reads=[YB[g]], writes=[ynb, sm2], first=None if g == 0 else [])
            kb.op("act", lambda e: e.activation(out=sm2[:, 4:6], in_=sm2[:, 2:4], func=AF.Ln, scale=1.0 / 512, bias=EPS),
                  reads=[sm2], writes=[sm2], first=[])
            kb.op("act", lambda e: e.activation(out=sm2[:, 4:6], in_=sm2[:, 4:6], func=AF.Exp, scale=-0.5),
                  reads=[sm2], writes=[sm2], first=[])
            for g in range(2):
                kb.op("dve", lambda e, g=g: e.tensor_scalar_mul(
                    out=ynb[:, g * 512:(g + 1) * 512], in0=y[:, g * 512:(g + 1) * 512], scalar1=sm2[:, 4 + g:5 + g]),
                    reads=[YB[g], sm2], writes=[ynb], first=None if g == 0 else [])
            pyt = kb.psalloc()
            pytb = pyt[:, :].bitcast(BF16)

            def try_(e):
                ins = None
                for c in range(8):
                    ins = e.transpose(out=pytb[:, c * 128:(c + 1) * 128], in_=ynb[:, c * 128:(c + 1) * 128], identity=identb)
                return ins
            kb.op("pe", try_, reads=[ynb, cb], writes=[pyt])
            kb.op("dve", lambda e: e.tensor_tensor(out=actT[:, 4:12, :], in0=pytb[:, :].rearrange("p (c t) -> p c t", c=8),
                                                   in1=A(colpar, l * NCP + CP_SG, [[1, 8], [0, 128]]), op=ALU.mult),
                  reads=[pyt, colpar], writes=[actT], first=actT_first())
            kb.psrel(pyt)

            if nxt_tile is not None and not pf:
                tile_state["pre"] = (norm_stage(*nxt_tile), False)

            if not sample:
                ST = Sb[0]
                pup = [kb.psalloc(), kb.psalloc()]
                for g in range(2):
                    kb.op("pe", lambda e, g=g: e.matmul(pup[g][:, :], lhsT=btok[:, g * 128:(g + 1) * 128],
                                                        rhs=xw[:, g * 512:(g + 1) * 512], start=True, stop=True),
                          reads=[btok, xw], writes=[pup[g]])
                kb.op("pool", lambda e: e.tensor_tensor(out=ST[:, :].rearrange("p (h d) -> p h d", h=16),
                                                        in0=ST[:, :].rearrange("p (h d) -> p h d", h=16),
                                                        in1=A(sm, 5 * 16, [[1, 16], [0, 64]]), op=ALU.mult),
                      reads=[ST, sm], writes=[ST])
                for g in range(2):
                    kb.op("dve", lambda e, g=g: e.tensor_tensor(out=ST[:, g * 512:(g + 1) * 512], in0=ST[:, g * 512:(g + 1) * 512],
                                                                in1=pup[g][:, :], op=ALU.add), reads=[ST, pup[g]], writes=[ST])
                kb.psrel(pup[0], pup[1])
                if last_p:
                    for half in range(2):
                        pt_ = kb.psalloc()

                        def trs(e, pt_=pt_, half=half):
                            ins = None
                            for c in range(4):
                                cc = half * 4 + c
                                ins = e.transpose(out=pt_[:, c * 128:(c + 1) * 128], in_=ST[:, cc * 128:(cc + 1) * 128],
                                                  identity=identf)
                            return ins
                        kb.op("pe", trs, reads=[ST, cf], writes=[pt_])
                        kb.op("act", lambda e, pt_=pt_: e.copy(out=scr[:, 0:512], in_=pt_[:, :]), reads=[pt_], writes=[scr])
                        kb.psrel(pt_)
                        kb.dma("sp", o_ssm_p[l, half * 512:(half + 1) * 512, :].rearrange("(c p) n -> p c n", p=128),
                               scr[:, 0:512].rearrange("p (c n) -> p c n", c=4), reads=[scr], writes=[o_ssm_p], sembuf=scr)
                else:
                    kb.op("act", lambda e: e.copy(out=stb16[:, :], in_=ST[:, :]), reads=[ST], writes=[stb16])
            else:
                attn_tail(tile_state["psum_"], tile_state["pxa"])

            po = [kb.psalloc(), kb.psalloc()]
            for n in range(2):
                def omm(e, n=n):
                    ins = None
                    for k in range(16):
                        ins = e.matmul(po[n][:, :], lhsT=actT[:, k, :], rhs=wout[:, k, n * 512:(n + 1) * 512],
                                       start=(k == 0), stop=(k == 15))
                    return ins
                kb.op("pe", omm, reads=[actT, wout], writes=[po[n]])
            for n in range(2):
                kb.op("dve", lambda e, n=n: e.tensor_tensor(out=hb[:, n * 512:(n + 1) * 512], in0=hb[:, n * 512:(n + 1) * 512],
                                                            in1=po[n][:, :], op=ALU.add), reads=[hb, po[n]], writes=[hb])
            kb.psrel(po[0], po[1])
            if sample and l < LL - 1:
                load_wout(l + 1)
                load_attn(l + 1)
            if l < LL - 1:
                kb.dma("sp", hs[hidx * 128:(hidx + 1) * 128, :], hb[:, :], reads=[hb], writes=[hsB[hidx]], sembuf=hb)
            else:
                ss = sm2[:, 8:9]
                rs = sm2[:, 9:10]
                fj = scr if not special else xd
                kb.op("act", lambda e: e.activation(out=fj[:, 0:1024], in_=hb[:, :], func=AF.Square, accum_out=ss),
                      reads=[hb], writes=[fj, sm2n])
                kb.op("act", lambda e: e.activation(out=rs, in_=ss, func=AF.Ln, scale=1.0 / D, bias=EPS),
                      reads=[sm2n], writes=[sm2n])
                kb.op("act", lambda e: e.activation(out=rs, in_=rs, func=AF.Exp, scale=-0.5), reads=[sm2n], writes=[sm2n])
                kb.op("dve", lambda e: e.scalar_tensor_tensor(out=hb[:, :], in0=hb[:, :], scalar=rs, in1=fng[:, :],
                                                              op0=ALU.mult, op1=ALU.mult), reads=[hb, sm2n, fng], writes=[hb])
                dst = ys[:, :] if sample else yp[ti * 128:(ti + 1) * 128, :]
                kb.dma("sp", dst, hb[:, :], reads=[hb], writes=[ys if sample else yp], sembuf=hb)

        def sample_seq_phase(l):
            abcast = big6
            kb.op("pool", lambda e: e.tensor_copy(out=abcast[:, 0:1024].rearrange("p (h d) -> p h d", h=16),
                                                  in_=A(sm, 16, [[1, 16], [0, 64]])), reads=[sm], writes=[abcast])
            pdc = kb.psalloc()

            def dcmm(e):
                ins = None
                for c in range(8):
                    ins = e.matmul(pdc[:, c * 16:(c + 1) * 16], lhsT=abcast[:, c * 128:(c + 1) * 128],
                                   rhs=cf[:, CF_BM:CF_BM + 16], start=True, stop=True)
                return ins
            kb.op("pe", dcmm, reads=[abcast, cf], writes=[pdc])
            kb.op("act", lambda e: e.activation(out=decT[:, :, :].rearrange("p a b -> p (a b)"), in_=pdc[:, 0:128], func=AF.Exp),
                  reads=[pdc], writes=[decT])
            kb.psrel(pdc)
            pyit = [kb.psalloc(), kb.psalloc()]
            psum_ = kb.psalloc()
            pxa = kb.psalloc()
            tile_state["psum_"], tile_state["pxa"] = psum_, pxa
            for s_ in range(NS):
                c0 = s_ * 8
                Sin = kb.nxt("Sb")
                kb.dma("sp", Sin[:, :].rearrange("p (c n) -> p c n", c=8),
                       st_ssm[l, s_, :, :].rearrange("(c p) n -> p c n", p=128), writes=[Sin], sembuf=Sin)
                for half in range(2):
                    pts_ = kb.psalloc()

                    def trs(e, pts_=pts_, half=half):
                        ins = None
                        for c in range(4):
                            cc = half * 4 + c
                            ins = e.transpose(out=pts_[:, c * 128:(c + 1) * 128], in_=Sin[:, cc * 128:(cc + 1) * 128],
                                              identity=identf)
                        return ins
                    kb.op("pe", trs, reads=[Sin, cf], writes=[pts_])
                    kb.op("act" if half == 0 else "dve",
                          (lambda e, pts_=pts_, half=half: e.copy(out=stb16[:, half * 512:(half + 1) * 512], in_=pts_[:, :]))
                          if half == 0 else
                          (lambda e, pts_=pts_, half=half: e.tensor_copy(out=stb16[:, half * 512:(half + 1) * 512], in_=pts_[:, :])),
                          reads=[pts_], writes=[stb16], first=None if half == 0 else [])
                    kb.psrel(pts_)

                def yimm(e):
                    ins = None
                    for c in range(8):
                        ins = e.matmul(pyit[c // 4][:, (c % 4) * 128 + c0:(c % 4) * 128 + c0 + 8],
                                       lhsT=stb16[:, c * 128:(c + 1) * 128], rhs=xbcsT[:, 10 + c // 4, c0:c0 + 8],
                                       start=True, stop=True)
                    return ins
                kb.op("pe", yimm, reads=[stb16, xbcsT], writes=[pyit[0], pyit[1]], first=None if s_ == 0 else [])
                Bs = kb.nxt("Bs")
                kb.op("pool", lambda e, Bs=Bs, s_=s_: e.tensor_scalar_mul(out=Bs[:, :], in0=btok[:, :],
                                                                      scalar1=cf[:, CF_BM + s_:CF_BM + s_ + 1]), reads=[btok, cf], writes=[Bs])
                pup = [kb.psalloc(), kb.psalloc()]

                def upmm(e, Bs=Bs, pup=pup):
                    ins = None
                    for c in range(8):
                        g = c // 4
                        ins = e.matmul(pup[g][:, (c % 4) * 128:(c % 4 + 1) * 128], lhsT=xw[:, c * 128:(c + 1) * 128],
                                       rhs=Bs[:, g * 128:(g + 1) * 128], start=True, stop=True)
                    return ins
                kb.op("pe", upmm, reads=[xw, Bs], writes=[pup[0], pup[1]])
                for c in range(8):
                    kb.op("dve", lambda e, c=c, s_=s_, Sin=Sin, pup=pup: e.scalar_tensor_tensor(
                        out=Sin[:, c * 128:(c + 1) * 128], in0=Sin[:, c * 128:(c + 1) * 128], scalar=decT[:, c, s_:s_ + 1],
                        in1=pup[c // 4][:, (c % 4) * 128:(c % 4 + 1) * 128], op0=ALU.mult, op1=ALU.add),
                        reads=[Sin, decT, pup[c // 4]], writes=[Sin])
                kb.psrel(pup[0], pup[1])
                kb.dma("sp", o_ssm_s[l, s_, :, :].rearrange("(c p) n -> p c n", p=128),
                       Sin[:, :].rearrange("p (c n) -> p c n", c=8), reads=[Sin], writes=[o_ssm_s], sembuf=Sin)
                Kb = kb.nxt("Kb")
                Vb = kb.nxt("Vb")
                kb.dma("pool", Kb[:, :, :], ck[l, s_, :, :].rearrange("(c p) n -> p c n", p=128), writes=[Kb], sembuf=Kb)
                kb.dma("pool", Vb[:, :, :], cv[l, s_, :, :].rearrange("(c p) n -> p c n", p=128), writes=[Vb], sembuf=Vb)
                pkt = kb.psalloc()
                pktb = pkt[:, :].bitcast(BF16)

                def trk(e, Kb=Kb):
                    ins = None
                    for h in range(4):
                        for mc in range(2):
                            ins = e.transpose(out=pktb[:, (h * 2 + mc) * 128:(h * 2 + mc + 1) * 128],
                                              in_=Kb[:, mc, h * 128:(h + 1) * 128], identity=identb)
                    return ins
                kb.op("pe", trk, reads=[Kb, cb], writes=[pkt])
                kb.op("act", lambda e: e.copy(out=ktl[:, :, :].rearrange("p h m -> p (h m)"), in_=pktb[:, :]),
                      reads=[pkt], writes=[ktl])
                kb.psrel(pkt)
                psc = kb.psalloc()

                def scmm(e):
                    ins = None
                    for h in range(4):
                        for mc in range(2):
                            ins = e.matmul(psc[:, mc * 32 + h * 8:mc * 32 + h * 8 + 8], lhsT=ktl[:, h, mc * 128:(mc + 1) * 128],
                                           rhs=qT[:, h, c0:c0 + 8], start=True, stop=True)
                    return ins
                kb.op("pe", scmm, reads=[ktl, qT], writes=[psc])
                kb.op("act", lambda e: e.activation(out=pts[:, :], in_=psc[:, 0:64], func=AF.Exp, scale=SCALE),
                      reads=[psc], writes=[pts])
                kb.psrel(psc)

                def smm(e, s_=s_, c0=c0, Vb=Vb):
                    o = psum_[:, s_ * 32:(s_ + 1) * 32]
                    e.matmul(o, lhsT=onesb, rhs=pts[:, 0:32], start=True, stop=False)
                    ins = e.matmul(o, lhsT=onesb, rhs=pts[:, 32:64], start=False, stop=True)
                    for h in range(4):
                        for mc in range(2):
                            ins = e.matmul(pxa[:, h * 128 + c0:h * 128 + c0 + 8], lhsT=Vb[:, mc, h * 128:(h + 1) * 128],
                                           rhs=pts[:, mc * 32 + h * 8:mc * 32 + h * 8 + 8], start=(mc == 0), stop=(mc == 1))
                    return ins
                kb.op("pe", smm, reads=[cb, pts, Vb], writes=[psum_, pxa], first=None if s_ == 0 else [])
            return pyit

        tile_state = {}
        d2d = Buf("d2d", None)
        order = []
        for l in range(cfg['layers']):
            for ti in range(cfg['tiles']):
                order.append((l, ti, False))
            if cfg['sample']:
                order.append((l, 0, True))
        if order:
            load_win(0)
            load_poolw(0)
            load_attn(0)
            load_wout(0)
        for i, (l, ti, smp) in enumerate(order):
            do_tile(l, ti, smp, order[i + 1] if i + 1 < len(order) else None)
        kb.finish(outs)
        print("total ops", kb.nops)
    return nc


def _constants():
    cf = np.zeros((128, NCF), np.float32)
    r = np.arange(128)
    cf[:, CF_ID:CF_ID + 128] = np.eye(128)
    tri = (r[:, None] <= r[None, :]).astype(np.float32)
    same = (r[:, None] // 8 == r[None, :] // 8).astype(np.float32)
    cf[:, CF_TRI:CF_TRI + 128] = tri
    cf[:, CF_TRIS:CF_TRIS + 128] = tri * same
    cf[:, CF_SU:CF_SU + 128] = (r[:, None] > r[None, :]).astype(np.float32)
    cf[:, CF_ONE:CF_ONE + 128] = 1.0
    cf[:, CF_BONE:CF_BONE + 128] = same
    cf[:, CF_BM:CF_BM + 16] = (r[:, None] // 8 == np.arange(16)[None, :]).astype(np.float32)
    cb = np.zeros((128, NCB), np.float32)
    cb[:, CB_ID:CB_ID + 128] = np.eye(128)
    cb[:, CB_ONE:CB_ONE + 128] = 1.0
    for g, w in enumerate(POOL_W):
        s = r[:, None]
        t = r[None, :]
        cur = ((s <= t) & (s > t - w)).astype(np.float32) - w * (s == t)
        prev = np.zeros((128, 128), np.float32)
        srel = s - 128
        prev[:, :] = ((srel > t - w)).astype(np.float32)
        prev[:64, :] = 0.0
        cnt0 = np.minimum(t + 1, w).astype(np.float32)
        cur0 = ((s <= t) & (s > t - w)).astype(np.float32) - cnt0 * (s == t)
        cf[:, CF_IC0 + g * 128:CF_IC0 + (g + 1) * 128] = np.broadcast_to(1.0 / cnt0, (128, 128))
        ss_, ts_ = s // 8, s % 8
        sc_, tc_ = t // 8, t % 8
        bs = ((ss_ == sc_) & (ts_ <= tc_) & (ts_ > tc_ - w)).astype(np.float32) - w * (s == t)
        sta = np.zeros((128, 128), np.float32)
        stbm = np.zeros((128, 128), np.float32)
        rows = np.arange(120)
        sq, rr = rows // 15, rows % 15
        for half, m in ((0, sta), (1, stbm)):
            m[:120, :] = ((sq[:, None] + 8 * half == sc_) & ((rr[:, None] - 15) > (tc_ - w))).astype(np.float32)
        for k, m in enumerate((cur, prev, cur0, bs, sta, stbm)):
            o = CB_BAND + (g * 6 + k) * 128
            cb[:, o:o + 128] = m
    return cf, cb


_NC_CACHE = {}
_DBG_CFG = None


def kernel(x_prompt, x_sample, mem_prompt, state_pool, state_conv, state_ssm, cache_mem_k, cache_mem_v,
           norm_g, w_in, pool_w, pool_scale, conv_w, conv_b, dt_bias, a_log, d_skip, ssd_norm_g,
           mem_norm_g, w_mem_k, w_mem_v, w_out, final_norm_g):
    f = lambda a: np.ascontiguousarray(np.asarray(a, dtype=np.float32))
    x_prompt, x_sample, mem_prompt = f(x_prompt), f(x_sample), f(mem_prompt)
    state_pool, state_conv, state_ssm = f(state_pool), f(state_conv), f(state_ssm)
    cache_mem_k, cache_mem_v = f(cache_mem_k), f(cache_mem_v)
    cf, cb = _constants()
    colpar = np.zeros((128, DEPTH, NCP), np.float32)
    for l in range(DEPTH):
        colpar[:, l, CP_NG:CP_NG + 8] = f(norm_g)[l].reshape(8, 128).T
        colpar[:, l, CP_MG:CP_MG + 8] = f(mem_norm_g)[l].reshape(8, 128).T
        colpar[:, l, CP_SG:CP_SG + 8] = f(ssd_norm_g)[l].reshape(8, 128).T
        colpar[:, l, CP_PS:CP_PS + 4] = f(pool_scale)[l].reshape(4, 128).T
        colpar[:, l, CP_CW:CP_CW + 48] = f(conv_w)[l].reshape(4, 12, 128).transpose(2, 1, 0).reshape(128, 48)
        colpar[:, l, CP_CB:CP_CB + 12] = f(conv_b)[l].reshape(12, 128).T
    colpar = np.ascontiguousarray(colpar.reshape(128, DEPTH * NCP))
    rowpar = np.ascontiguousarray(np.concatenate([f(dt_bias), f(a_log), f(d_skip)], axis=1).reshape(-1))
    shared = {
        "w_in": f(w_in), "w_out": f(w_out), "pool_w": f(pool_w), "w_mk": f(w_mem_k), "w_mv": f(w_mem_v),
        "colpar": colpar, "rowpar": rowpar, "fng": f(final_norm_g), "cstf": cf, "cstb": cb,
    }
    in_maps = []
    for c in range(NCORES):
        sl = slice(c * NS, (c + 1) * NS)
        m = dict(shared)
        m["xp"] = x_prompt[c]
        m["xs"] = np.ascontiguousarray(x_sample[sl].reshape(128, D))
        m["mem"] = mem_prompt[c]
        m["st_pool"] = np.ascontiguousarray(state_pool[:, sl].reshape(DEPTH, NS * 15, 512))
        m["st_conv"] = np.ascontiguousarray(state_conv[:, sl].reshape(DEPTH, NS * 3, 1536))
        m["st_ssm"] = np.ascontiguousarray(state_ssm[:, sl].reshape(DEPTH, NS, 1024, 128))
        m["ck"] = np.ascontiguousarray(cache_mem_k[:, sl].reshape(DEPTH, NS, 256, 512))
        m["cv"] = np.ascontiguousarray(cache_mem_v[:, sl].reshape(DEPTH, NS, 256, 512))
        in_maps.append(m)
    if "nc" not in _NC_CACHE:
        _NC_CACHE["nc"] = build_nc(_DBG_CFG)
    nc = _NC_CACHE["nc"]
    res = run_bass_kernel_spmd(nc, in_maps, core_ids=list(range(NCORES)))
    R = res.results
    g = lambda name, c: np.asarray(R[c][name], dtype=np.float32)
    y_prompt = np.stack([g("yp", c) for c in range(NCORES)]).reshape(8, 2048, D)
    y_sample = np.concatenate([g("ys", c).reshape(NS, 8, D) for c in range(NCORES)], axis=0)
    new_pool_p = np.stack([g("o_pool_p", c) for c in range(NCORES)], axis=1)
    new_conv_p = np.stack([g("o_conv_p", c) for c in range(NCORES)], axis=1)
    new_ssm_p = np.stack([g("o_ssm_p", c).reshape(DEPTH, 16, 64, 128) for c in range(NCORES)], axis=1)
    new_mk = np.stack([g("o_mk", c).reshape(DEPTH, 256, 4, 128) for c in range(NCORES)], axis=1)
    new_mv = np.stack([g("o_mv", c).reshape(DEPTH, 256, 4, 128) for c in range(NCORES)], axis=1)
    new_pool_s = np.concatenate([g("o_pool_s", c) for c in range(NCORES)], axis=1)
    new_conv_s = np.concatenate([g("o_conv_s", c) for c in range(NCORES)], axis=1)
    new_ssm_s = np.concatenate([g("o_ssm_s", c).reshape(DEPTH, NS, 16, 64, 128) for c in range(NCORES)], axis=1)
    return (y_prompt, y_sample, new_pool_p, new_conv_p, new_ssm_p, new_mk, new_mv, new_pool_s, new_conv_s, new_ssm_s)
```

```python
import contextlib
import math
import numpy as np
import concourse.bass as bass
import concourse.mybir as mybir
from concourse.bass_utils import run_bass_kernel_spmd

F32 = mybir.dt.float32
BF16 = mybir.dt.bfloat16
AF = mybir.ActivationFunctionType
ALU = mybir.AluOpType

NCORES = 8
DEPTH = 4
D = 1024
NT = 16
NS = 16
DIN = 4624
C_U, C_GP, C_Z, C_XBC, C_DT, C_Q, C_GX = 0, 512, 1024, 2048, 3584, 3600, 4112
EPS = 1e-6
POOL_W = (2, 4, 8, 16)
SEM_MAX = 3600

CP_NG, CP_MG, CP_SG, CP_PS, CP_CW, CP_CB, NCP = 0, 8, 16, 24, 28, 76, 88
CF_ID, CF_TRI, CF_TRIS, CF_SU, CF_ONE, CF_BONE, CF_BM, CF_IC0, NCF = 0, 128, 256, 384, 512, 640, 768, 784, 1296
CB_ID, CB_ONE, CB_BAND, NCB = 0, 128, 256, 256 + 24 * 128


class Buf:
    __slots__ = ("name", "t", "writes", "reads", "old", "dsem", "dval", "partial")

    def __init__(self, name, t, partial=False):
        self.name = name
        self.t = t
        self.writes = {}
        self.reads = {}
        self.old = {}
        self.dsem = None
        self.dval = 0
        self.partial = partial

    def __getitem__(self, k):
        return self.t[k]


def _merge(d, ev):
    for k, v in ev.items():
        if d.get(k, 0) < v:
            d[k] = v


class KB:
    def __init__(self, nc, stack):
        self.nc = nc
        self.stack = stack
        self.E = {"pe": nc.tensor, "act": nc.scalar, "dve": nc.vector, "pool": nc.gpsimd, "sp": nc.sync}
        self.sem, self.cnt, self.seen = {}, {}, {}
        for e in self.E:
            self.sem[e] = stack.enter_context(nc.semaphore("s_" + e))
            self.cnt[e] = 0
            self.seen[e] = {}
        self.nbuf = 0
        self.rr = {}
        self.psfree = []
        self.maxwaited = {}
        self.dmafinal = {}
        self.nops = 0
        self.limit = 1 << 60

    def sb(self, name, shape, dtype):
        self.nbuf += 1
        t = self.stack.enter_context(self.nc.sbuf_tensor(f"{name}_{self.nbuf}", list(shape), dtype))
        return Buf(name, t)

    def ps(self, name, shape, dtype=F32):
        self.nbuf += 1
        t = self.stack.enter_context(self.nc.psum_tensor(f"{name}_{self.nbuf}", list(shape), dtype))
        return Buf(name, t)

    def dram(self, name, shape, dtype, kind):
        t = self.nc.dram_tensor(name, list(shape), dtype, kind=kind)
        return Buf(name, t, partial=True)

    def pool(self, name, shape, dtype, n):
        bufs = [self.sb(f"{name}{i}", shape, dtype) for i in range(n)]
        self.rr[name] = [bufs, 0]
        return bufs

    def nxt(self, name):
        r = self.rr[name]
        b = r[0][r[1] % len(r[0])]
        r[1] += 1
        return b

    def psalloc(self):
        assert self.psfree, "out of PSUM banks"
        return self.psfree.pop(0)

    def psrel(self, *bs):
        for b in bs:
            self.psfree.append(b)

    def _isfirst(self, b, first):
        if b.partial:
            return False
        return first is None or b in first

    def _deps(self, reads, writes, first):
        dep = {}
        for b in reads:
            _merge(dep, b.writes)
        for b in writes:
            if self._isfirst(b, first):
                _merge(dep, b.writes)
                _merge(dep, b.reads)
            else:
                _merge(dep, b.old)
        return dep

    def _wait(self, e, dep):
        eng = self.E[e]
        seen = self.seen[e]
        for sem, val in dep.items():
            if seen.get(sem, 0) < val:
                eng.wait_ge(sem, val)
                seen[sem] = val
            if self.maxwaited.get(sem, 0) < val:
                self.maxwaited[sem] = val

    def _commit(self, ev, reads, writes, first):
        for b in writes:
            if self._isfirst(b, first):
                old = {}
                _merge(old, b.writes)
                _merge(old, b.reads)
                b.old = old
                b.writes = dict(ev)
                b.reads = {}
            else:
                _merge(b.writes, ev)
        for b in reads:
            if b in writes:
                continue
            _merge(b.reads, ev)

    def op(self, e, fn, reads=(), writes=(), first=None):
        self.nops += 1
        if self.nops > self.limit:
            return
        dep = self._deps(reads, writes, first)
        self._wait(e, dep)
        ins = fn(self.E[e])
        self.cnt[e] += 1
        ins.then_inc(self.sem[e], 1)
        self._commit({self.sem[e]: self.cnt[e]}, reads, writes, first)
        if self.cnt[e] >= SEM_MAX:
            self.nbuf += 1
            self.sem[e] = self.stack.enter_context(self.nc.semaphore(f"s_{e}_{self.nbuf}"))
            self.cnt[e] = 0

    def dma(self, q, out_ap, in_ap, reads=(), writes=(), first=None, sembuf=None, **kw):
        self.nops += 1
        if self.nops > self.limit:
            return
        dep = self._deps(reads, writes, first)
        b = sembuf
        if b.dsem is None or b.dval + 16 > SEM_MAX:
            self.nbuf += 1
            b.dsem = self.stack.enter_context(self.nc.semaphore(f"d_{b.name}_{self.nbuf}"))
            b.dval = 0
        if b.dsem is not None and self.maxwaited.get(b.dsem, 0) > 0:
            _merge(dep, {b.dsem: self.maxwaited[b.dsem]})
        self._wait(q, dep)
        ins = self.E[q].dma_start(out=out_ap, in_=in_ap, **kw)
        b.dval += 16
        ins.then_inc(b.dsem, 16)
        self.dmafinal[b.dsem] = b.dval
        self._commit({b.dsem: b.dval}, reads, writes, first)

    def finish(self, outs, e="sp"):
        dep = {}
        for b in outs:
            _merge(dep, b.writes)
        _merge(dep, self.dmafinal)
        self._wait(e, dep)


def A(buf, off, dims, p0=0, np_=128):
    row = int(np.prod(buf.t.shape[1:]))
    return bass.AP(buf.t, p0 * row + off, [[row, np_]] + [list(d) for d in dims])


def DA(buf, off, dims):
    return bass.AP(buf.t, off, [list(d) for d in dims])


def build_nc(cfg=None):
    cfg = dict(mem=1, layers=DEPTH, tiles=NT, sample=1) if cfg is None else cfg
    nc = bass.Bass("TRN2", target_bir_lowering=False)
    with contextlib.ExitStack() as st:
        kb = KB(nc, st)
        kb.limit = cfg.get('stop', 1 << 60)
        I, O = "ExternalInput", "ExternalOutput"
        xp = kb.dram("xp", [2048, D], F32, I)
        xs = kb.dram("xs", [128, D], F32, I)
        mem = kb.dram("mem", [256, D], F32, I)
        st_pool = kb.dram("st_pool", [DEPTH, NS * 15, 512], F32, I)
        st_conv = kb.dram("st_conv", [DEPTH, NS * 3, 1536], F32, I)
        st_ssm = kb.dram("st_ssm", [DEPTH, NS, 1024, 128], F32, I)
        ck = kb.dram("ck", [DEPTH, NS, 256, 512], F32, I)
        cv = kb.dram("cv", [DEPTH, NS, 256, 512], F32, I)
        w_in = kb.dram("w_in", [DEPTH, D, DIN], F32, I)
        w_out = kb.dram("w_out", [DEPTH, 2048, D], F32, I)
        pool_w = kb.dram("pool_w", [DEPTH, 4, 128, 128], F32, I)
        w_mk = kb.dram("w_mk", [DEPTH, D, 512], F32, I)
        w_mv = kb.dram("w_mv", [DEPTH, D, 512], F32, I)
        colpar_d = kb.dram("colpar", [128, DEPTH * NCP], F32, I)
        rowpar_d = kb.dram("rowpar", [DEPTH * 48], F32, I)
        fng_d = kb.dram("fng", [D], F32, I)
        cstf_d = kb.dram("cstf", [128, NCF], F32, I)
        cstb_d = kb.dram("cstb", [128, NCB], F32, I)

        yp = kb.dram("yp", [2048, D], F32, O)
        ys = kb.dram("ys", [128, D], F32, O)
        o_pool_p = kb.dram("o_pool_p", [DEPTH, 15, 512], F32, O)
        o_conv_p = kb.dram("o_conv_p", [DEPTH, 3, 1536], F32, O)
        o_ssm_p = kb.dram("o_ssm_p", [DEPTH, 1024, 128], F32, O)
        o_mk = kb.dram("o_mk", [DEPTH, 256, 512], F32, O)
        o_mv = kb.dram("o_mv", [DEPTH, 256, 512], F32, O)
        o_pool_s = kb.dram("o_pool_s", [DEPTH, NS, 15, 512], F32, O)
        o_conv_s = kb.dram("o_conv_s", [DEPTH, NS, 3, 1536], F32, O)
        o_ssm_s = kb.dram("o_ssm_s", [DEPTH, NS, 1024, 128], F32, O)
        hs = kb.dram("hs", [17 * 128, D], F32, "Internal")
        ktscr = kb.dram("ktscr", [DEPTH, 128, 1024], BF16, "Internal")
        hsB = [Buf(f"hs{i}", hs.t, partial=True) for i in range(17)]
        outs = [yp, ys, o_pool_p, o_conv_p, o_ssm_p, o_mk, o_mv, o_pool_s, o_conv_s, o_ssm_s]

        pad0 = kb.sb("pad0", [128, 16], F32)
        win = kb.sb("win", [128, 8, DIN], BF16)
        wout = kb.sb("wout", [128, 16, D], BF16)
        poolw = kb.sb("poolw", [128, 4, 128], BF16)
        ktl = kb.sb("ktl", [128, 4, 256], BF16)
        vl = kb.sb("vl", [128, 2, 512], BF16)
        fng = kb.sb("fng", [128, D], F32)
        colpar = kb.sb("colpar", [128, DEPTH * NCP], F32)
        rowpar = kb.sb("rowpar", [128, DEPTH * 48], F32)
        abc = kb.sb("abc", [128, DEPTH * 16], F32)
        cf = kb.sb("cf", [128, NCF], F32)
        cb = kb.sb("cb", [128, NCB], BF16)

        kb.pool("hbuf", [128, D], F32, 2)
        xn = kb.sb("xn", [128, D], BF16)
        xnT = kb.sb("xnT", [128, 8, 128], BF16)
        kb.pool("ubf", [128, 512], BF16, 2)
        sgpT = kb.sb("sgpT", [128, 4, 128], BF16)
        dT = kb.sb("dT", [128, 4, 128], BF16)
        qT = kb.sb("qT", [128, 4, 128], BF16)
        sgxT = kb.sb("sgxT", [128, 4, 128], BF16)
        sz = kb.sb("sz", [128, D], BF16)
        xb = kb.sb("xb", [128, 12 * 176], F32)
        carry = kb.sb("carry", [128, 12, 3], F32)
        big6 = kb.sb("big6", [128, 12 * 128], F32)
        big6b = Buf("big6b", big6.t)
        xbcsT = kb.sb("xbcsT", [128, 12, 128], BF16)
        xdt = kb.sb("xdt", [128, D], BF16)
        xd = kb.sb("xd", [128, D], BF16)
        xw = kb.sb("xw", [128, D], BF16)
        btok = kb.sb("btok", [128, 256], BF16)
        sm = kb.sb("sm", [128, 6 * 16], F32)
        sm2 = kb.sb("sm2", [128, 16], F32)
        Rb = kb.sb("Rb", [128, 4, 128], F32)
        kb.pool("E", [128, 4, 128], BF16, 2)
        kb.pool("MT", [128, 4, 128], BF16, 2)
        cbm = kb.sb("cbm", [128, 2, 128], BF16)
        Sb = kb.pool("Sb", [128, D], F32, 2)
        stb16 = kb.sb("stb16", [128, D], BF16)
        actT = kb.sb("actT", [128, 16, 128], BF16)
        scr = kb.sb("scr", [128, D], F32)
        kb.pool("Kb", [128, 2, 512], BF16, 2)
        kb.pool("Vb", [128, 2, 512], BF16, 2)
        kb.pool("Bs", [128, 256], BF16, 1)
        spool16 = kb.sb("spool16", [128, 2, 512], BF16)
        decT = kb.sb("decT", [128, 8, 16], F32)
        pts = kb.sb("pts", [128, 64], BF16)
        ctmp = kb.sb("ctmp", [128, 128], F32)
        print("sbuf bytes remaining after alloc:", nc.sbuf_bytes_remaining)

        for i in range(8):
            kb.psfree.append(kb.ps(f"bank{i}", [128, 512], F32))

        def smt(i):
            return sm[:, i * 16:(i + 1) * 16]

        kb.dma("sp", cf[:, :], cstf_d[:, :], writes=[cf], sembuf=cf)
        kb.dma("pool", cb[:, :], cstb_d[:, :], writes=[cb], sembuf=cb)
        kb.dma("sp", colpar[:, :], colpar_d[:, :], writes=[colpar], sembuf=colpar)
        kb.dma("sp", rowpar[:, :], DA(rowpar_d, 0, [[0, 128], [1, DEPTH * 48]]), writes=[rowpar], sembuf=rowpar)
        kb.dma("sp", fng[:, :], DA(fng_d, 0, [[0, 128], [1, D]]), writes=[fng], sembuf=fng)
        for l in range(DEPTH):
            kb.op("act", lambda e, l=l: e.activation(out=abc[:, l * 16:(l + 1) * 16],
                                                      in_=rowpar[:, l * 48 + 16:l * 48 + 32], func=AF.Exp),
                  reads=[rowpar], writes=[abc])
        kb.op("dve", lambda e: e.tensor_scalar_mul(out=abc[:, :], in0=abc[:, :], scalar1=-1.0),
              reads=[abc], writes=[abc])

        identf = cf[:, CF_ID:CF_ID + 128]
        identb = cb[:, CB_ID:CB_ID + 128]
        onesb = cb[:, CB_ONE:CB_ONE + 128]

        def band(g, k):
            o = CB_BAND + (g * 6 + k) * 128
            return cb[:, o:o + 128]

        def cpcol(l, off, n=1):
            return colpar[:, l * NCP + off:l * NCP + off + n]

        def rmsnorm_T(hb, gcol_off, l):
            ss = sm2[:, 0:1]
            rs = sm2[:, 1:2]
            kb.op("act", lambda e: e.activation(out=xn[:, :], in_=hb[:, :], func=AF.Square, accum_out=ss),
                  reads=[hb], writes=[xn, sm2])
            kb.op("act", lambda e: e.activation(out=rs, in_=ss, func=AF.Ln, scale=1.0 / D, bias=EPS),
                  reads=[sm2], writes=[sm2])
            kb.op("act", lambda e: e.activation(out=rs, in_=rs, func=AF.Exp, scale=-0.5), reads=[sm2], writes=[sm2])
            kb.op("dve", lambda e: e.tensor_scalar_mul(out=xn[:, :], in0=hb[:, :], scalar1=rs),
                  reads=[hb, sm2], writes=[xn])
            pt = kb.psalloc()
            ptb = pt[:, :].bitcast(BF16)

            def tr(e):
                ins = None
                for k in range(8):
                    ins = e.transpose(out=ptb[:, k * 128:(k + 1) * 128], in_=xn[:, k * 128:(k + 1) * 128], identity=identb)
                return ins
            kb.op("pe", tr, reads=[xn, cb], writes=[pt])
            kb.op("dve", lambda e: e.tensor_tensor(out=xnT[:, :, :], in0=ptb[:, :].rearrange("p (k t) -> p k t", k=8),
                                                   in1=A(colpar, l * NCP + gcol_off, [[1, 8], [0, 128]]), op=ALU.mult),
                  reads=[pt, colpar], writes=[xnT])
            kb.psrel(pt)

        def mm_tok(pbank, ncols, wbuf, wap_fn):
            def f(e):
                ins = None
                for k in range(8):
                    ins = e.matmul(pbank[:, 0:ncols], lhsT=xnT[:, k, :], rhs=wap_fn(k), start=(k == 0), stop=(k == 7))
                return ins
            kb.op("pe", f, reads=[xnT, wbuf], writes=[pbank])

        def mm_feat(pbank, nchunk, wbuf, wap_fn):
            def f(e):
                ins = None
                for c in range(nchunk):
                    for k in range(8):
                        ins = e.matmul(pbank[:, c * 128:(c + 1) * 128], lhsT=wap_fn(k, c), rhs=xnT[:, k, :],
                                       start=(k == 0), stop=(k == 7))
                return ins
            kb.op("pe", f, reads=[xnT, wbuf], writes=[pbank])

        for l in range(DEPTH if cfg['mem'] else 0):
            s = l % 2
            for k in range(8):
                kb.dma("pool", wout[:, 8 * s + k, 0:512], w_mk[l, k * 128:(k + 1) * 128, :], writes=[wout],
                       first=None if (k == 0 and s == 0) else [], sembuf=wout)
                kb.dma("pool", wout[:, 8 * s + k, 512:1024], w_mv[l, k * 128:(k + 1) * 128, :], writes=[wout],
                       first=[], sembuf=wout)
            for mt in range(2):
                hb = kb.nxt("hbuf")
                kb.dma("sp", hb[:, :], mem[mt * 128:(mt + 1) * 128, :], writes=[hb], sembuf=hb)
                rmsnorm_T(hb, CP_MG, l)
                pk = kb.psalloc()
                mm_tok(pk, 512, wout, lambda k: wout[:, 8 * s + k, 0:512])
                kb.op("act", lambda e: e.copy(out=scr[:, 0:512], in_=pk[:, :]), reads=[pk], writes=[scr])
                kb.psrel(pk)
                pv = kb.psalloc()
                mm_tok(pv, 512, wout, lambda k: wout[:, 8 * s + k, 512:1024])
                kb.op("dve", lambda e: e.tensor_copy(out=scr[:, 512:1024], in_=pv[:, :]), reads=[pv], writes=[scr], first=[])
                kb.psrel(pv)
                kb.dma("sp", o_mk[l, mt * 128:(mt + 1) * 128, :], scr[:, 0:512], reads=[scr], writes=[o_mk], sembuf=scr)
                kb.dma("sp", o_mv[l, mt * 128:(mt + 1) * 128, :], scr[:, 512:1024], reads=[scr], writes=[o_mv], sembuf=scr)
                pkt = kb.psalloc()
                mm_feat(pkt, 4, wout, lambda k, c: wout[:, 8 * s + k, c * 128:(c + 1) * 128])
                kb.op("act", lambda e: e.copy(out=ktl[:, :, mt * 128:(mt + 1) * 128],
                                              in_=pkt[:, :].rearrange("p (h m) -> p h m", h=4)),
                      reads=[pkt], writes=[ktl], first=None if mt == 0 else [])
                kb.psrel(pkt)
            kb.dma("sp", ktscr[l, :, :], ktl[:, :, :].rearrange("p h m -> p (h m)"), reads=[ktl], writes=[ktscr], sembuf=ktl)

        def load_win(l):
            for k in range(8):
                kb.dma("pool", win[:, k, :], w_in[l, k * 128:(k + 1) * 128, :], writes=[win],
                       first=None if k == 0 else [], sembuf=win)

        def load_poolw(l):
            kb.dma("pool", poolw[:, :, :], pool_w[l, :, :, :].rearrange("g c d -> c g d"), writes=[poolw], sembuf=poolw)

        def load_attn(l):
            kb.dma("sp", ktl[:, :, :].rearrange("p h m -> p (h m)"), ktscr[l, :, :], reads=[ktscr], writes=[ktl], sembuf=ktl)
            kb.dma("pool", vl[:, :, :], o_mv[l, :, :].rearrange("(c p) n -> p c n", p=128), reads=[o_mv], writes=[vl], sembuf=vl)

        def load_wout(l):
            for k in range(16):
                kb.dma("pool", wout[:, k, :], w_out[l, k * 128:(k + 1) * 128, :], writes=[wout],
                       first=None if k == 0 else [], sembuf=wout)

        SCALE = 1.0 / math.sqrt(128.0)

        XB = [xb, Buf("xb1", xb.t), Buf("xb2", xb.t)]
        BIG = [big6, Buf("big1", big6.t), big6b]
        XS = [xbcsT, Buf("xs1", xbcsT.t), Buf("xs2", xbcsT.t)]
        sm2n = Buf("sm2n", sm2.t)

        def norm_stage(l, ti, sample):
            hidx = 16 if sample else ti
            hb = kb.nxt("hbuf")
            if l == 0:
                src = xs[:, :] if sample else xp[ti * 128:(ti + 1) * 128, :]
                kb.dma("sp", hb[:, :], src, writes=[hb], sembuf=hb)
            else:
                kb.dma("sp", hb[:, :], hs[hidx * 128:(hidx + 1) * 128, :], reads=[hsB[hidx]], writes=[hb], sembuf=hb)
            rmsnorm_T(hb, CP_NG, l)
            return hb

        def do_tile(l, ti, sample, nxt_tile):
            last_p = (not sample) and ti == cfg['tiles'] - 1
            LL = cfg['layers']
            special = sample or last_p
            hidx = 16 if sample else ti
            pre = tile_state.pop("pre", None)
            if pre is not None:
                hb, xbc_done = pre
            else:
                hb, xbc_done = norm_stage(l, ti, sample), False
            rp = l * 48
            afirst = [None]

            def actT_first():
                f = afirst[0]
                afirst[0] = []
                return f

            if not xbc_done:
                if sample:
                    for half in range(2):
                        kb.dma("sp", scr[0:48, 0:768], st_conv[l, :, half * 768:(half + 1) * 768], writes=[scr], sembuf=scr)
                        pc = kb.psalloc()

                        def trc(e, pc=pc):
                            ins = None
                            for c in range(6):
                                ins = e.transpose(out=pc[:, c * 48:(c + 1) * 48], in_=scr[0:48, c * 128:(c + 1) * 128],
                                                  identity=cf[0:48, CF_ID:CF_ID + 48])
                            return ins
                        kb.op("pe", trc, reads=[scr, cf], writes=[pc])
                        wr = [XB[0], XB[1]] if half == 0 else [XB[1], XB[2]]
                        kb.op("act", lambda e, pc=pc, half=half: e.copy(
                            out=A(xb, half * 6 * 176, [[176, 6], [11, 16], [1, 3]]),
                            in_=A(pc, 0, [[48, 6], [3, 16], [1, 3]])), reads=[pc], writes=wr,
                            first=None if half == 0 else [XB[2]])
                        kb.psrel(pc)
                else:
                    if ti == 0:
                        kb.op("pool", lambda e: e.memset(carry[:, :, :], 0.0), writes=[carry])
                    kb.op("pool", lambda e: e.tensor_copy(out=A(xb, 0, [[176, 12], [1, 3]]), in_=carry[:, :, :]),
                          reads=[carry], writes=XB)
                for b3 in range(3):
                    pxb = kb.psalloc()
                    mm_feat(pxb, 4, win, lambda k, c, b3=b3: win[:, k, C_XBC + (b3 * 4 + c) * 128:C_XBC + (b3 * 4 + c + 1) * 128])
                    if sample:
                        o_ap = A(xb, b3 * 4 * 176 + 3, [[176, 4], [11, 16], [1, 8]])
                        i_ap = A(pxb, 0, [[128, 4], [8, 16], [1, 8]])
                    else:
                        o_ap = A(xb, b3 * 4 * 176 + 3, [[176, 4], [1, 128]])
                        i_ap = A(pxb, 0, [[128, 4], [1, 128]])
                    kb.op("act", lambda e, o_ap=o_ap, i_ap=i_ap: e.copy(out=o_ap, in_=i_ap), reads=[pxb], writes=[XB[b3]], first=[])
                    kb.psrel(pxb)
                if not sample:
                    kb.op("pool", lambda e: e.tensor_copy(out=carry[:, :, :], in_=A(xb, 128, [[176, 12], [1, 3]])),
                          reads=XB, writes=[carry])
            pq = kb.psalloc()
            mm_feat(pq, 4, win, lambda k, c: win[:, k, C_Q + c * 128:C_Q + (c + 1) * 128])
            kb.op("act", lambda e: e.copy(out=qT[:, :, :].rearrange("p a b -> p (a b)"), in_=pq[:, :]),
                  reads=[pq], writes=[qT])
            kb.psrel(pq)
            pgx = kb.psalloc()
            mm_feat(pgx, 4, win, lambda k, c: win[:, k, C_GX + c * 128:C_GX + (c + 1) * 128])
            kb.op("act", lambda e: e.activation(out=sgxT[:, :, :].rearrange("p a b -> p (a b)"), in_=pgx[:, :], func=AF.Silu),
                  reads=[pgx], writes=[sgxT])
            kb.psrel(pgx)

            if not sample:
                psc = [kb.psalloc(), kb.psalloc()]
                for hp in range(2):
                    def scmm(e, hp=hp):
                        ins = None
                        for hh in range(2):
                            h = hp * 2 + hh
                            for mc in range(2):
                                ins = e.matmul(psc[hp][:, (hh * 2 + mc) * 128:(hh * 2 + mc + 1) * 128],
                                               lhsT=ktl[:, h, mc * 128:(mc + 1) * 128], rhs=qT[:, h, :], start=True, stop=True)
                        return ins
                    kb.op("pe", scmm, reads=[ktl, qT], writes=[psc[hp]])
                PT = [kb.nxt("E"), kb.nxt("E")]
                for hp in range(2):
                    kb.op("act", lambda e, hp=hp: e.activation(out=PT[hp][:, :, :].rearrange("p a b -> p (a b)"),
                                                               in_=psc[hp][:, :], func=AF.Exp, scale=SCALE),
                          reads=[psc[hp]], writes=[PT[hp]])
                kb.psrel(psc[0], psc[1])
                psum_ = kb.psalloc()
                pxa = kb.psalloc()

                def summm(e):
                    ins = None
                    for h in range(4):
                        for mc in range(2):
                            ins = e.matmul(psum_[:, h * 128:(h + 1) * 128], lhsT=onesb, rhs=PT[h // 2][:, (h % 2) * 2 + mc, :],
                                           start=(mc == 0), stop=(mc == 1))
                    return ins
                kb.op("pe", summm, reads=[cb, PT[0], PT[1]], writes=[psum_])

                def pvmm(e):
                    ins = None
                    for h in range(4):
                        for mc in range(2):
                            ins = e.matmul(pxa[:, h * 128:(h + 1) * 128], lhsT=vl[:, mc, h * 128:(h + 1) * 128],
                                           rhs=PT[h // 2][:, (h % 2) * 2 + mc, :], start=(mc == 0), stop=(mc == 1))
                    return ins
                kb.op("pe", pvmm, reads=[vl, PT[0], PT[1]], writes=[pxa])

            for ch in range(12):
                gi = ch // 4
                if sample:
                    def xin(k, ch=ch):
                        return A(xb, ch * 176 + k, [[11, 16], [1, 8]])
                    tout = A(big6, ch * 128, [[8, 16], [1, 8]])
                else:
                    def xin(k, ch=ch):
                        return A(xb, ch * 176 + k, [[1, 128]])
                    tout = A(big6, ch * 128, [[1, 128]])
                if ch < 8:
                    kb.op("dve", lambda e, xin=xin, tout=tout, ch=ch: e.tensor_scalar_mul(
                        out=tout, in0=xin(0), scalar1=cpcol(l, CP_CW + ch * 4 + 0)),
                        reads=[XB[gi], colpar], writes=[BIG[gi]], first=None if ch % 4 == 0 else [])
                    for k in range(1, 4):
                        kb.op("dve", lambda e, xin=xin, tout=tout, ch=ch, k=k: e.scalar_tensor_tensor(
                            out=tout, in0=xin(k), scalar=cpcol(l, CP_CW + ch * 4 + k), in1=tout, op0=ALU.mult, op1=ALU.add),
                            reads=[XB[gi], colpar, BIG[gi]], writes=[BIG[gi]], first=[])
                else:
                    bdims = [[0, 16], [0, 8]] if sample else [[0, 128]]
                    t2 = A(ctmp, 0, [[8, 16], [1, 8]]) if sample else ctmp[:, :]

                    def wbc(k, ch=ch, bdims=bdims):
                        return A(colpar, l * NCP + CP_CW + ch * 4 + k, bdims)
                    kb.op("pool", lambda e, xin=xin, tout=tout, wbc=wbc: e.tensor_tensor(out=tout, in0=xin(0), in1=wbc(0), op=ALU.mult),
                          reads=[XB[gi], colpar], writes=[BIG[gi]], first=None if ch % 4 == 0 else [])
                    for k in range(1, 4):
                        kb.op("pool", lambda e, xin=xin, t2=t2, wbc=wbc, k=k: e.tensor_tensor(out=t2, in0=xin(k), in1=wbc(k), op=ALU.mult),
                              reads=[XB[gi], colpar], writes=[ctmp])
                        kb.op("pool", lambda e, tout=tout, t2=t2: e.tensor_tensor(out=tout, in0=tout, in1=t2, op=ALU.add),
                              reads=[BIG[gi], ctmp], writes=[BIG[gi]], first=[])
                kb.op("act", lambda e, ch=ch: e.activation(out=xbcsT[:, ch, :], in_=big6[:, ch * 128:(ch + 1) * 128], func=AF.Silu,
                                                           bias=cpcol(l, CP_CB + ch)),
                      reads=[BIG[gi], colpar], writes=[XS[gi]], first=None if ch % 4 == 0 else [])

            def attn_tail(psum_, pxa):
                if sample:
                    kb.op("dve", lambda e: e.reciprocal(out=A(Rb, 0, [[128, 4], [8, 16], [1, 8]]),
                                                        in_=A(psum_, 0, [[8, 4], [32, 16], [1, 8]])), reads=[psum_], writes=[Rb])
                else:
                    kb.op("dve", lambda e: e.reciprocal(out=Rb[:, :, :].rearrange("p a b -> p (a b)"), in_=psum_[:, :]),
                          reads=[psum_], writes=[Rb])
                kb.op("dve", lambda e: e.tensor_tensor(out=Rb[:, :, :].rearrange("p a b -> p (a b)"),
                                                       in0=pxa[:, :], in1=Rb[:, :, :].rearrange("p a b -> p (a b)"), op=ALU.mult),
                      reads=[pxa, Rb], writes=[Rb])
                kb.psrel(psum_, pxa)
                kb.op("pool", lambda e: e.tensor_tensor(out=actT[:, 12:16, :], in0=Rb[:, :, :], in1=sgxT[:, :, :], op=ALU.mult),
                      reads=[Rb, sgxT], writes=[actT], first=actT_first())
            if not sample:
                attn_tail(psum_, pxa)

            pu = kb.psalloc()
            mm_tok(pu, 512, win, lambda k: win[:, k, C_U:C_U + 512])
            ub = kb.nxt("ubf")
            kb.op("act", lambda e: e.copy(out=ub[:, :], in_=pu[:, :]), reads=[pu], writes=[ub])
            if special:
                kb.op("act", lambda e: e.copy(out=scr[:, 0:512], in_=pu[:, :]), reads=[pu], writes=[scr])
                if sample:
                    for s_ in range(NS):
                        kb.dma("sp", o_pool_s[l, s_, 7:15, :], scr[s_ * 8:(s_ + 1) * 8, 0:512], reads=[scr],
                               writes=[o_pool_s], sembuf=scr)
                else:
                    kb.dma("sp", o_pool_p[l, :, :], scr[113:128, 0:512], reads=[scr], writes=[o_pool_p], sembuf=scr)
            kb.psrel(pu)
            pg = kb.psalloc()
            mm_feat(pg, 4, win, lambda k, c: win[:, k, C_GP + c * 128:C_GP + (c + 1) * 128])
            kb.op("act", lambda e: e.activation(out=sgpT[:, :, :].rearrange("p a b -> p (a b)"), in_=pg[:, :], func=AF.Silu),
                  reads=[pg], writes=[sgpT])
            kb.psrel(pg)
            for half in range(2):
                pz = kb.psalloc()
                mm_tok(pz, 512, win, lambda k, half=half: win[:, k, C_Z + half * 512:C_Z + (half + 1) * 512])
                kb.op("act", lambda e, half=half, pz=pz: e.activation(out=sz[:, half * 512:(half + 1) * 512], in_=pz[:, :],
                                                                      func=AF.Silu),
                      reads=[pz], writes=[sz], first=None if half == 0 else [])
                kb.psrel(pz)
            pd = kb.psalloc()
            mm_tok(pd, 16, win, lambda k: win[:, k, C_DT:C_DT + 16])
            dt = smt(0)
            kb.op("dve", lambda e: e.tensor_tensor(out=dt, in0=pd[:, 0:16], in1=rowpar[:, rp:rp + 16], op=ALU.add),
                  reads=[pd, rowpar], writes=[sm])
            kb.psrel(pd)
            kb.op("act", lambda e: e.activation(out=dt, in_=dt, func=AF.Exp), reads=[sm], writes=[sm])
            kb.op("act", lambda e: e.activation(out=dt, in_=dt, func=AF.Ln, bias=1.0), reads=[sm], writes=[sm])
            if special:
                for part in range(3):
                    px = kb.psalloc()
                    mm_tok(px, 512, win, lambda k, part=part: win[:, k, C_XBC + part * 512:C_XBC + (part + 1) * 512])
                    kb.op("act", lambda e, px=px: e.copy(out=scr[:, 512:1024], in_=px[:, :]), reads=[px], writes=[scr])
                    kb.psrel(px)
                    if sample:
                        for s_ in range(NS):
                            kb.dma("sp", o_conv_s[l, s_, :, part * 512:(part + 1) * 512],
                                   scr[s_ * 8 + 5:s_ * 8 + 8, 512:1024], reads=[scr], writes=[o_conv_s], sembuf=scr)
                    else:
                        kb.dma("sp", o_conv_p[l, :, part * 512:(part + 1) * 512], scr[125:128, 512:1024], reads=[scr],
                               writes=[o_conv_p], sembuf=scr)
            if sample and l < LL - 1:
                load_win(l + 1)

            pdT = kb.psalloc()

            def poolmm(e):
                ins = None
                for g in range(4):
                    o = pdT[:, g * 128:(g + 1) * 128]
                    ul = ub[:, g * 128:(g + 1) * 128]
                    if sample:
                        e.matmul(o, lhsT=ul, rhs=band(g, 3), start=True, stop=False)
                        e.matmul(o, lhsT=spool16[0:120, 0, g * 128:(g + 1) * 128], rhs=band(g, 4)[0:120, :],
                                 start=False, stop=False)
                        ins = e.matmul(o, lhsT=spool16[0:120, 1, g * 128:(g + 1) * 128], rhs=band(g, 5)[0:120, :],
                                       start=False, stop=True)
                    elif ti == 0:
                        ins = e.matmul(o, lhsT=ul, rhs=band(g, 2), start=True, stop=True)
                    else:
                        e.matmul(o, lhsT=ul, rhs=band(g, 0), start=True, stop=False)
                        ins = e.matmul(o, lhsT=uprev[64:128, g * 128:(g + 1) * 128], rhs=band(g, 1)[64:128, :],
                                       start=False, stop=True)
                return ins
            if sample:
                kb.dma("pool", spool16[0:120, 0, :], st_pool[l, 0:120, :], writes=[spool16], sembuf=spool16)
                kb.dma("pool", spool16[0:120, 1, :], st_pool[l, 120:240, :], writes=[spool16], first=[], sembuf=spool16)
                kb.dma("sp", o_pool_s[l, :, 0:7, :], st_pool[l, :, :].rearrange("(s r) c -> s r c", r=15)[:, 8:15, :],
                       writes=[o_pool_s], sembuf=d2d)
                kb.op("pe", poolmm, reads=[ub, cb, spool16], writes=[pdT])
            elif ti == 0:
                kb.op("pe", poolmm, reads=[ub, cb], writes=[pdT])
            else:
                uprev = tile_state["uprev"]
                kb.op("pe", poolmm, reads=[ub, cb, uprev], writes=[pdT])
            tile_state["uprev"] = ub
            for g in range(4):
                if (not sample) and ti == 0:
                    kb.op("dve", lambda e, g=g: e.tensor_tensor(out=dT[:, g, :], in0=pdT[:, g * 128:(g + 1) * 128],
                                                                in1=cf[:, CF_IC0 + g * 128:CF_IC0 + (g + 1) * 128], op=ALU.mult),
                          reads=[pdT, cf], writes=[dT], first=None if g == 0 else [])
                else:
                    kb.op("act", lambda e, g=g: e.mul(out=dT[:, g, :], in_=pdT[:, g * 128:(g + 1) * 128], mul=1.0 / POOL_W[g]),
                          reads=[pdT], writes=[dT], first=None if g == 0 else [])
            kb.psrel(pdT)
            py = kb.psalloc()

            def poolmm2(e):
                ins = None
                for g in range(4):
                    ins = e.matmul(py[:, g * 128:(g + 1) * 128], lhsT=poolw[:, g, :], rhs=dT[:, g, :], start=True, stop=True)
                return ins
            kb.op("pe", poolmm2, reads=[poolw, dT], writes=[py])
            for g in range(4):
                kb.op("dve", lambda e, g=g: e.scalar_tensor_tensor(out=actT[:, g, :], in0=py[:, g * 128:(g + 1) * 128],
                                                                   scalar=cpcol(l, CP_PS + g), in1=sgpT[:, g, :],
                                                                   op0=ALU.mult, op1=ALU.mult),
                      reads=[py, colpar, sgpT], writes=[actT], first=actT_first())
            kb.psrel(py)
            if sample and l < LL - 1:
                load_poolw(l + 1)

            tri = cf[:, (CF_TRIS if sample else CF_TRI):(CF_TRIS if sample else CF_TRI) + 128]
            onem = cf[:, (CF_BONE if sample else CF_ONE):(CF_BONE if sample else CF_ONE) + 128]
            su = cf[:, CF_SU:CF_SU + 128]
            a_ = smt(1)
            cum = smt(2)
            e_ = smt(3)
            wend = smt(4)
            dec = smt(5)
            kb.op("dve", lambda e: e.tensor_tensor(out=a_, in0=dt, in1=abc[:, l * 16:(l + 1) * 16], op=ALU.mult),
                  reads=[sm, abc], writes=[sm], first=[])
            pcm = kb.psalloc()

            def cummm(e):
                e.matmul(pcm[:, 0:16], lhsT=tri, rhs=a_, start=True, stop=True)
                return e.matmul(pcm[:, 16:32], lhsT=onem, rhs=a_, start=True, stop=True)
            kb.op("pe", cummm, reads=[cf, sm], writes=[pcm])
            kb.op("act", lambda e: e.copy(out=cum, in_=pcm[:, 0:16]), reads=[pcm], writes=[sm], first=[])
            kb.op("act", lambda e: e.activation(out=e_, in_=pcm[:, 0:16], func=AF.Exp), reads=[pcm], writes=[sm], first=[])
            kb.op("act", lambda e: e.activation(out=dec, in_=pcm[:, 16:32], func=AF.Exp), reads=[pcm], writes=[sm], first=[])
            kb.op("dve", lambda e: e.tensor_tensor(out=wend, in0=pcm[:, 16:32], in1=cum, op=ALU.subtract),
                  reads=[pcm, sm], writes=[sm], first=[])
            kb.psrel(pcm)
            kb.op("act", lambda e: e.activation(out=wend, in_=wend, func=AF.Exp), reads=[sm], writes=[sm], first=[])
            kb.op("dve", lambda e: e.tensor_tensor(out=wend, in0=wend, in1=dt, op=ALU.mult), reads=[sm], writes=[sm], first=[])

            ptx = kb.psalloc()
            ptxb = ptx[:, :].bitcast(BF16)

            def trx(e):
                ins = None
                for c in range(8):
                    ins = e.transpose(out=ptxb[:, c * 128:(c + 1) * 128], in_=xbcsT[:, c, :], identity=identb)
                return ins
            kb.op("pe", trx, reads=[XS[0], XS[1], cb], writes=[ptx])
            ptx3 = ptxb.rearrange("p (h d) -> p h d", h=16)

            def bc16(off):
                return A(sm, off, [[1, 16], [0, 64]])
            kb.op("dve", lambda e: e.tensor_tensor(out=xdt[:, :].rearrange("p (h d) -> p h d", h=16), in0=ptx3,
                                                   in1=bc16(0), op=ALU.mult), reads=[ptx, sm], writes=[xdt])
            kb.op("dve", lambda e: e.tensor_tensor(out=xw[:, :].rearrange("p (h d) -> p h d", h=16), in0=ptx3,
                                                   in1=bc16(4 * 16), op=ALU.mult), reads=[ptx, sm], writes=[xw])
            kb.op("dve", lambda e: e.tensor_tensor(out=xd[:, :].rearrange("p (h d) -> p h d", h=16), in0=ptx3,
                                                   in1=A(rowpar, rp + 32, [[1, 16], [0, 64]]), op=ALU.mult),
                  reads=[ptx, rowpar], writes=[xd])
            kb.psrel(ptx)
            ptb_ = kb.psalloc()
            ptbb = ptb_[:, :].bitcast(BF16)

            def trb(e):
                e.transpose(out=ptbb[:, 0:128], in_=xbcsT[:, 8, :], identity=identb)
                return e.transpose(out=ptbb[:, 128:256], in_=xbcsT[:, 9, :], identity=identb)
            kb.op("pe", trb, reads=[XS[2], cb], writes=[ptb_])
            kb.op("act", lambda e: e.copy(out=btok[:, :], in_=ptbb[:, 0:256]), reads=[ptb_], writes=[btok])
            kb.psrel(ptb_)
            pcb = kb.psalloc()

            def cbmm(e):
                e.matmul(pcb[:, 0:128], lhsT=xbcsT[:, 8, :], rhs=xbcsT[:, 10, :], start=True, stop=True)
                return e.matmul(pcb[:, 128:256], lhsT=xbcsT[:, 9, :], rhs=xbcsT[:, 11, :], start=True, stop=True)
            kb.op("pe", cbmm, reads=[XS[2]], writes=[pcb])
            tri_off = CF_TRIS if sample else CF_TRI
            kb.op("dve", lambda e: e.tensor_tensor(out=cbm[:, :, :], in0=pcb[:, 0:256].rearrange("p (g i) -> p g i", g=2),
                                                   in1=A(cf, tri_off, [[0, 2], [1, 128]]), op=ALU.mult),
                  reads=[pcb, cf], writes=[cbm])
            kb.psrel(pcb)
            if sample:
                pyit = sample_seq_phase(l)
            pY = [kb.psalloc(), kb.psalloc()]
            for g in range(2):
                kb.op("pe", lambda e, g=g: e.matmul(pY[g][:, :], lhsT=identb, rhs=xd[:, g * 512:(g + 1) * 512],
                                                    start=True, stop=False), reads=[cb, xd], writes=[pY[g]])
            stage = {}

            def intra_a(q4):
                if q4 % 2 == 0:
                    Rbuf, R3, R2 = Rb, Rb[:, :, :], Rb[:, :, :].rearrange("p a b -> p (a b)")
                else:
                    Rbuf, R3, R2 = BIG[2], A(big6, 1024, [[128, 4], [1, 128]]), big6[:, 1024:1536]
                kb.op("pool", lambda e: e.tensor_tensor(out=R3, in0=A(sm, 16 + q4 * 4, [[1, 4], [0, 128]]),
                                                        in1=A(cf, tri_off, [[0, 4], [1, 128]]), op=ALU.mult),
                      reads=[sm, cf], writes=[Rbuf])
                pseg = kb.psalloc()
                kb.op("pe", lambda e: e.matmul(pseg[:, :], lhsT=su, rhs=R2, start=True, stop=True),
                      reads=[cf, Rbuf], writes=[pseg])
                stage[q4] = pseg

            def intra_b(q4):
                pseg = stage.pop(q4)
                Eb = kb.nxt("E")
                kb.op("act", lambda e: e.activation(out=Eb[:, :, :].rearrange("p a b -> p (a b)"),
                                                    in_=pseg[:, :], func=AF.Exp), reads=[pseg], writes=[Eb])
                kb.psrel(pseg)
                MTb = kb.nxt("MT")
                g = q4 // 2
                kb.op("dve", lambda e: e.tensor_tensor(
                    out=MTb[:, :, :], in0=Eb[:, :, :], in1=A(cbm, g * 128, [[0, 4], [1, 128]]), op=ALU.mult),
                    reads=[Eb, cbm], writes=[MTb])

                def ymm(e):
                    ins = None
                    for hh in range(4):
                        h = q4 * 4 + hh
                        o = pY[h // 8][:, (h % 8) * 64:(h % 8 + 1) * 64]
                        ins = e.matmul(o, lhsT=MTb[:, hh, :], rhs=xdt[:, h * 64:(h + 1) * 64], start=False, stop=(h % 8 == 7))
                    return ins
                kb.op("pe", ymm, reads=[MTb, xdt], writes=[pY[q4 // 2]], first=[])
            pf = (not sample) and nxt_tile is not None and (not nxt_tile[2])
            pfb = []
            if pf:
                hb_n = norm_stage(*nxt_tile)
                kb.op("pool", lambda e: e.tensor_copy(out=A(xb, 0, [[176, 12], [1, 3]]), in_=carry[:, :, :]),
                      reads=[carry], writes=XB)
            intra_a(0)
            for q4 in range(4):
                if q4 + 1 < 4:
                    intra_a(q4 + 1)
                if pf and q4 < 3:
                    pxb = kb.psalloc()
                    mm_feat(pxb, 4, win, lambda k, c, b3=q4: win[:, k, C_XBC + (b3 * 4 + c) * 128:C_XBC + (b3 * 4 + c + 1) * 128])
                    pfb.append(pxb)
                intra_b(q4)
            if pf:
                for b3, pxb in enumerate(pfb):
                    kb.op("act", lambda e, b3=b3, pxb=pxb: e.copy(out=A(xb, b3 * 4 * 176 + 3, [[176, 4], [1, 128]]),
                                                                  in_=A(pxb, 0, [[128, 4], [1, 128]])),
                          reads=[pxb], writes=[XB[b3]], first=[])
                    kb.psrel(pxb)
                kb.op("pool", lambda e: e.tensor_copy(out=carry[:, :, :], in_=A(xb, 128, [[176, 12], [1, 3]])),
                      reads=XB, writes=[carry])
                tile_state["pre"] = (hb_n, True)

            if sample:
                for g in range(2):
                    kb.op("act" if g == 0 else "dve",
                          (lambda e, g=g: e.copy(out=scr[:, g * 512:(g + 1) * 512], in_=pyit[g][:, :])) if g == 0 else
                          (lambda e, g=g: e.tensor_copy(out=scr[:, g * 512:(g + 1) * 512], in_=pyit[g][:, :])),
                          reads=[pyit[g]], writes=[scr], first=None if g == 0 else [])
                kb.psrel(pyit[0], pyit[1])
            pYI = [kb.psalloc(), kb.psalloc()]
            if sample:
                for g in range(2):
                    def tryi(e, g=g):
                        ins = None
                        for c in range(4):
                            cc = g * 4 + c
                            ins = e.transpose(out=pYI[g][:, c * 128:(c + 1) * 128], in_=scr[:, cc * 128:(cc + 1) * 128],
                                              identity=identf)
                        return ins
                    kb.op("pe", tryi, reads=[scr, cf], writes=[pYI[g]])
            if not sample:
                ST = Sb[0]
                if ti == 0:
                    kb.op("pool", lambda e: e.memset(ST[:, :], 0.0), writes=[ST])
                    kb.op("pool", lambda e: e.memset(stb16[:, :], 0.0), writes=[stb16])
                for g in range(2):
                    kb.op("pe", lambda e, g=g: e.matmul(pYI[g][:, :], lhsT=xbcsT[:, 10 + g, :],
                                                        rhs=stb16[:, g * 512:(g + 1) * 512], start=True, stop=True),
                          reads=[XS[2], stb16], writes=[pYI[g]])

            y = big6
            YB = [BIG[0], BIG[1]]
            for g in range(2):
                kb.op("dve", lambda e, g=g: e.tensor_tensor(
                    out=A(y, g * 512, [[64, 8], [1, 64]]), in0=pYI[g][:, :].rearrange("p (h d) -> p h d", h=8),
                    in1=A(sm, 3 * 16 + g * 8, [[1, 8], [0, 64]]), op=ALU.mult),
                    reads=[pYI[g], sm], writes=[YB[g]])
            for g in range(2):
                kb.op("dve", lambda e, g=g: e.tensor_tensor(out=y[:, g * 512:(g + 1) * 512], in0=y[:, g * 512:(g + 1) * 512],
                                                            in1=pY[g][:, :], op=ALU.add), reads=[YB[g], pY[g]], writes=[YB[g]])
            kb.psrel(pYI[0], pYI[1], pY[0], pY[1])
            for g in range(2):
                kb.op("pool", lambda e, g=g: e.tensor_tensor(out=y[:, g * 512:(g + 1) * 512], in0=y[:, g * 512:(g + 1) * 512],
                                                             in1=sz[:, g * 512:(g + 1) * 512], op=ALU.mult),
                      reads=[YB[g], sz], writes=[YB[g]])
            ynb = xn
            for g in range(2):
                kb.op("act", lambda e, g=g: e.activation(out=ynb[:, g * 512:(g + 1) * 512], in_=y[:, g * 512:(g + 1) * 512],
                                                         func=AF.Square, accum_out=sm2[:, 2 + g:3 + g]),
                      reads=[YB[g]], writes=[ynb, sm2], first=None if g == 0 else [])
            kb.op("act", lambda e: e.activation(out=sm2[:, 4:6], in_=sm2[:, 2:4], func=AF.Ln, scale=1.0 / 512, bias=EPS),
                  reads=[sm2], writes=[sm2], first=[])
            kb.op("act", lambda e: e.activation(out=sm2[:, 4:6], in_=sm2[:, 4:6], func=AF.Exp, scale=-0.5),
                  reads=[sm2], writes=[sm2], first=[])
            for g in range(2):
                kb.op("dve", lambda e, g=g: e.tensor_scalar_mul(
                    out=ynb[:, g * 512:(g + 1) * 512], in0=y[:, g * 512:(g + 1) * 512], scalar1=sm2[:, 4 + g:5 + g]),
                    reads=[YB[g], sm2], writes=[ynb], first=None if g == 0 else [])
            pyt = kb.psalloc()
            pytb = pyt[:, :].bitcast(BF16)

            def try_(e):
                ins = None
                for c in range(8):
                    ins = e.transpose(out=pytb[:, c * 128:(c + 1) * 128], in_=ynb[:, c * 128:(c + 1) * 128], identity=identb)
                return ins
            kb.op("pe", try_, reads=[ynb, cb], writes=[pyt])
            kb.op("dve", lambda e: e.tensor_tensor(out=actT[:, 4:12, :], in0=pytb[:, :].rearrange("p (c t) -> p c t", c=8),
                                                   in1=A(colpar, l * NCP + CP_SG, [[1, 8], [0, 128]]), op=ALU.mult),
                  reads=[pyt, colpar], writes=[actT], first=actT_first())
            kb.psrel(pyt)

            if nxt_tile is not None and not pf:
                tile_state["pre"] = (norm_stage(*nxt_tile), False)

            if not sample:
                ST = Sb[0]
                pup = [kb.psalloc(), kb.psalloc()]
                for g in range(2):
                    kb.op("pe", lambda e, g=g: e.matmul(pup[g][:, :], lhsT=btok[:, g * 128:(g + 1) * 128],
                                                        rhs=xw[:, g * 512:(g + 1) * 512], start=True, stop=True),
                          reads=[btok, xw], writes=[pup[g]])
                kb.op("pool", lambda e: e.tensor_tensor(out=ST[:, :].rearrange("p (h d) -> p h d", h=16),
                                                        in0=ST[:, :].rearrange("p (h d) -> p h d", h=16),
                                                        in1=A(sm, 5 * 16, [[1, 16], [0, 64]]), op=ALU.mult),
                      reads=[ST, sm], writes=[ST])
                for g in range(2):
                    kb.op("dve", lambda e, g=g: e.tensor_tensor(out=ST[:, g * 512:(g + 1) * 512], in0=ST[:, g * 512:(g + 1) * 512],
                                                                in1=pup[g][:, :], op=ALU.add), reads=[ST, pup[g]], writes=[ST])
                kb.psrel(pup[0], pup[1])
                if last_p:
                    for half in range(2):
                        pt_ = kb.psalloc()

                        def trs(e, pt_=pt_, half=half):
                            ins = None
                            for c in range(4):
                                cc = half * 4 + c
                                ins = e.transpose(out=pt_[:, c * 128:(c + 1) * 128], in_=ST[:, cc * 128:(cc + 1) * 128],
                                                  identity=identf)
                            return ins
                        kb.op("pe", trs, reads=[ST, cf], writes=[pt_])
                        kb.op("act", lambda e, pt_=pt_: e.copy(out=scr[:, 0:512], in_=pt_[:, :]), reads=[pt_], writes=[scr])
                        kb.psrel(pt_)
                        kb.dma("sp", o_ssm_p[l, half * 512:(half + 1) * 512, :].rearrange("(c p) n -> p c n", p=128),
                               scr[:, 0:512].rearrange("p (c n) -> p c n", c=4), reads=[scr], writes=[o_ssm_p], sembuf=scr)
                else:
                    kb.op("act", lambda e: e.copy(out=stb16[:, :], in_=ST[:, :]), reads=[ST], writes=[stb16])
            else:
                attn_tail(tile_state["psum_"], tile_state["pxa"])

            po = [kb.psalloc(), kb.psalloc()]
            for n in range(2):
                def omm(e, n=n):
                    ins = None
                    for k in range(16):
                        ins = e.matmul(po[n][:, :], lhsT=actT[:, k, :], rhs=wout[:, k, n * 512:(n + 1) * 512],
                                       start=(k == 0), stop=(k == 15))
                    return ins
                kb.op("pe", omm, reads=[actT, wout], writes=[po[n]])
            for n in range(2):
                kb.op("dve", lambda e, n=n: e.tensor_tensor(out=hb[:, n * 512:(n + 1) * 512], in0=hb[:, n * 512:(n + 1) * 512],
                                                            in1=po[n][:, :], op=ALU.add), reads=[hb, po[n]], writes=[hb])
            kb.psrel(po[0], po[1])
            if sample and l < LL - 1:
                load_wout(l + 1)
                load_attn(l + 1)
            if l < LL - 1:
                kb.dma("sp", hs[hidx * 128:(hidx + 1) * 128, :], hb[:, :], reads=[hb], writes=[hsB[hidx]], sembuf=hb)
            else:
                ss = sm2[:, 8:9]
                rs = sm2[:, 9:10]
                fj = scr if not special else xd
                kb.op("act", lambda e: e.activation(out=fj[:, 0:1024], in_=hb[:, :], func=AF.Square, accum_out=ss),
                      reads=[hb], writes=[fj, sm2n])
                kb.op("act", lambda e: e.activation(out=rs, in_=ss, func=AF.Ln, scale=1.0 / D, bias=EPS),
                      reads=[sm2n], writes=[sm2n])
                kb.op("act", lambda e: e.activation(out=rs, in_=rs, func=AF.Exp, scale=-0.5), reads=[sm2n], writes=[sm2n])
                kb.op("dve", lambda e: e.scalar_tensor_tensor(out=hb[:, :], in0=hb[:, :], scalar=rs, in1=fng[:, :],
                                                              op0=ALU.mult, op1=ALU.mult), reads=[hb, sm2n, fng], writes=[hb])
                dst = ys[:, :] if sample else yp[ti * 128:(ti + 1) * 128, :]
                kb.dma("sp", dst, hb[:, :], reads=[hb], writes=[ys if sample else yp], sembuf=hb)

        def sample_seq_phase(l):
            abcast = big6
            kb.op("pool", lambda e: e.tensor_copy(out=abcast[:, 0:1024].rearrange("p (h d) -> p h d", h=16),
                                                  in_=A(sm, 16, [[1, 16], [0, 64]])), reads=[sm], writes=[abcast])
            pdc = kb.psalloc()

            def dcmm(e):
                ins = None
                for c in range(8):
                    ins = e.matmul(pdc[:, c * 16:(c + 1) * 16], lhsT=abcast[:, c * 128:(c + 1) * 128],
                                   rhs=cf[:, CF_BM:CF_BM + 16], start=True, stop=True)
                return ins
            kb.op("pe", dcmm, reads=[abcast, cf], writes=[pdc])
            kb.op("act", lambda e: e.activation(out=decT[:, :, :].rearrange("p a b -> p (a b)"), in_=pdc[:, 0:128], func=AF.Exp),
                  reads=[pdc], writes=[decT])
            kb.psrel(pdc)
            pyit = [kb.psalloc(), kb.psalloc()]
            psum_ = kb.psalloc()
            pxa = kb.psalloc()
            tile_state["psum_"], tile_state["pxa"] = psum_, pxa
            for s_ in range(NS):
                c0 = s_ * 8
                Sin = kb.nxt("Sb")
                kb.dma("sp", Sin[:, :].rearrange("p (c n) -> p c n", c=8),
                       st_ssm[l, s_, :, :].rearrange("(c p) n -> p c n", p=128), writes=[Sin], sembuf=Sin)
                for half in range(2):
                    pts_ = kb.psalloc()

                    def trs(e, pts_=pts_, half=half):
                        ins = None
                        for c in range(4):
                            cc = half * 4 + c
                            ins = e.transpose(out=pts_[:, c * 128:(c + 1) * 128], in_=Sin[:, cc * 128:(cc + 1) * 128],
                                              identity=identf)
                        return ins
                    kb.op("pe", trs, reads=[Sin, cf], writes=[pts_])
                    kb.op("act" if half == 0 else "dve",
                          (lambda e, pts_=pts_, half=half: e.copy(out=stb16[:, half * 512:(half + 1) * 512], in_=pts_[:, :]))
                          if half == 0 else
                          (lambda e, pts_=pts_, half=half: e.tensor_copy(out=stb16[:, half * 512:(half + 1) * 512], in_=pts_[:, :])),
                          reads=[pts_], writes=[stb16], first=None if half == 0 else [])
                    kb.psrel(pts_)

                def yimm(e):
                    ins = None
                    for c in range(8):
                        ins = e.matmul(pyit[c // 4][:, (c % 4) * 128 + c0:(c % 4) * 128 + c0 + 8],
                                       lhsT=stb16[:, c * 128:(c + 1) * 128], rhs=xbcsT[:, 10 + c // 4, c0:c0 + 8],
                                       start=True, stop=True)
                    return ins
                kb.op("pe", yimm, reads=[stb16, xbcsT], writes=[pyit[0], pyit[1]], first=None if s_ == 0 else [])
                Bs = kb.nxt("Bs")
                kb.op("pool", lambda e, Bs=Bs, s_=s_: e.tensor_scalar_mul(out=Bs[:, :], in0=btok[:, :],
                                                                      scalar1=cf[:, CF_BM + s_:CF_BM + s_ + 1]), reads=[btok, cf], writes=[Bs])
                pup = [kb.psalloc(), kb.psalloc()]

                def upmm(e, Bs=Bs, pup=pup):
                    ins = None
                    for c in range(8):
                        g = c // 4
                        ins = e.matmul(pup[g][:, (c % 4) * 128:(c % 4 + 1) * 128], lhsT=xw[:, c * 128:(c + 1) * 128],
                                       rhs=Bs[:, g * 128:(g + 1) * 128], start=True, stop=True)
                    return ins
                kb.op("pe", upmm, reads=[xw, Bs], writes=[pup[0], pup[1]])
                for c in range(8):
                    kb.op("dve", lambda e, c=c, s_=s_, Sin=Sin, pup=pup: e.scalar_tensor_tensor(
                        out=Sin[:, c * 128:(c + 1) * 128], in0=Sin[:, c * 128:(c + 1) * 128], scalar=decT[:, c, s_:s_ + 1],
                        in1=pup[c // 4][:, (c % 4) * 128:(c % 4 + 1) * 128], op0=ALU.mult, op1=ALU.add),
                        reads=[Sin, decT, pup[c // 4]], writes=[Sin])
                kb.psrel(pup[0], pup[1])
                kb.dma("act", o_ssm_s[l, s_, :, :].rearrange("(c p) n -> p c n", p=128),
                       Sin[:, :].rearrange("p (c n) -> p c n", c=8), reads=[Sin], writes=[o_ssm_s],
                       sembuf=sst_sem[s_ % 2])
                Kb = kb.nxt("Kb")
                Vb = kb.nxt("Vb")
                kb.dma("pool", Kb[:, :, :], ck[l, s_, :, :].rearrange("(c p) n -> p c n", p=128), writes=[Kb], sembuf=Kb)
                kb.dma("pool", Vb[:, :, :], cv[l, s_, :, :].rearrange("(c p) n -> p c n", p=128), writes=[Vb], sembuf=Vb)
                pkt = kb.psalloc()
                pktb = pkt[:, :].bitcast(BF16)

                def trk(e, Kb=Kb):
                    ins = None
                    for h in range(4):
                        for mc in range(2):
                            ins = e.transpose(out=pktb[:, (h * 2 + mc) * 128:(h * 2 + mc + 1) * 128],
                                              in_=Kb[:, mc, h * 128:(h + 1) * 128], identity=identb)
                    return ins
                kb.op("pe", trk, reads=[Kb, cb], writes=[pkt])
                kb.op("act", lambda e: e.copy(out=ktl[:, :, :].rearrange("p h m -> p (h m)"), in_=pktb[:, :]),
                      reads=[pkt], writes=[ktl])
                kb.psrel(pkt)
                psc = kb.psalloc()

                def scmm(e):
                    ins = None
                    for h in range(4):
                        for mc in range(2):
                            ins = e.matmul(psc[:, mc * 32 + h * 8:mc * 32 + h * 8 + 8], lhsT=ktl[:, h, mc * 128:(mc + 1) * 128],
                                           rhs=qT[:, h, c0:c0 + 8], start=True, stop=True)
                    return ins
                kb.op("pe", scmm, reads=[ktl, qT], writes=[psc])
                kb.op("act", lambda e: e.activation(out=pts[:, :], in_=psc[:, 0:64], func=AF.Exp, scale=SCALE),
                      reads=[psc], writes=[pts])
                kb.psrel(psc)

                def smm(e, s_=s_, c0=c0, Vb=Vb):
                    o = psum_[:, s_ * 32:(s_ + 1) * 32]
                    e.matmul(o, lhsT=onesb, rhs=pts[:, 0:32], start=True, stop=False)
                    ins = e.matmul(o, lhsT=onesb, rhs=pts[:, 32:64], start=False, stop=True)
                    for h in range(4):
                        for mc in range(2):
                            ins = e.matmul(pxa[:, h * 128 + c0:h * 128 + c0 + 8], lhsT=Vb[:, mc, h * 128:(h + 1) * 128],
                                           rhs=pts[:, mc * 32 + h * 8:mc * 32 + h * 8 + 8], start=(mc == 0), stop=(mc == 1))
                    return ins
                kb.op("pe", smm, reads=[cb, pts, Vb], writes=[psum_, pxa], first=None if s_ == 0 else [])
            return pyit

        tile_state = {}
        d2d = Buf("d2d", None)
        sst_sem = [Buf("sst0", None), Buf("sst1", None)]
        order = []
        for l in range(cfg['layers']):
            for ti in range(cfg['tiles']):
                order.append((l, ti, False))
            if cfg['sample']:
                order.append((l, 0, True))
        if order:
            load_win(0)
            load_poolw(0)
            load_attn(0)
            load_wout(0)
        for i, (l, ti, smp) in enumerate(order):
            do_tile(l, ti, smp, order[i + 1] if i + 1 < len(order) else None)
        kb.finish(outs)
        print("total ops", kb.nops)
    return nc


def _constants():
    cf = np.zeros((128, NCF), np.float32)
    r = np.arange(128)
    cf[:, CF_ID:CF_ID + 128] = np.eye(128)
    tri = (r[:, None] <= r[None, :]).astype(np.float32)
    same = (r[:, None] // 8 == r[None, :] // 8).astype(np.float32)
    cf[:, CF_TRI:CF_TRI + 128] = tri
    cf[:, CF_TRIS:CF_TRIS + 128] = tri * same
    cf[:, CF_SU:CF_SU + 128] = (r[:, None] > r[None, :]).astype(np.float32)
    cf[:, CF_ONE:CF_ONE + 128] = 1.0
    cf[:, CF_BONE:CF_BONE + 128] = same
    cf[:, CF_BM:CF_BM + 16] = (r[:, None] // 8 == np.arange(16)[None, :]).astype(np.float32)
    cb = np.zeros((128, NCB), np.float32)
    cb[:, CB_ID:CB_ID + 128] = np.eye(128)
    cb[:, CB_ONE:CB_ONE + 128] = 1.0
    for g, w in enumerate(POOL_W):
        s = r[:, None]
        t = r[None, :]
        cur = ((s <= t) & (s > t - w)).astype(np.float32) - w * (s == t)
        prev = np.zeros((128, 128), np.float32)
        srel = s - 128
        prev[:, :] = ((srel > t - w)).astype(np.float32)
        prev[:64, :] = 0.0
        cnt0 = np.minimum(t + 1, w).astype(np.float32)
        cur0 = ((s <= t) & (s > t - w)).astype(np.float32) - cnt0 * (s == t)
        cf[:, CF_IC0 + g * 128:CF_IC0 + (g + 1) * 128] = np.broadcast_to(1.0 / cnt0, (128, 128))
        ss_, ts_ = s // 8, s % 8
        sc_, tc_ = t // 8, t % 8
        bs = ((ss_ == sc_) & (ts_ <= tc_) & (ts_ > tc_ - w)).astype(np.float32) - w * (s == t)
        sta = np.zeros((128, 128), np.float32)
        stbm = np.zeros((128, 128), np.float32)
        rows = np.arange(120)
        sq, rr = rows // 15, rows % 15
        for half, m in ((0, sta), (1, stbm)):
            m[:120, :] = ((sq[:, None] + 8 * half == sc_) & ((rr[:, None] - 15) > (tc_ - w))).astype(np.float32)
        for k, m in enumerate((cur, prev, cur0, bs, sta, stbm)):
            o = CB_BAND + (g * 6 + k) * 128
            cb[:, o:o + 128] = m
    return cf, cb


_NC_CACHE = {}
_DBG_CFG = None


def kernel(x_prompt, x_sample, mem_prompt, state_pool, state_conv, state_ssm, cache_mem_k, cache_mem_v,
           norm_g, w_in, pool_w, pool_scale, conv_w, conv_b, dt_bias, a_log, d_skip, ssd_norm_g,
           mem_norm_g, w_mem_k, w_mem_v, w_out, final_norm_g):
    f = lambda a: np.ascontiguousarray(np.asarray(a, dtype=np.float32))
    x_prompt, x_sample, mem_prompt = f(x_prompt), f(x_sample), f(mem_prompt)
    state_pool, state_conv, state_ssm = f(state_pool), f(state_conv), f(state_ssm)
    cache_mem_k, cache_mem_v = f(cache_mem_k), f(cache_mem_v)
    cf, cb = _constants()
    colpar = np.zeros((128, DEPTH, NCP), np.float32)
    for l in range(DEPTH):
        colpar[:, l, CP_NG:CP_NG + 8] = f(norm_g)[l].reshape(8, 128).T
        colpar[:, l, CP_MG:CP_MG + 8] = f(mem_norm_g)[l].reshape(8, 128).T
        colpar[:, l, CP_SG:CP_SG + 8] = f(ssd_norm_g)[l].reshape(8, 128).T
        colpar[:, l, CP_PS:CP_PS + 4] = f(pool_scale)[l].reshape(4, 128).T
        colpar[:, l, CP_CW:CP_CW + 48] = f(conv_w)[l].reshape(4, 12, 128).transpose(2, 1, 0).reshape(128, 48)
        colpar[:, l, CP_CB:CP_CB + 12] = f(conv_b)[l].reshape(12, 128).T
    colpar = np.ascontiguousarray(colpar.reshape(128, DEPTH * NCP))
    rowpar = np.ascontiguousarray(np.concatenate([f(dt_bias), f(a_log), f(d_skip)], axis=1).reshape(-1))
    shared = {
        "w_in": f(w_in), "w_out": f(w_out), "pool_w": f(pool_w), "w_mk": f(w_mem_k), "w_mv": f(w_mem_v),
        "colpar": colpar, "rowpar": rowpar, "fng": f(final_norm_g), "cstf": cf, "cstb": cb,
    }
    in_maps = []
    for c in range(NCORES):
        sl = slice(c * NS, (c + 1) * NS)
        m = dict(shared)
        m["xp"] = x_prompt[c]
        m["xs"] = np.ascontiguousarray(x_sample[sl].reshape(128, D))
        m["mem"] = mem_prompt[c]
        m["st_pool"] = np.ascontiguousarray(state_pool[:, sl].reshape(DEPTH, NS * 15, 512))
        m["st_conv"] = np.ascontiguousarray(state_conv[:, sl].reshape(DEPTH, NS * 3, 1536))
        m["st_ssm"] = np.ascontiguousarray(state_ssm[:, sl].reshape(DEPTH, NS, 1024, 128))
        m["ck"] = np.ascontiguousarray(cache_mem_k[:, sl].reshape(DEPTH, NS, 256, 512))
        m["cv"] = np.ascontiguousarray(cache_mem_v[:, sl].reshape(DEPTH, NS, 256, 512))
        in_maps.append(m)
    if "nc" not in _NC_CACHE:
        _NC_CACHE["nc"] = build_nc(_DBG_CFG)
    nc = _NC_CACHE["nc"]
    res = run_bass_kernel_spmd(nc, in_maps, core_ids=list(range(NCORES)))
    R = res.results
    g = lambda name, c: np.asarray(R[c][name], dtype=np.float32)
    y_prompt = np.stack([g("yp", c) for c in range(NCORES)]).reshape(8, 2048, D)
    y_sample = np.concatenate([g("ys", c).reshape(NS, 8, D) for c in range(NCORES)], axis=0)
    new_pool_p = np.stack([g("o_pool_p", c) for c in range(NCORES)], axis=1)
    new_conv_p = np.stack([g("o_conv_p", c) for c in range(NCORES)], axis=1)
    new_ssm_p = np.stack([g("o_ssm_p", c).reshape(DEPTH, 16, 64, 128) for c in range(NCORES)], axis=1)
    new_mk = np.stack([g("o_mk", c).reshape(DEPTH, 256, 4, 128) for c in range(NCORES)], axis=1)
    new_mv = np.stack([g("o_mv", c).reshape(DEPTH, 256, 4, 128) for c in range(NCORES)], axis=1)
    new_pool_s = np.concatenate([g("o_pool_s", c) for c in range(NCORES)], axis=1)
    new_conv_s = np.concatenate([g("o_conv_s", c) for c in range(NCORES)], axis=1)
    new_ssm_s = np.concatenate([g("o_ssm_s", c).reshape(DEPTH, NS, 16, 64, 128) for c in range(NCORES)], axis=1)
    return (y_prompt, y_sample, new_pool_p, new_conv_p, new_ssm_p, new_mk, new_mv, new_pool_s, new_conv_s, new_ssm_s)
```

```python
import contextlib
import math
import numpy as np
import concourse.bass as bass
import concourse.mybir as mybir
from concourse.bass_utils import run_bass_kernel_spmd

F32 = mybir.dt.float32
BF16 = mybir.dt.bfloat16
AF = mybir.ActivationFunctionType
ALU = mybir.AluOpType

NCORES = 8
DEPTH = 4
D = 1024
NT = 16
NS = 16
DIN = 4624
C_U, C_GP, C_Z, C_XBC, C_DT, C_Q, C_GX = 0, 512, 1024, 2048, 3584, 3600, 4112
EPS = 1e-6
POOL_W = (2, 4, 8, 16)
SEM_MAX = 3600

CP_NG, CP_MG, CP_SG, CP_PS, CP_CW, CP_CB, NCP = 0, 8, 16, 24, 28, 76, 88
CF_ID, CF_TRI, CF_TRIS, CF_SU, CF_ONE, CF_BONE, CF_BM, CF_IC0, NCF = 0, 128, 256, 384, 512, 640, 768, 784, 1296
CB_ID, CB_ONE, CB_BAND, NCB = 0, 128, 256, 256 + 24 * 128


class Buf:
    __slots__ = ("name", "t", "writes", "reads", "old", "dsem", "dval", "partial")

    def __init__(self, name, t, partial=False):
        self.name = name
        self.t = t
        self.writes = {}
        self.reads = {}
        self.old = {}
        self.dsem = None
        self.dval = 0
        self.partial = partial

    def __getitem__(self, k):
        return self.t[k]


def _merge(d, ev):
    for k, v in ev.items():
        if d.get(k, 0) < v:
            d[k] = v


class KB:
    def __init__(self, nc, stack):
        self.nc = nc
        self.stack = stack
        self.E = {"pe": nc.tensor, "act": nc.scalar, "dve": nc.vector, "pool": nc.gpsimd, "sp": nc.sync}
        self.sem, self.cnt, self.seen = {}, {}, {}
        for e in self.E:
            self.sem[e] = stack.enter_context(nc.semaphore("s_" + e))
            self.cnt[e] = 0
            self.seen[e] = {}
        self.nbuf = 0
        self.rr = {}
        self.psfree = []
        self.maxwaited = {}
        self.dmafinal = {}
        self.nops = 0
        self.limit = 1 << 60

    def sb(self, name, shape, dtype):
        self.nbuf += 1
        t = self.stack.enter_context(self.nc.sbuf_tensor(f"{name}_{self.nbuf}", list(shape), dtype))
        return Buf(name, t)

    def ps(self, name, shape, dtype=F32):
        self.nbuf += 1
        t = self.stack.enter_context(self.nc.psum_tensor(f"{name}_{self.nbuf}", list(shape), dtype))
        return Buf(name, t)

    def dram(self, name, shape, dtype, kind):
        t = self.nc.dram_tensor(name, list(shape), dtype, kind=kind)
        return Buf(name, t, partial=True)

    def pool(self, name, shape, dtype, n):
        bufs = [self.sb(f"{name}{i}", shape, dtype) for i in range(n)]
        self.rr[name] = [bufs, 0]
        return bufs

    def nxt(self, name):
        r = self.rr[name]
        b = r[0][r[1] % len(r[0])]
        r[1] += 1
        return b

    def psalloc(self):
        assert self.psfree, "out of PSUM banks"
        return self.psfree.pop(0)

    def psrel(self, *bs):
        for b in bs:
            self.psfree.append(b)

    def _isfirst(self, b, first):
        if b.partial:
            return False
        return first is None or b in first

    def _deps(self, reads, writes, first):
        dep = {}
        for b in reads:
            _merge(dep, b.writes)
        for b in writes:
            if self._isfirst(b, first):
                _merge(dep, b.writes)
                _merge(dep, b.reads)
            else:
                _merge(dep, b.old)
        return dep

    def _wait(self, e, dep):
        eng = self.E[e]
        seen = self.seen[e]
        for sem, val in dep.items():
            if seen.get(sem, 0) < val:
                eng.wait_ge(sem, val)
                seen[sem] = val
            if self.maxwaited.get(sem, 0) < val:
                self.maxwaited[sem] = val

    def _commit(self, ev, reads, writes, first):
        for b in writes:
            if self._isfirst(b, first):
                old = {}
                _merge(old, b.writes)
                _merge(old, b.reads)
                b.old = old
                b.writes = dict(ev)
                b.reads = {}
            else:
                _merge(b.writes, ev)
        for b in reads:
            if b in writes:
                continue
            _merge(b.reads, ev)

    def op(self, e, fn, reads=(), writes=(), first=None):
        self.nops += 1
        if self.nops > self.limit:
            return
        dep = self._deps(reads, writes, first)
        self._wait(e, dep)
        ins = fn(self.E[e])
        self.cnt[e] += 1
        ins.then_inc(self.sem[e], 1)
        self._commit({self.sem[e]: self.cnt[e]}, reads, writes, first)
        if self.cnt[e] >= SEM_MAX:
            self.nbuf += 1
            self.sem[e] = self.stack.enter_context(self.nc.semaphore(f"s_{e}_{self.nbuf}"))
            self.cnt[e] = 0

    def dma(self, q, out_ap, in_ap, reads=(), writes=(), first=None, sembuf=None, **kw):
        self.nops += 1
        if self.nops > self.limit:
            return
        dep = self._deps(reads, writes, first)
        b = sembuf
        if b.dsem is None or b.dval + 16 > SEM_MAX:
            self.nbuf += 1
            b.dsem = self.stack.enter_context(self.nc.semaphore(f"d_{b.name}_{self.nbuf}"))
            b.dval = 0
        if b.dsem is not None and self.maxwaited.get(b.dsem, 0) > 0:
            _merge(dep, {b.dsem: self.maxwaited[b.dsem]})
        self._wait(q, dep)
        ins = self.E[q].dma_start(out=out_ap, in_=in_ap, **kw)
        b.dval += 16
        ins.then_inc(b.dsem, 16)
        self.dmafinal[b.dsem] = b.dval
        self._commit({b.dsem: b.dval}, reads, writes, first)

    def finish(self, outs, e="sp"):
        dep = {}
        for b in outs:
            _merge(dep, b.writes)
        _merge(dep, self.dmafinal)
        self._wait(e, dep)


def A(buf, off, dims, p0=0, np_=128):
    row = int(np.prod(buf.t.shape[1:]))
    return bass.AP(buf.t, p0 * row + off, [[row, np_]] + [list(d) for d in dims])


def DA(buf, off, dims):
    return bass.AP(buf.t, off, [list(d) for d in dims])


def build_nc(cfg=None):
    cfg = dict(mem=1, layers=DEPTH, tiles=NT, sample=1) if cfg is None else cfg
    nc = bass.Bass("TRN2", target_bir_lowering=False)
    with contextlib.ExitStack() as st:
        kb = KB(nc, st)
        kb.limit = cfg.get('stop', 1 << 60)
        I, O = "ExternalInput", "ExternalOutput"
        xp = kb.dram("xp", [2048, D], F32, I)
        xs = kb.dram("xs", [128, D], F32, I)
        mem = kb.dram("mem", [256, D], F32, I)
        st_pool = kb.dram("st_pool", [DEPTH, NS * 15, 512], F32, I)
        st_conv = kb.dram("st_conv", [DEPTH, NS * 3, 1536], F32, I)
        st_ssm = kb.dram("st_ssm", [DEPTH, NS, 1024, 128], F32, I)
        ck = kb.dram("ck", [DEPTH, NS, 256, 512], F32, I)
        cv = kb.dram("cv", [DEPTH, NS, 256, 512], F32, I)
        w_in = kb.dram("w_in", [DEPTH, D, DIN], F32, I)
        w_out = kb.dram("w_out", [DEPTH, 2048, D], F32, I)
        pool_w = kb.dram("pool_w", [DEPTH, 4, 128, 128], F32, I)
        w_mk = kb.dram("w_mk", [DEPTH, D, 512], F32, I)
        w_mv = kb.dram("w_mv", [DEPTH, D, 512], F32, I)
        colpar_d = kb.dram("colpar", [128, DEPTH * NCP], F32, I)
        rowpar_d = kb.dram("rowpar", [DEPTH * 48], F32, I)
        fng_d = kb.dram("fng", [D], F32, I)
        cstf_d = kb.dram("cstf", [128, NCF], F32, I)
        cstb_d = kb.dram("cstb", [128, NCB], F32, I)

        yp = kb.dram("yp", [2048, D], F32, O)
        ys = kb.dram("ys", [128, D], F32, O)
        o_pool_p = kb.dram("o_pool_p", [DEPTH, 15, 512], F32, O)
        o_conv_p = kb.dram("o_conv_p", [DEPTH, 3, 1536], F32, O)
        o_ssm_p = kb.dram("o_ssm_p", [DEPTH, 1024, 128], F32, O)
        o_mk = kb.dram("o_mk", [DEPTH, 256, 512], F32, O)
        o_mv = kb.dram("o_mv", [DEPTH, 256, 512], F32, O)
        o_pool_s = kb.dram("o_pool_s", [DEPTH, NS, 15, 512], F32, O)
        o_conv_s = kb.dram("o_conv_s", [DEPTH, NS, 3, 1536], F32, O)
        o_ssm_s = kb.dram("o_ssm_s", [DEPTH, NS, 1024, 128], F32, O)
        hs = kb.dram("hs", [17 * 128, D], F32, "Internal")
        ktscr = kb.dram("ktscr", [DEPTH, 128, 1024], BF16, "Internal")
        hsB = [Buf(f"hs{i}", hs.t, partial=True) for i in range(17)]
        outs = [yp, ys, o_pool_p, o_conv_p, o_ssm_p, o_mk, o_mv, o_pool_s, o_conv_s, o_ssm_s]

        pad0 = kb.sb("pad0", [128, 16], F32)
        win = kb.sb("win", [128, 8, DIN], BF16)
        wout = kb.sb("wout", [128, 16, D], BF16)
        poolw = kb.sb("poolw", [128, 4, 128], BF16)
        ktl = kb.sb("ktl", [128, 4, 256], BF16)
        vl = kb.sb("vl", [128, 2, 512], BF16)
        fng = kb.sb("fng", [128, D], F32)
        colpar = kb.sb("colpar", [128, DEPTH * NCP], F32)
        rowpar = kb.sb("rowpar", [128, DEPTH * 48], F32)
        abc = kb.sb("abc", [128, DEPTH * 16], F32)
        cf = kb.sb("cf", [128, NCF], F32)
        cb = kb.sb("cb", [128, NCB], BF16)

        kb.pool("hbuf", [128, D], F32, 2)
        xn = kb.sb("xn", [128, D], BF16)
        xnT = kb.sb("xnT", [128, 8, 128], BF16)
        kb.pool("ubf", [128, 512], BF16, 2)
        sgpT = kb.sb("sgpT", [128, 4, 128], BF16)
        dT = kb.sb("dT", [128, 4, 128], BF16)
        qT = kb.sb("qT", [128, 4, 128], BF16)
        sgxT = kb.sb("sgxT", [128, 4, 128], BF16)
        sz = kb.sb("sz", [128, D], BF16)
        xb = kb.sb("xb", [128, 12 * 176], F32)
        carry = kb.sb("carry", [128, 12, 3], F32)
        big6 = kb.sb("big6", [128, 12 * 128], F32)
        big6b = Buf("big6b", big6.t)
        xbcsT = kb.sb("xbcsT", [128, 12, 128], BF16)
        xdt = kb.sb("xdt", [128, D], BF16)
        xd = kb.sb("xd", [128, D], BF16)
        xw = kb.sb("xw", [128, D], BF16)
        btok = kb.sb("btok", [128, 256], BF16)
        sm = kb.sb("sm", [128, 6 * 16], F32)
        sm2 = kb.sb("sm2", [128, 16], F32)
        Rb = kb.sb("Rb", [128, 4, 128], F32)
        kb.pool("E", [128, 4, 128], BF16, 2)
        kb.pool("MT", [128, 4, 128], BF16, 2)
        cbm = kb.sb("cbm", [128, 2, 128], BF16)
        Sb = kb.pool("Sb", [128, D], F32, 2)
        stb16 = kb.sb("stb16", [128, D], BF16)
        actT = kb.sb("actT", [128, 16, 128], BF16)
        scr = kb.sb("scr", [128, D], F32)
        kb.pool("Kb", [128, 2, 512], BF16, 2)
        kb.pool("Vb", [128, 2, 512], BF16, 2)
        kb.pool("Bs", [128, 256], BF16, 1)
        spool16 = kb.sb("spool16", [128, 2, 512], BF16)
        decT = kb.sb("decT", [128, 8, 16], F32)
        pts = kb.sb("pts", [128, 64], BF16)
        ctmp = kb.sb("ctmp", [128, 128], F32)
        print("sbuf bytes remaining after alloc:", nc.sbuf_bytes_remaining)

        for i in range(8):
            kb.psfree.append(kb.ps(f"bank{i}", [128, 512], F32))

        def smt(i):
            return sm[:, i * 16:(i + 1) * 16]

        kb.dma("sp", cf[:, :], cstf_d[:, :], writes=[cf], sembuf=cf)
        kb.dma("pool", cb[:, :], cstb_d[:, :], writes=[cb], sembuf=cb)
        kb.dma("sp", colpar[:, :], colpar_d[:, :], writes=[colpar], sembuf=colpar)
        kb.dma("sp", rowpar[:, :], DA(rowpar_d, 0, [[0, 128], [1, DEPTH * 48]]), writes=[rowpar], sembuf=rowpar)
        kb.dma("sp", fng[:, :], DA(fng_d, 0, [[0, 128], [1, D]]), writes=[fng], sembuf=fng)
        for l in range(DEPTH):
            kb.op("act", lambda e, l=l: e.activation(out=abc[:, l * 16:(l + 1) * 16],
                                                      in_=rowpar[:, l * 48 + 16:l * 48 + 32], func=AF.Exp),
                  reads=[rowpar], writes=[abc])
        kb.op("dve", lambda e: e.tensor_scalar_mul(out=abc[:, :], in0=abc[:, :], scalar1=-1.0),
              reads=[abc], writes=[abc])

        identf = cf[:, CF_ID:CF_ID + 128]
        identb = cb[:, CB_ID:CB_ID + 128]
        onesb = cb[:, CB_ONE:CB_ONE + 128]

        def band(g, k):
            o = CB_BAND + (g * 6 + k) * 128
            return cb[:, o:o + 128]

        def cpcol(l, off, n=1):
            return colpar[:, l * NCP + off:l * NCP + off + n]

        def rmsnorm_T(hb, gcol_off, l):
            ss = sm2[:, 0:1]
            rs = sm2[:, 1:2]
            kb.op("act", lambda e: e.activation(out=xn[:, :], in_=hb[:, :], func=AF.Square, accum_out=ss),
                  reads=[hb], writes=[xn, sm2])
            kb.op("act", lambda e: e.activation(out=rs, in_=ss, func=AF.Ln, scale=1.0 / D, bias=EPS),
                  reads=[sm2], writes=[sm2])
            kb.op("act", lambda e: e.activation(out=rs, in_=rs, func=AF.Exp, scale=-0.5), reads=[sm2], writes=[sm2])
            kb.op("dve", lambda e: e.tensor_scalar_mul(out=xn[:, :], in0=hb[:, :], scalar1=rs),
                  reads=[hb, sm2], writes=[xn])
            pt = kb.psalloc()
            ptb = pt[:, :].bitcast(BF16)

            def tr(e):
                ins = None
                for k in range(8):
                    ins = e.transpose(out=ptb[:, k * 128:(k + 1) * 128], in_=xn[:, k * 128:(k + 1) * 128], identity=identb)
                return ins
            kb.op("pe", tr, reads=[xn, cb], writes=[pt])
            kb.op("dve", lambda e: e.tensor_tensor(out=xnT[:, :, :], in0=ptb[:, :].rearrange("p (k t) -> p k t", k=8),
                                                   in1=A(colpar, l * NCP + gcol_off, [[1, 8], [0, 128]]), op=ALU.mult),
                  reads=[pt, colpar], writes=[xnT])
            kb.psrel(pt)

        def mm_tok(pbank, ncols, wbuf, wap_fn):
            def f(e):
                ins = None
                for k in range(8):
                    ins = e.matmul(pbank[:, 0:ncols], lhsT=xnT[:, k, :], rhs=wap_fn(k), start=(k == 0), stop=(k == 7))
                return ins
            kb.op("pe", f, reads=[xnT, wbuf], writes=[pbank])

        def mm_feat(pbank, nchunk, wbuf, wap_fn):
            def f(e):
                ins = None
                for c in range(nchunk):
                    for k in range(8):
                        ins = e.matmul(pbank[:, c * 128:(c + 1) * 128], lhsT=wap_fn(k, c), rhs=xnT[:, k, :],
                                       start=(k == 0), stop=(k == 7))
                return ins
            kb.op("pe", f, reads=[xnT, wbuf], writes=[pbank])

        for l in range(DEPTH if cfg['mem'] else 0):
            s = l % 2
            for k in range(8):
                kb.dma("pool", wout[:, 8 * s + k, 0:512], w_mk[l, k * 128:(k + 1) * 128, :], writes=[wout],
                       first=None if (k == 0 and s == 0) else [], sembuf=wout)
                kb.dma("pool", wout[:, 8 * s + k, 512:1024], w_mv[l, k * 128:(k + 1) * 128, :], writes=[wout],
                       first=[], sembuf=wout)
            for mt in range(2):
                hb = kb.nxt("hbuf")
                kb.dma("sp", hb[:, :], mem[mt * 128:(mt + 1) * 128, :], writes=[hb], sembuf=hb)
                rmsnorm_T(hb, CP_MG, l)
                pk = kb.psalloc()
                mm_tok(pk, 512, wout, lambda k: wout[:, 8 * s + k, 0:512])
                kb.op("act", lambda e: e.copy(out=scr[:, 0:512], in_=pk[:, :]), reads=[pk], writes=[scr])
                kb.psrel(pk)
                pv = kb.psalloc()
                mm_tok(pv, 512, wout, lambda k: wout[:, 8 * s + k, 512:1024])
                kb.op("dve", lambda e: e.tensor_copy(out=scr[:, 512:1024], in_=pv[:, :]), reads=[pv], writes=[scr], first=[])
                kb.psrel(pv)
                kb.dma("sp", o_mk[l, mt * 128:(mt + 1) * 128, :], scr[:, 0:512], reads=[scr], writes=[o_mk], sembuf=scr)
                kb.dma("sp", o_mv[l, mt * 128:(mt + 1) * 128, :], scr[:, 512:1024], reads=[scr], writes=[o_mv], sembuf=scr)
                pkt = kb.psalloc()
                mm_feat(pkt, 4, wout, lambda k, c: wout[:, 8 * s + k, c * 128:(c + 1) * 128])
                kb.op("act", lambda e: e.copy(out=ktl[:, :, mt * 128:(mt + 1) * 128],
                                              in_=pkt[:, :].rearrange("p (h m) -> p h m", h=4)),
                      reads=[pkt], writes=[ktl], first=None if mt == 0 else [])
                kb.psrel(pkt)
            kb.dma("sp", ktscr[l, :, :], ktl[:, :, :].rearrange("p h m -> p (h m)"), reads=[ktl], writes=[ktscr], sembuf=ktl)

        def load_win(l):
            for k in range(8):
                kb.dma("pool", win[:, k, :], w_in[l, k * 128:(k + 1) * 128, :], writes=[win],
                       first=None if k == 0 else [], sembuf=win)

        def load_poolw(l):
            kb.dma("pool", poolw[:, :, :], pool_w[l, :, :, :].rearrange("g c d -> c g d"), writes=[poolw], sembuf=poolw)

        def load_attn(l):
            kb.dma("sp", ktl[:, :, :].rearrange("p h m -> p (h m)"), ktscr[l, :, :], reads=[ktscr], writes=[ktl], sembuf=ktl)
            kb.dma("pool", vl[:, :, :], o_mv[l, :, :].rearrange("(c p) n -> p c n", p=128), reads=[o_mv], writes=[vl], sembuf=vl)

        def load_wout(l):
            for k in range(16):
                kb.dma("pool", wout[:, k, :], w_out[l, k * 128:(k + 1) * 128, :], writes=[wout],
                       first=None if k == 0 else [], sembuf=wout)

        SCALE = 1.0 / math.sqrt(128.0)

        XB = [xb, Buf("xb1", xb.t), Buf("xb2", xb.t)]
        BIG = [big6, Buf("big1", big6.t), big6b]
        XS = [xbcsT, Buf("xs1", xbcsT.t), Buf("xs2", xbcsT.t)]
        sm2n = Buf("sm2n", sm2.t)

        def norm_stage(l, ti, sample):
            hidx = 16 if sample else ti
            hb = kb.nxt("hbuf")
            if l == 0:
                src = xs[:, :] if sample else xp[ti * 128:(ti + 1) * 128, :]
                kb.dma("sp", hb[:, :], src, writes=[hb], sembuf=hb)
            else:
                kb.dma("sp", hb[:, :], hs[hidx * 128:(hidx + 1) * 128, :], reads=[hsB[hidx]], writes=[hb], sembuf=hb)
            rmsnorm_T(hb, CP_NG, l)
            return hb

        def do_tile(l, ti, sample, nxt_tile):
            last_p = (not sample) and ti == cfg['tiles'] - 1
            LL = cfg['layers']
            special = sample or last_p
            hidx = 16 if sample else ti
            pre = tile_state.pop("pre", None)
            if pre is not None:
                hb, xbc_done = pre
            else:
                hb, xbc_done = norm_stage(l, ti, sample), False
            rp = l * 48
            afirst = [None]

            def actT_first():
                f = afirst[0]
                afirst[0] = []
                return f

            if not xbc_done:
                if sample:
                    for half in range(2):
                        kb.dma("sp", scr[0:48, 0:768], st_conv[l, :, half * 768:(half + 1) * 768], writes=[scr], sembuf=scr)
                        pc = kb.psalloc()

                        def trc(e, pc=pc):
                            ins = None
                            for c in range(6):
                                ins = e.transpose(out=pc[:, c * 48:(c + 1) * 48], in_=scr[0:48, c * 128:(c + 1) * 128],
                                                  identity=cf[0:48, CF_ID:CF_ID + 48])
                            return ins
                        kb.op("pe", trc, reads=[scr, cf], writes=[pc])
                        wr = [XB[0], XB[1]] if half == 0 else [XB[1], XB[2]]
                        kb.op("act", lambda e, pc=pc, half=half: e.copy(
                            out=A(xb, half * 6 * 176, [[176, 6], [11, 16], [1, 3]]),
                            in_=A(pc, 0, [[48, 6], [3, 16], [1, 3]])), reads=[pc], writes=wr,
                            first=None if half == 0 else [XB[2]])
                        kb.psrel(pc)
                else:
                    if ti == 0:
                        kb.op("pool", lambda e: e.memset(carry[:, :, :], 0.0), writes=[carry])
                    kb.op("pool", lambda e: e.tensor_copy(out=A(xb, 0, [[176, 12], [1, 3]]), in_=carry[:, :, :]),
                          reads=[carry], writes=XB)
                for b3 in range(3):
                    pxb = kb.psalloc()
                    mm_feat(pxb, 4, win, lambda k, c, b3=b3: win[:, k, C_XBC + (b3 * 4 + c) * 128:C_XBC + (b3 * 4 + c + 1) * 128])
                    if sample:
                        o_ap = A(xb, b3 * 4 * 176 + 3, [[176, 4], [11, 16], [1, 8]])
                        i_ap = A(pxb, 0, [[128, 4], [8, 16], [1, 8]])
                    else:
                        o_ap = A(xb, b3 * 4 * 176 + 3, [[176, 4], [1, 128]])
                        i_ap = A(pxb, 0, [[128, 4], [1, 128]])
                    kb.op("act", lambda e, o_ap=o_ap, i_ap=i_ap: e.copy(out=o_ap, in_=i_ap), reads=[pxb], writes=[XB[b3]], first=[])
                    kb.psrel(pxb)
                if not sample:
                    kb.op("pool", lambda e: e.tensor_copy(out=carry[:, :, :], in_=A(xb, 128, [[176, 12], [1, 3]])),
                          reads=XB, writes=[carry])
            pq = kb.psalloc()
            mm_feat(pq, 4, win, lambda k, c: win[:, k, C_Q + c * 128:C_Q + (c + 1) * 128])
            kb.op("act", lambda e: e.copy(out=qT[:, :, :].rearrange("p a b -> p (a b)"), in_=pq[:, :]),
                  reads=[pq], writes=[qT])
            kb.psrel(pq)
            pgx = kb.psalloc()
            mm_feat(pgx, 4, win, lambda k, c: win[:, k, C_GX + c * 128:C_GX + (c + 1) * 128])
            kb.op("act", lambda e: e.activation(out=sgxT[:, :, :].rearrange("p a b -> p (a b)"), in_=pgx[:, :], func=AF.Silu),
                  reads=[pgx], writes=[sgxT])
            kb.psrel(pgx)

            if not sample:
                psc = [kb.psalloc(), kb.psalloc()]
                for hp in range(2):
                    def scmm(e, hp=hp):
                        ins = None
                        for hh in range(2):
                            h = hp * 2 + hh
                            for mc in range(2):
                                ins = e.matmul(psc[hp][:, (hh * 2 + mc) * 128:(hh * 2 + mc + 1) * 128],
                                               lhsT=ktl[:, h, mc * 128:(mc + 1) * 128], rhs=qT[:, h, :], start=True, stop=True)
                        return ins
                    kb.op("pe", scmm, reads=[ktl, qT], writes=[psc[hp]])
                PT = [kb.nxt("E"), kb.nxt("E")]
                for hp in range(2):
                    kb.op("act", lambda e, hp=hp: e.activation(out=PT[hp][:, :, :].rearrange("p a b -> p (a b)"),
                                                               in_=psc[hp][:, :], func=AF.Exp, scale=SCALE),
                          reads=[psc[hp]], writes=[PT[hp]])
                kb.psrel(psc[0], psc[1])
                psum_ = kb.psalloc()
                pxa = kb.psalloc()

                def summm(e):
                    ins = None
                    for h in range(4):
                        for mc in range(2):
                            ins = e.matmul(psum_[:, h * 128:(h + 1) * 128], lhsT=onesb, rhs=PT[h // 2][:, (h % 2) * 2 + mc, :],
                                           start=(mc == 0), stop=(mc == 1))
                    return ins
                kb.op("pe", summm, reads=[cb, PT[0], PT[1]], writes=[psum_])

                def pvmm(e):
                    ins = None
                    for h in range(4):
                        for mc in range(2):
                            ins = e.matmul(pxa[:, h * 128:(h + 1) * 128], lhsT=vl[:, mc, h * 128:(h + 1) * 128],
                                           rhs=PT[h // 2][:, (h % 2) * 2 + mc, :], start=(mc == 0), stop=(mc == 1))
                    return ins
                kb.op("pe", pvmm, reads=[vl, PT[0], PT[1]], writes=[pxa])

            for ch in range(12):
                gi = ch // 4
                if sample:
                    def xin(k, ch=ch):
                        return A(xb, ch * 176 + k, [[11, 16], [1, 8]])
                    tout = A(big6, ch * 128, [[8, 16], [1, 8]])
                else:
                    def xin(k, ch=ch):
                        return A(xb, ch * 176 + k, [[1, 128]])
                    tout = A(big6, ch * 128, [[1, 128]])
                if ch < 8:
                    kb.op("dve", lambda e, xin=xin, tout=tout, ch=ch: e.tensor_scalar_mul(
                        out=tout, in0=xin(0), scalar1=cpcol(l, CP_CW + ch * 4 + 0)),
                        reads=[XB[gi], colpar], writes=[BIG[gi]], first=None if ch % 4 == 0 else [])
                    for k in range(1, 4):
                        kb.op("dve", lambda e, xin=xin, tout=tout, ch=ch, k=k: e.scalar_tensor_tensor(
                            out=tout, in0=xin(k), scalar=cpcol(l, CP_CW + ch * 4 + k), in1=tout, op0=ALU.mult, op1=ALU.add),
                            reads=[XB[gi], colpar, BIG[gi]], writes=[BIG[gi]], first=[])
                else:
                    bdims = [[0, 16], [0, 8]] if sample else [[0, 128]]
                    t2 = A(ctmp, 0, [[8, 16], [1, 8]]) if sample else ctmp[:, :]

                    def wbc(k, ch=ch, bdims=bdims):
                        return A(colpar, l * NCP + CP_CW + ch * 4 + k, bdims)
                    kb.op("pool", lambda e, xin=xin, tout=tout, wbc=wbc: e.tensor_tensor(out=tout, in0=xin(0), in1=wbc(0), op=ALU.mult),
                          reads=[XB[gi], colpar], writes=[BIG[gi]], first=None if ch % 4 == 0 else [])
                    for k in range(1, 4):
                        kb.op("pool", lambda e, xin=xin, t2=t2, wbc=wbc, k=k: e.tensor_tensor(out=t2, in0=xin(k), in1=wbc(k), op=ALU.mult),
                              reads=[XB[gi], colpar], writes=[ctmp])
                        kb.op("pool", lambda e, tout=tout, t2=t2: e.tensor_tensor(out=tout, in0=tout, in1=t2, op=ALU.add),
                              reads=[BIG[gi], ctmp], writes=[BIG[gi]], first=[])
                kb.op("act", lambda e, ch=ch: e.activation(out=xbcsT[:, ch, :], in_=big6[:, ch * 128:(ch + 1) * 128], func=AF.Silu,
                                                           bias=cpcol(l, CP_CB + ch)),
                      reads=[BIG[gi], colpar], writes=[XS[gi]], first=None if ch % 4 == 0 else [])

            def attn_tail(psum_, pxa):
                if sample:
                    kb.op("dve", lambda e: e.reciprocal(out=A(Rb, 0, [[128, 4], [8, 16], [1, 8]]),
                                                        in_=A(psum_, 0, [[8, 4], [32, 16], [1, 8]])), reads=[psum_], writes=[Rb])
                else:
                    kb.op("dve", lambda e: e.reciprocal(out=Rb[:, :, :].rearrange("p a b -> p (a b)"), in_=psum_[:, :]),
                          reads=[psum_], writes=[Rb])
                kb.op("dve", lambda e: e.tensor_tensor(out=Rb[:, :, :].rearrange("p a b -> p (a b)"),
                                                       in0=pxa[:, :], in1=Rb[:, :, :].rearrange("p a b -> p (a b)"), op=ALU.mult),
                      reads=[pxa, Rb], writes=[Rb])
                kb.psrel(psum_, pxa)
                kb.op("pool", lambda e: e.tensor_tensor(out=actT[:, 12:16, :], in0=Rb[:, :, :], in1=sgxT[:, :, :], op=ALU.mult),
                      reads=[Rb, sgxT], writes=[actT], first=actT_first())
            if not sample:
                attn_tail(psum_, pxa)

            pu = kb.psalloc()
            mm_tok(pu, 512, win, lambda k: win[:, k, C_U:C_U + 512])
            ub = kb.nxt("ubf")
            kb.op("act", lambda e: e.copy(out=ub[:, :], in_=pu[:, :]), reads=[pu], writes=[ub])
            if special:
                kb.op("act", lambda e: e.copy(out=scr[:, 0:512], in_=pu[:, :]), reads=[pu], writes=[scr])
                if sample:
                    for s_ in range(NS):
                        kb.dma("sp", o_pool_s[l, s_, 7:15, :], scr[s_ * 8:(s_ + 1) * 8, 0:512], reads=[scr],
                               writes=[o_pool_s], sembuf=scr)
                else:
                    kb.dma("sp", o_pool_p[l, :, :], scr[113:128, 0:512], reads=[scr], writes=[o_pool_p], sembuf=scr)
            kb.psrel(pu)
            pg = kb.psalloc()
            mm_feat(pg, 4, win, lambda k, c: win[:, k, C_GP + c * 128:C_GP + (c + 1) * 128])
            kb.op("act", lambda e: e.activation(out=sgpT[:, :, :].rearrange("p a b -> p (a b)"), in_=pg[:, :], func=AF.Silu),
                  reads=[pg], writes=[sgpT])
            kb.psrel(pg)
            for half in range(2):
                pz = kb.psalloc()
                mm_tok(pz, 512, win, lambda k, half=half: win[:, k, C_Z + half * 512:C_Z + (half + 1) * 512])
                kb.op("act", lambda e, half=half, pz=pz: e.activation(out=sz[:, half * 512:(half + 1) * 512], in_=pz[:, :],
                                                                      func=AF.Silu),
                      reads=[pz], writes=[sz], first=None if half == 0 else [])
                kb.psrel(pz)
            pd = kb.psalloc()
            mm_tok(pd, 16, win, lambda k: win[:, k, C_DT:C_DT + 16])
            dt = smt(0)
            kb.op("dve", lambda e: e.tensor_tensor(out=dt, in0=pd[:, 0:16], in1=rowpar[:, rp:rp + 16], op=ALU.add),
                  reads=[pd, rowpar], writes=[sm])
            kb.psrel(pd)
            kb.op("act", lambda e: e.activation(out=dt, in_=dt, func=AF.Exp), reads=[sm], writes=[sm])
            kb.op("act", lambda e: e.activation(out=dt, in_=dt, func=AF.Ln, bias=1.0), reads=[sm], writes=[sm])
            if special:
                for part in range(3):
                    px = kb.psalloc()
                    mm_tok(px, 512, win, lambda k, part=part: win[:, k, C_XBC + part * 512:C_XBC + (part + 1) * 512])
                    kb.op("act", lambda e, px=px: e.copy(out=scr[:, 512:1024], in_=px[:, :]), reads=[px], writes=[scr])
                    kb.psrel(px)
                    if sample:
                        for s_ in range(NS):
                            kb.dma("sp", o_conv_s[l, s_, :, part * 512:(part + 1) * 512],
                                   scr[s_ * 8 + 5:s_ * 8 + 8, 512:1024], reads=[scr], writes=[o_conv_s], sembuf=scr)
                    else:
                        kb.dma("sp", o_conv_p[l, :, part * 512:(part + 1) * 512], scr[125:128, 512:1024], reads=[scr],
                               writes=[o_conv_p], sembuf=scr)
            if sample and l < LL - 1:
                load_win(l + 1)

            pdT = kb.psalloc()

            def poolmm(e):
                ins = None
                for g in range(4):
                    o = pdT[:, g * 128:(g + 1) * 128]
                    ul = ub[:, g * 128:(g + 1) * 128]
                    if sample:
                        e.matmul(o, lhsT=ul, rhs=band(g, 3), start=True, stop=False)
                        e.matmul(o, lhsT=spool16[0:120, 0, g * 128:(g + 1) * 128], rhs=band(g, 4)[0:120, :],
                                 start=False, stop=False)
                        ins = e.matmul(o, lhsT=spool16[0:120, 1, g * 128:(g + 1) * 128], rhs=band(g, 5)[0:120, :],
                                       start=False, stop=True)
                    elif ti == 0:
                        ins = e.matmul(o, lhsT=ul, rhs=band(g, 2), start=True, stop=True)
                    else:
                        e.matmul(o, lhsT=ul, rhs=band(g, 0), start=True, stop=False)
                        ins = e.matmul(o, lhsT=uprev[64:128, g * 128:(g + 1) * 128], rhs=band(g, 1)[64:128, :],
                                       start=False, stop=True)
                return ins
            if sample:
                kb.dma("pool", spool16[0:120, 0, :], st_pool[l, 0:120, :], writes=[spool16], sembuf=spool16)
                kb.dma("pool", spool16[0:120, 1, :], st_pool[l, 120:240, :], writes=[spool16], first=[], sembuf=spool16)
                kb.dma("sp", o_pool_s[l, :, 0:7, :], st_pool[l, :, :].rearrange("(s r) c -> s r c", r=15)[:, 8:15, :],
                       writes=[o_pool_s], sembuf=d2d)
                kb.op("pe", poolmm, reads=[ub, cb, spool16], writes=[pdT])
            elif ti == 0:
                kb.op("pe", poolmm, reads=[ub, cb], writes=[pdT])
            else:
                uprev = tile_state["uprev"]
                kb.op("pe", poolmm, reads=[ub, cb, uprev], writes=[pdT])
            tile_state["uprev"] = ub
            for g in range(4):
                if (not sample) and ti == 0:
                    kb.op("dve", lambda e, g=g: e.tensor_tensor(out=dT[:, g, :], in0=pdT[:, g * 128:(g + 1) * 128],
                                                                in1=cf[:, CF_IC0 + g * 128:CF_IC0 + (g + 1) * 128], op=ALU.mult),
                          reads=[pdT, cf], writes=[dT], first=None if g == 0 else [])
                else:
                    kb.op("act", lambda e, g=g: e.mul(out=dT[:, g, :], in_=pdT[:, g * 128:(g + 1) * 128], mul=1.0 / POOL_W[g]),
                          reads=[pdT], writes=[dT], first=None if g == 0 else [])
            kb.psrel(pdT)
            py = kb.psalloc()

            def poolmm2(e):
                ins = None
                for g in range(4):
                    ins = e.matmul(py[:, g * 128:(g + 1) * 128], lhsT=poolw[:, g, :], rhs=dT[:, g, :], start=True, stop=True)
                return ins
            kb.op("pe", poolmm2, reads=[poolw, dT], writes=[py])
            for g in range(4):
                kb.op("dve", lambda e, g=g: e.scalar_tensor_tensor(out=actT[:, g, :], in0=py[:, g * 128:(g + 1) * 128],
                                                                   scalar=cpcol(l, CP_PS + g), in1=sgpT[:, g, :],
                                                                   op0=ALU.mult, op1=ALU.mult),
                      reads=[py, colpar, sgpT], writes=[actT], first=actT_first())
            kb.psrel(py)
            if sample and l < LL - 1:
                load_poolw(l + 1)

            tri = cf[:, (CF_TRIS if sample else CF_TRI):(CF_TRIS if sample else CF_TRI) + 128]
            onem = cf[:, (CF_BONE if sample else CF_ONE):(CF_BONE if sample else CF_ONE) + 128]
            su = cf[:, CF_SU:CF_SU + 128]
            a_ = smt(1)
            cum = smt(2)
            e_ = smt(3)
            wend = smt(4)
            dec = smt(5)
            kb.op("dve", lambda e: e.tensor_tensor(out=a_, in0=dt, in1=abc[:, l * 16:(l + 1) * 16], op=ALU.mult),
                  reads=[sm, abc], writes=[sm], first=[])
            pcm = kb.psalloc()

            def cummm(e):
                e.matmul(pcm[:, 0:16], lhsT=tri, rhs=a_, start=True, stop=True)
                return e.matmul(pcm[:, 16:32], lhsT=onem, rhs=a_, start=True, stop=True)
            kb.op("pe", cummm, reads=[cf, sm], writes=[pcm])
            kb.op("act", lambda e: e.copy(out=cum, in_=pcm[:, 0:16]), reads=[pcm], writes=[sm], first=[])
            kb.op("act", lambda e: e.activation(out=e_, in_=pcm[:, 0:16], func=AF.Exp), reads=[pcm], writes=[sm], first=[])
            kb.op("act", lambda e: e.activation(out=dec, in_=pcm[:, 16:32], func=AF.Exp), reads=[pcm], writes=[sm], first=[])
            kb.op("dve", lambda e: e.tensor_tensor(out=wend, in0=pcm[:, 16:32], in1=cum, op=ALU.subtract),
                  reads=[pcm, sm], writes=[sm], first=[])
            kb.psrel(pcm)
            kb.op("act", lambda e: e.activation(out=wend, in_=wend, func=AF.Exp), reads=[sm], writes=[sm], first=[])
            kb.op("dve", lambda e: e.tensor_tensor(out=wend, in0=wend, in1=dt, op=ALU.mult), reads=[sm], writes=[sm], first=[])

            ptx = kb.psalloc()
            ptxb = ptx[:, :].bitcast(BF16)

            def trx(e):
                ins = None
                for c in range(8):
                    ins = e.transpose(out=ptxb[:, c * 128:(c + 1) * 128], in_=xbcsT[:, c, :], identity=identb)
                return ins
            kb.op("pe", trx, reads=[XS[0], XS[1], cb], writes=[ptx])
            ptx3 = ptxb.rearrange("p (h d) -> p h d", h=16)

            def bc16(off):
                return A(sm, off, [[1, 16], [0, 64]])
            kb.op("dve", lambda e: e.tensor_tensor(out=xdt[:, :].rearrange("p (h d) -> p h d", h=16), in0=ptx3,
                                                   in1=bc16(0), op=ALU.mult), reads=[ptx, sm], writes=[xdt])
            kb.op("dve", lambda e: e.tensor_tensor(out=xw[:, :].rearrange("p (h d) -> p h d", h=16), in0=ptx3,
                                                   in1=bc16(4 * 16), op=ALU.mult), reads=[ptx, sm], writes=[xw])
            kb.op("dve", lambda e: e.tensor_tensor(out=xd[:, :].rearrange("p (h d) -> p h d", h=16), in0=ptx3,
                                                   in1=A(rowpar, rp + 32, [[1, 16], [0, 64]]), op=ALU.mult),
                  reads=[ptx, rowpar], writes=[xd])
            kb.psrel(ptx)
            ptb_ = kb.psalloc()
            ptbb = ptb_[:, :].bitcast(BF16)

            def trb(e):
                e.transpose(out=ptbb[:, 0:128], in_=xbcsT[:, 8, :], identity=identb)
                return e.transpose(out=ptbb[:, 128:256], in_=xbcsT[:, 9, :], identity=identb)
            kb.op("pe", trb, reads=[XS[2], cb], writes=[ptb_])
            kb.op("act", lambda e: e.copy(out=btok[:, :], in_=ptbb[:, 0:256]), reads=[ptb_], writes=[btok])
            kb.psrel(ptb_)
            pcb = kb.psalloc()

            def cbmm(e):
                e.matmul(pcb[:, 0:128], lhsT=xbcsT[:, 8, :], rhs=xbcsT[:, 10, :], start=True, stop=True)
                return e.matmul(pcb[:, 128:256], lhsT=xbcsT[:, 9, :], rhs=xbcsT[:, 11, :], start=True, stop=True)
            kb.op("pe", cbmm, reads=[XS[2]], writes=[pcb])
            tri_off = CF_TRIS if sample else CF_TRI
            kb.op("dve", lambda e: e.tensor_tensor(out=cbm[:, :, :], in0=pcb[:, 0:256].rearrange("p (g i) -> p g i", g=2),
                                                   in1=A(cf, tri_off, [[0, 2], [1, 128]]), op=ALU.mult),
                  reads=[pcb, cf], writes=[cbm])
            kb.psrel(pcb)
            if sample:
                pyit = sample_seq_phase(l)
            pY = [kb.psalloc(), kb.psalloc()]
            for g in range(2):
                kb.op("pe", lambda e, g=g: e.matmul(pY[g][:, :], lhsT=identb, rhs=xd[:, g * 512:(g + 1) * 512],
                                                    start=True, stop=False), reads=[cb, xd], writes=[pY[g]])
            stage = {}

            def intra_a(q4):
                if q4 % 2 == 0:
                    Rbuf, R3, R2 = Rb, Rb[:, :, :], Rb[:, :, :].rearrange("p a b -> p (a b)")
                else:
                    Rbuf, R3, R2 = BIG[2], A(big6, 1024, [[128, 4], [1, 128]]), big6[:, 1024:1536]
                kb.op("pool", lambda e: e.tensor_tensor(out=R3, in0=A(sm, 16 + q4 * 4, [[1, 4], [0, 128]]),
                                                        in1=A(cf, tri_off, [[0, 4], [1, 128]]), op=ALU.mult),
                      reads=[sm, cf], writes=[Rbuf])
                pseg = kb.psalloc()
                kb.op("pe", lambda e: e.matmul(pseg[:, :], lhsT=su, rhs=R2, start=True, stop=True),
                      reads=[cf, Rbuf], writes=[pseg])
                stage[q4] = pseg

            def intra_b(q4):
                pseg = stage.pop(q4)
                Eb = kb.nxt("E")
                kb.op("act", lambda e: e.activation(out=Eb[:, :, :].rearrange("p a b -> p (a b)"),
                                                    in_=pseg[:, :], func=AF.Exp), reads=[pseg], writes=[Eb])
                kb.psrel(pseg)
                MTb = kb.nxt("MT")
                g = q4 // 2
                kb.op("dve", lambda e: e.tensor_tensor(
                    out=MTb[:, :, :], in0=Eb[:, :, :], in1=A(cbm, g * 128, [[0, 4], [1, 128]]), op=ALU.mult),
                    reads=[Eb, cbm], writes=[MTb])

                def ymm(e):
                    ins = None
                    for hh in range(4):
                        h = q4 * 4 + hh
                        o = pY[h // 8][:, (h % 8) * 64:(h % 8 + 1) * 64]
                        ins = e.matmul(o, lhsT=MTb[:, hh, :], rhs=xdt[:, h * 64:(h + 1) * 64], start=False, stop=(h % 8 == 7))
                    return ins
                kb.op("pe", ymm, reads=[MTb, xdt], writes=[pY[q4 // 2]], first=[])
            pf = (not sample) and nxt_tile is not None and (not nxt_tile[2])
            pfb = []
            if pf:
                hb_n = norm_stage(*nxt_tile)
                kb.op("pool", lambda e: e.tensor_copy(out=A(xb, 0, [[176, 12], [1, 3]]), in_=carry[:, :, :]),
                      reads=[carry], writes=XB)
            intra_a(0)
            for q4 in range(4):
                if q4 + 1 < 4:
                    intra_a(q4 + 1)
                if pf and q4 < 3:
                    pxb = kb.psalloc()
                    mm_feat(pxb, 4, win, lambda k, c, b3=q4: win[:, k, C_XBC + (b3 * 4 + c) * 128:C_XBC + (b3 * 4 + c + 1) * 128])
                    pfb.append(pxb)
                intra_b(q4)
            if pf:
                for b3, pxb in enumerate(pfb):
                    kb.op("act", lambda e, b3=b3, pxb=pxb: e.copy(out=A(xb, b3 * 4 * 176 + 3, [[176, 4], [1, 128]]),
                                                                  in_=A(pxb, 0, [[128, 4], [1, 128]])),
                          reads=[pxb], writes=[XB[b3]], first=[])
                    kb.psrel(pxb)
                kb.op("pool", lambda e: e.tensor_copy(out=carry[:, :, :], in_=A(xb, 128, [[176, 12], [1, 3]])),
                      reads=XB, writes=[carry])
                tile_state["pre"] = (hb_n, True)

            if sample:
                for g in range(2):
                    kb.op("act" if g == 0 else "dve",
                          (lambda e, g=g: e.copy(out=scr[:, g * 512:(g + 1) * 512], in_=pyit[g][:, :])) if g == 0 else
                          (lambda e, g=g: e.tensor_copy(out=scr[:, g * 512:(g + 1) * 512], in_=pyit[g][:, :])),
                          reads=[pyit[g]], writes=[scr], first=None if g == 0 else [])
                kb.psrel(pyit[0], pyit[1])
            pYI = [kb.psalloc(), kb.psalloc()]
            if sample:
                for g in range(2):
                    def tryi(e, g=g):
                        ins = None
                        for c in range(4):
                            cc = g * 4 + c
                            ins = e.transpose(out=pYI[g][:, c * 128:(c + 1) * 128], in_=scr[:, cc * 128:(cc + 1) * 128],
                                              identity=identf)
                        return ins
                    kb.op("pe", tryi, reads=[scr, cf], writes=[pYI[g]])
            if not sample:
                ST = Sb[0]
                if ti == 0:
                    kb.op("pool", lambda e: e.memset(ST[:, :], 0.0), writes=[ST])
                    kb.op("pool", lambda e: e.memset(stb16[:, :], 0.0), writes=[stb16])
                for g in range(2):
                    kb.op("pe", lambda e, g=g: e.matmul(pYI[g][:, :], lhsT=xbcsT[:, 10 + g, :],
                                                        rhs=stb16[:, g * 512:(g + 1) * 512], start=True, stop=True),
                          reads=[XS[2], stb16], writes=[pYI[g]])

            y = big6
            YB = [BIG[0], BIG[1]]
            for g in range(2):
                kb.op("dve", lambda e, g=g: e.tensor_tensor(
                    out=A(y, g * 512, [[64, 8], [1, 64]]), in0=pYI[g][:, :].rearrange("p (h d) -> p h d", h=8),
                    in1=A(sm, 3 * 16 + g * 8, [[1, 8], [0, 64]]), op=ALU.mult),
                    reads=[pYI[g], sm], writes=[YB[g]])
            for g in range(2):
                kb.op("dve", lambda e, g=g: e.tensor_tensor(out=y[:, g * 512:(g + 1) * 512], in0=y[:, g * 512:(g + 1) * 512],
                                                            in1=pY[g][:, :], op=ALU.add), reads=[YB[g], pY[g]], writes=[YB[g]])
            kb.psrel(pYI[0], pYI[1], pY[0], pY[1])
            for g in range(2):
                kb.op("pool", lambda e, g=g: e.tensor_tensor(out=y[:, g * 512:(g + 1) * 512], in0=y[:, g * 512:(g + 1) * 512],
                                                             in1=sz[:, g * 512:(g + 1) * 512], op=ALU.mult),
                      reads=[YB[g], sz], writes=[YB[g]])
            ynb = xn
            for g in range(2):
                kb.op("act", lambda e, g=g: e.activation(out=ynb[:, g * 512:(g + 1) * 512], in_=y[:, g * 512:(g + 1) * 512],
                                                         func=AF.Square, accum_out=sm2[:, 2 + g:3 + g]),
                      reads=[YB[g]], writes=[ynb, sm2], first=None if g == 0 else [])
            kb.op("act", lambda e: e.activation(out=sm2[:, 4:6], in_=sm2[:, 2:4], func=AF.Ln, scale=1.0 / 512, bias=EPS),
                  reads=[sm2], writes=[sm2], first=[])
            kb.op("act", lambda e: e.activation(out=sm2[:, 4:6], in_=sm2[:, 4:6], func=AF.Exp, scale=-0.5),
                  reads=[sm2], writes=[sm2], first=[])
            for g in range(2):
                kb.op("dve", lambda e, g=g: e.tensor_scalar_mul(
                    out=ynb[:, g * 512:(g + 1) * 512], in0=y[:, g * 512:(g + 1) * 512], scalar1=sm2[:, 4 + g:5 + g]),
                    reads=[YB[g], sm2], writes=[ynb], first=None if g == 0 else [])
            pyt = kb.psalloc()
            pytb = pyt[:, :].bitcast(BF16)

            def try_(e):
                ins = None
                for c in range(8):
                    ins = e.transpose(out=pytb[:, c * 128:(c + 1) * 128], in_=ynb[:, c * 128:(c + 1) * 128], identity=identb)
                return ins
            kb.op("pe", try_, reads=[ynb, cb], writes=[pyt])
            kb.op("dve", lambda e: e.tensor_tensor(out=actT[:, 4:12, :], in0=pytb[:, :].rearrange("p (c t) -> p c t", c=8),
                                                   in1=A(colpar, l * NCP + CP_SG, [[1, 8], [0, 128]]), op=ALU.mult),
                  reads=[pyt, colpar], writes=[actT], first=actT_first())
            kb.psrel(pyt)

            if nxt_tile is not None and not pf:
                tile_state["pre"] = (norm_stage(*nxt_tile), False)

            if not sample:
                ST = Sb[0]
                pup = [kb.psalloc(), kb.psalloc()]
                for g in range(2):
                    kb.op("pe", lambda e, g=g: e.matmul(pup[g][:, :], lhsT=btok[:, g * 128:(g + 1) * 128],
                                                        rhs=xw[:, g * 512:(g + 1) * 512], start=True, stop=True),
                          reads=[btok, xw], writes=[pup[g]])
                kb.op("pool", lambda e: e.tensor_tensor(out=ST[:, :].rearrange("p (h d) -> p h d", h=16),
                                                        in0=ST[:, :].rearrange("p (h d) -> p h d", h=16),
                                                        in1=A(sm, 5 * 16, [[1, 16], [0, 64]]), op=ALU.mult),
                      reads=[ST, sm], writes=[ST])
                for g in range(2):
                    kb.op("dve", lambda e, g=g: e.tensor_tensor(out=ST[:, g * 512:(g + 1) * 512], in0=ST[:, g * 512:(g + 1) * 512],
                                                                in1=pup[g][:, :], op=ALU.add), reads=[ST, pup[g]], writes=[ST])
                kb.psrel(pup[0], pup[1])
                if last_p:
                    for half in range(2):
                        pt_ = kb.psalloc()

                        def trs(e, pt_=pt_, half=half):
                            ins = None
                            for c in range(4):
                                cc = half * 4 + c
                                ins = e.transpose(out=pt_[:, c * 128:(c + 1) * 128], in_=ST[:, cc * 128:(cc + 1) * 128],
                                                  identity=identf)
                            return ins
                        kb.op("pe", trs, reads=[ST, cf], writes=[pt_])
                        kb.op("act", lambda e, pt_=pt_: e.copy(out=scr[:, 0:512], in_=pt_[:, :]), reads=[pt_], writes=[scr])
                        kb.psrel(pt_)
                        kb.dma("sp", o_ssm_p[l, half * 512:(half + 1) * 512, :].rearrange("(c p) n -> p c n", p=128),
                               scr[:, 0:512].rearrange("p (c n) -> p c n", c=4), reads=[scr], writes=[o_ssm_p], sembuf=scr)
                else:
                    kb.op("act", lambda e: e.copy(out=stb16[:, :], in_=ST[:, :]), reads=[ST], writes=[stb16])
            else:
                attn_tail(tile_state["psum_"], tile_state["pxa"])

            po = [kb.psalloc(), kb.psalloc()]
            for n in range(2):
                def omm(e, n=n):
                    ins = None
                    for k in range(16):
                        ins = e.matmul(po[n][:, :], lhsT=actT[:, k, :], rhs=wout[:, k, n * 512:(n + 1) * 512],
                                       start=(k == 0), stop=(k == 15))
                    return ins
                kb.op("pe", omm, reads=[actT, wout], writes=[po[n]])
            for n in range(2):
                kb.op("dve", lambda e, n=n: e.tensor_tensor(out=hb[:, n * 512:(n + 1) * 512], in0=hb[:, n * 512:(n + 1) * 512],
                                                            in1=po[n][:, :], op=ALU.add), reads=[hb, po[n]], writes=[hb])
            kb.psrel(po[0], po[1])
            if sample and l < LL - 1:
                load_wout(l + 1)
                load_attn(l + 1)
            if l < LL - 1:
                kb.dma("sp", hs[hidx * 128:(hidx + 1) * 128, :], hb[:, :], reads=[hb], writes=[hsB[hidx]], sembuf=hb)
            else:
                ss = sm2[:, 8:9]
                rs = sm2[:, 9:10]
                fj = scr if not special else xd
                kb.op("act", lambda e: e.activation(out=fj[:, 0:1024], in_=hb[:, :], func=AF.Square, accum_out=ss),
                      reads=[hb], writes=[fj, sm2n])
                kb.op("act", lambda e: e.activation(out=rs, in_=ss, func=AF.Ln, scale=1.0 / D, bias=EPS),
                      reads=[sm2n], writes=[sm2n])
                kb.op("act", lambda e: e.activation(out=rs, in_=rs, func=AF.Exp, scale=-0.5), reads=[sm2n], writes=[sm2n])
                kb.op("dve", lambda e: e.scalar_tensor_tensor(out=hb[:, :], in0=hb[:, :], scalar=rs, in1=fng[:, :],
                                                              op0=ALU.mult, op1=ALU.mult), reads=[hb, sm2n, fng], writes=[hb])
                dst = ys[:, :] if sample else yp[ti * 128:(ti + 1) * 128, :]
                kb.dma("sp", dst, hb[:, :], reads=[hb], writes=[ys if sample else yp], sembuf=hb)

        def sample_seq_phase(l):
            abcast = big6
            kb.op("pool", lambda e: e.tensor_copy(out=abcast[:, 0:1024].rearrange("p (h d) -> p h d", h=16),
                                                  in_=A(sm, 16, [[1, 16], [0, 64]])), reads=[sm], writes=[abcast])
            pdc = kb.psalloc()

            def dcmm(e):
                ins = None
                for c in range(8):
                    ins = e.matmul(pdc[:, c * 16:(c + 1) * 16], lhsT=abcast[:, c * 128:(c + 1) * 128],
                                   rhs=cf[:, CF_BM:CF_BM + 16], start=True, stop=True)
                return ins
            kb.op("pe", dcmm, reads=[abcast, cf], writes=[pdc])
            kb.op("act", lambda e: e.activation(out=decT[:, :, :].rearrange("p a b -> p (a b)"), in_=pdc[:, 0:128], func=AF.Exp),
                  reads=[pdc], writes=[decT])
            kb.psrel(pdc)
            pyit = [kb.psalloc(), kb.psalloc()]
            psum_ = kb.psalloc()
            pxa = kb.psalloc()
            tile_state["psum_"], tile_state["pxa"] = psum_, pxa
            for s_ in range(NS):
                c0 = s_ * 8
                Kb = kb.nxt("Kb")
                Vb = kb.nxt("Vb")
                kb.dma("pool", Kb[:, :, :], ck[l, s_, :, :].rearrange("(c p) n -> p c n", p=128), writes=[Kb], sembuf=Kb)
                kb.dma("pool", Vb[:, :, :], cv[l, s_, :, :].rearrange("(c p) n -> p c n", p=128), writes=[Vb], sembuf=Vb)
                Sin = kb.nxt("Sb")
                kb.dma("sp", Sin[:, :].rearrange("p (c n) -> p c n", c=8),
                       st_ssm[l, s_, :, :].rearrange("(c p) n -> p c n", p=128), writes=[Sin], sembuf=Sin)
                for half in range(2):
                    pts_ = kb.psalloc()

                    def trs(e, pts_=pts_, half=half):
                        ins = None
                        for c in range(4):
                            cc = half * 4 + c
                            ins = e.transpose(out=pts_[:, c * 128:(c + 1) * 128], in_=Sin[:, cc * 128:(cc + 1) * 128],
                                              identity=identf)
                        return ins
                    kb.op("pe", trs, reads=[Sin, cf], writes=[pts_])
                    kb.op("act" if half == 0 else "dve",
                          (lambda e, pts_=pts_, half=half: e.copy(out=stb16[:, half * 512:(half + 1) * 512], in_=pts_[:, :]))
                          if half == 0 else
                          (lambda e, pts_=pts_, half=half: e.tensor_copy(out=stb16[:, half * 512:(half + 1) * 512], in_=pts_[:, :])),
                          reads=[pts_], writes=[stb16], first=None if half == 0 else [])
                    kb.psrel(pts_)

                def yimm(e):
                    ins = None
                    for c in range(8):
                        ins = e.matmul(pyit[c // 4][:, (c % 4) * 128 + c0:(c % 4) * 128 + c0 + 8],
                                       lhsT=stb16[:, c * 128:(c + 1) * 128], rhs=xbcsT[:, 10 + c // 4, c0:c0 + 8],
                                       start=True, stop=True)
                    return ins
                kb.op("pe", yimm, reads=[stb16, xbcsT], writes=[pyit[0], pyit[1]], first=None if s_ == 0 else [])
                Bs = kb.nxt("Bs")
                kb.op("pool", lambda e, Bs=Bs, s_=s_: e.tensor_scalar_mul(out=Bs[:, :], in0=btok[:, :],
                                                                      scalar1=cf[:, CF_BM + s_:CF_BM + s_ + 1]), reads=[btok, cf], writes=[Bs])
                pup = [kb.psalloc(), kb.psalloc()]

                def upmm(e, Bs=Bs, pup=pup):
                    ins = None
                    for c in range(8):
                        g = c // 4
                        ins = e.matmul(pup[g][:, (c % 4) * 128:(c % 4 + 1) * 128], lhsT=xw[:, c * 128:(c + 1) * 128],
                                       rhs=Bs[:, g * 128:(g + 1) * 128], start=True, stop=True)
                    return ins
                kb.op("pe", upmm, reads=[xw, Bs], writes=[pup[0], pup[1]])
                for c in range(8):
                    kb.op("dve", lambda e, c=c, s_=s_, Sin=Sin, pup=pup: e.scalar_tensor_tensor(
                        out=Sin[:, c * 128:(c + 1) * 128], in0=Sin[:, c * 128:(c + 1) * 128], scalar=decT[:, c, s_:s_ + 1],
                        in1=pup[c // 4][:, (c % 4) * 128:(c % 4 + 1) * 128], op0=ALU.mult, op1=ALU.add),
                        reads=[Sin, decT, pup[c // 4]], writes=[Sin])
                kb.psrel(pup[0], pup[1])
                kb.dma("act", o_ssm_s[l, s_, :, :].rearrange("(c p) n -> p c n", p=128),
                       Sin[:, :].rearrange("p (c n) -> p c n", c=8), reads=[Sin], writes=[o_ssm_s],
                       sembuf=sst_sem[s_ % 2])
                pkt = kb.psalloc()
                pktb = pkt[:, :].bitcast(BF16)

                def trk(e, Kb=Kb):
                    ins = None
                    for h in range(4):
                        for mc in range(2):
                            ins = e.transpose(out=pktb[:, (h * 2 + mc) * 128:(h * 2 + mc + 1) * 128],
                                              in_=Kb[:, mc, h * 128:(h + 1) * 128], identity=identb)
                    return ins
                kb.op("pe", trk, reads=[Kb, cb], writes=[pkt])
                kb.op("act", lambda e: e.copy(out=ktl[:, :, :].rearrange("p h m -> p (h m)"), in_=pktb[:, :]),
                      reads=[pkt], writes=[ktl])
                kb.psrel(pkt)
                psc = kb.psalloc()

                def scmm(e):
                    ins = None
                    for h in range(4):
                        for mc in range(2):
                            ins = e.matmul(psc[:, mc * 32 + h * 8:mc * 32 + h * 8 + 8], lhsT=ktl[:, h, mc * 128:(mc + 1) * 128],
                                           rhs=qT[:, h, c0:c0 + 8], start=True, stop=True)
                    return ins
                kb.op("pe", scmm, reads=[ktl, qT], writes=[psc])
                kb.op("act", lambda e: e.activation(out=pts[:, :], in_=psc[:, 0:64], func=AF.Exp, scale=SCALE),
                      reads=[psc], writes=[pts])
                kb.psrel(psc)

                def smm(e, s_=s_, c0=c0, Vb=Vb):
                    o = psum_[:, s_ * 32:(s_ + 1) * 32]
                    e.matmul(o, lhsT=onesb, rhs=pts[:, 0:32], start=True, stop=False)
                    ins = e.matmul(o, lhsT=onesb, rhs=pts[:, 32:64], start=False, stop=True)
                    for h in range(4):
                        for mc in range(2):
                            ins = e.matmul(pxa[:, h * 128 + c0:h * 128 + c0 + 8], lhsT=Vb[:, mc, h * 128:(h + 1) * 128],
                                           rhs=pts[:, mc * 32 + h * 8:mc * 32 + h * 8 + 8], start=(mc == 0), stop=(mc == 1))
                    return ins
                kb.op("pe", smm, reads=[cb, pts, Vb], writes=[psum_, pxa], first=None if s_ == 0 else [])
            return pyit

        tile_state = {}
        d2d = Buf("d2d", None)
        sst_sem = [Buf("sst0", None), Buf("sst1", None)]
        order = []
        for l in range(cfg['layers']):
            for ti in range(cfg['tiles']):
                order.append((l, ti, False))
            if cfg['sample']:
                order.append((l, 0, True))
        if order:
            load_win(0)
            load_poolw(0)
            load_attn(0)
            load_wout(0)
        for i, (l, ti, smp) in enumerate(order):
            do_tile(l, ti, smp, order[i + 1] if i + 1 < len(order) else None)
        kb.finish(outs)
        print("total ops", kb.nops)
    return nc


def _constants():
    cf = np.zeros((128, NCF), np.float32)
    r = np.arange(128)
    cf[:, CF_ID:CF_ID + 128] = np.eye(128)
    tri = (r[:, None] <= r[None, :]).astype(np.float32)
    same = (r[:, None] // 8 == r[None, :] // 8).astype(np.float32)
    cf[:, CF_TRI:CF_TRI + 128] = tri
    cf[:, CF_TRIS:CF_TRIS + 128] = tri * same
    cf[:, CF_SU:CF_SU + 128] = (r[:, None] > r[None, :]).astype(np.float32)
    cf[:, CF_ONE:CF_ONE + 128] = 1.0
    cf[:, CF_BONE:CF_BONE + 128] = same
    cf[:, CF_BM:CF_BM + 16] = (r[:, None] // 8 == np.arange(16)[None, :]).astype(np.float32)
    cb = np.zeros((128, NCB), np.float32)
    cb[:, CB_ID:CB_ID + 128] = np.eye(128)
    cb[:, CB_ONE:CB_ONE + 128] = 1.0
    for g, w in enumerate(POOL_W):
        s = r[:, None]
        t = r[None, :]
        cur = ((s <= t) & (s > t - w)).astype(np.float32) - w * (s == t)
        prev = np.zeros((128, 128), np.float32)
        srel = s - 128
        prev[:, :] = ((srel > t - w)).astype(np.float32)
        prev[:64, :] = 0.0
        cnt0 = np.minimum(t + 1, w).astype(np.float32)
        cur0 = ((s <= t) & (s > t - w)).astype(np.float32) - cnt0 * (s == t)
        cf[:, CF_IC0 + g * 128:CF_IC0 + (g + 1) * 128] = np.broadcast_to(1.0 / cnt0, (128, 128))
        ss_, ts_ = s // 8, s % 8
        sc_, tc_ = t // 8, t % 8
        bs = ((ss_ == sc_) & (ts_ <= tc_) & (ts_ > tc_ - w)).astype(np.float32) - w * (s == t)
        sta = np.zeros((128, 128), np.float32)
        stbm = np.zeros((128, 128), np.float32)
        rows = np.arange(120)
        sq, rr = rows // 15, rows % 15
        for half, m in ((0, sta), (1, stbm)):
            m[:120, :] = ((sq[:, None] + 8 * half == sc_) & ((rr[:, None] - 15) > (tc_ - w))).astype(np.float32)
        for k, m in enumerate((cur, prev, cur0, bs, sta, stbm)):
            o = CB_BAND + (g * 6 + k) * 128
            cb[:, o:o + 128] = m
    return cf, cb


_NC_CACHE = {}
_DBG_CFG = None


def kernel(x_prompt, x_sample, mem_prompt, state_pool, state_conv, state_ssm, cache_mem_k, cache_mem_v,
           norm_g, w_in, pool_w, pool_scale, conv_w, conv_b, dt_bias, a_log, d_skip, ssd_norm_g,
           mem_norm_g, w_mem_k, w_mem_v, w_out, final_norm_g):
    f = lambda a: np.ascontiguousarray(np.asarray(a, dtype=np.float32))
    x_prompt, x_sample, mem_prompt = f(x_prompt), f(x_sample), f(mem_prompt)
    state_pool, state_conv, state_ssm = f(state_pool), f(state_conv), f(state_ssm)
    cache_mem_k, cache_mem_v = f(cache_mem_k), f(cache_mem_v)
    cf, cb = _constants()
    colpar = np.zeros((128, DEPTH, NCP), np.float32)
    for l in range(DEPTH):
        colpar[:, l, CP_NG:CP_NG + 8] = f(norm_g)[l].reshape(8, 128).T
        colpar[:, l, CP_MG:CP_MG + 8] = f(mem_norm_g)[l].reshape(8, 128).T
        colpar[:, l, CP_SG:CP_SG + 8] = f(ssd_norm_g)[l].reshape(8, 128).T
        colpar[:, l, CP_PS:CP_PS + 4] = f(pool_scale)[l].reshape(4, 128).T
        colpar[:, l, CP_CW:CP_CW + 48] = f(conv_w)[l].reshape(4, 12, 128).transpose(2, 1, 0).reshape(128, 48)
        colpar[:, l, CP_CB:CP_CB + 12] = f(conv_b)[l].reshape(12, 128).T
    colpar = np.ascontiguousarray(colpar.reshape(128, DEPTH * NCP))
    rowpar = np.ascontiguousarray(np.concatenate([f(dt_bias), f(a_log), f(d_skip)], axis=1).reshape(-1))
    shared = {
        "w_in": f(w_in), "w_out": f(w_out), "pool_w": f(pool_w), "w_mk": f(w_mem_k), "w_mv": f(w_mem_v),
        "colpar": colpar, "rowpar": rowpar, "fng": f(final_norm_g), "cstf": cf, "cstb": cb,
    }
    in_maps = []
    for c in range(NCORES):
        sl = slice(c * NS, (c + 1) * NS)
        m = dict(shared)
        m["xp"] = x_prompt[c]
        m["xs"] = np.ascontiguousarray(x_sample[sl].reshape(128, D))
        m["mem"] = mem_prompt[c]
        m["st_pool"] = np.ascontiguousarray(state_pool[:, sl].reshape(DEPTH, NS * 15, 512))
        m["st_conv"] = np.ascontiguousarray(state_conv[:, sl].reshape(DEPTH, NS * 3, 1536))
        m["st_ssm"] = np.ascontiguousarray(state_ssm[:, sl].reshape(DEPTH, NS, 1024, 128))
        m["ck"] = np.ascontiguousarray(cache_mem_k[:, sl].reshape(DEPTH, NS, 256, 512))
        m["cv"] = np.ascontiguousarray(cache_mem_v[:, sl].reshape(DEPTH, NS, 256, 512))
        in_maps.append(m)
    if "nc" not in _NC_CACHE:
        _NC_CACHE["nc"] = build_nc(_DBG_CFG)
    nc = _NC_CACHE["nc"]
    res = run_bass_kernel_spmd(nc, in_maps, core_ids=list(range(NCORES)))
    R = res.results
    g = lambda name, c: np.asarray(R[c][name], dtype=np.float32)
    y_prompt = np.stack([g("yp", c) for c in range(NCORES)]).reshape(8, 2048, D)
    y_sample = np.concatenate([g("ys", c).reshape(NS, 8, D) for c in range(NCORES)], axis=0)
    new_pool_p = np.stack([g("o_pool_p", c) for c in range(NCORES)], axis=1)
    new_conv_p = np.stack([g("o_conv_p", c) for c in range(NCORES)], axis=1)
    new_ssm_p = np.stack([g("o_ssm_p", c).reshape(DEPTH, 16, 64, 128) for c in range(NCORES)], axis=1)
    new_mk = np.stack([g("o_mk", c).reshape(DEPTH, 256, 4, 128) for c in range(NCORES)], axis=1)
    new_mv = np.stack([g("o_mv", c).reshape(DEPTH, 256, 4, 128) for c in range(NCORES)], axis=1)
    new_pool_s = np.concatenate([g("o_pool_s", c) for c in range(NCORES)], axis=1)
    new_conv_s = np.concatenate([g("o_conv_s", c) for c in range(NCORES)], axis=1)
    new_ssm_s = np.concatenate([g("o_ssm_s", c).reshape(DEPTH, NS, 16, 64, 128) for c in range(NCORES)], axis=1)
    return (y_prompt, y_sample, new_pool_p, new_conv_p, new_ssm_p, new_mk, new_mv, new_pool_s, new_conv_s, new_ssm_s)
```
